# Optimizing a Trainium2 kernel written in Bass

```python
import math
import jax, jax.numpy as jnp
from jax import lax
import numpy as np

D_MODEL = 1024
BATCH = 4
SEQ = 8192
DEPTH = 4
DEC_BATCH = 4
DEC_SEQ = 4096
PAST_LEN = 128

ATTN_WIDTH = D_MODEL // 2
CONV_WIDTH = D_MODEL - ATTN_WIDTH
HEAD_DIM = 64
V_DIM = 2 * HEAD_DIM
N_ATTN_HEADS = ATTN_WIDTH // V_DIM
N_CONV_GROUPS = 8
CONV_K = 3
D_FF = 4 * D_MODEL
NUM_BUCKETS = 32
MAX_DISTANCE = 128
Q_BLOCK = 128
QK_COLS = N_ATTN_HEADS * 2 * HEAD_DIM
IN_COLS = 3 * ATTN_WIDTH + 3 * CONV_WIDTH
LN_EPS = 1e-5
DEEPNORM_ALPHA = (2.0 * DEPTH) ** 0.25
DEEPNORM_BETA = (8.0 * DEPTH) ** -0.25

kernel_name = "hybrid_diffattn_shortconv_deepnorm_encoder"


def layer_norm(x, g, b):
    xf = x.astype(jnp.float32)
    mu = jnp.mean(xf, axis=-1, keepdims=True)
    var = jnp.mean(jnp.square(xf - mu), axis=-1, keepdims=True)
    y = (xf - mu) * lax.rsqrt(var + LN_EPS) * g.astype(jnp.float32) + b.astype(jnp.float32)
    return y.astype(x.dtype)


def t5_bucket(rel):
    half = NUM_BUCKETS // 2
    max_exact = half // 2
    ret = jnp.where(rel > 0, half, 0)
    n = jnp.abs(rel)
    nf = jnp.maximum(n, 1).astype(jnp.float32)
    large = max_exact + (jnp.log(nf / max_exact) / math.log(MAX_DISTANCE / max_exact)
                         * (half - max_exact)).astype(jnp.int32)
    large = jnp.minimum(large, half - 1)
    return ret + jnp.where(n < max_exact, n, large)


def diff_attention(q, k, v, lam, rel_bias):
    b, s = q.shape[0], q.shape[1]
    nb = s // Q_BLOCK
    scale = HEAD_DIM ** -0.5
    q1, q2 = q[..., :HEAD_DIM], q[..., HEAD_DIM:]
    k1, k2 = k[..., :HEAD_DIM], k[..., HEAD_DIM:]
    kpos = jnp.arange(s, dtype=jnp.int32)

    def block(args):
        i, q1b, q2b = args
        qpos = i * Q_BLOCK + jnp.arange(Q_BLOCK, dtype=jnp.int32)
        bucket = t5_bucket(kpos[None, :] - qpos[:, None])
        bias = jnp.transpose(rel_bias[bucket], (2, 0, 1))
        p1 = jax.nn.softmax(jnp.einsum('bqhd,bkhd->bhqk', q1b, k1) * scale + bias, axis=-1)
        p2 = jax.nn.softmax(jnp.einsum('bqhd,bkhd->bhqk', q2b, k2) * scale + bias, axis=-1)
        return jnp.einsum('bhqk,bkhe->bqhe', p1 - lam * p2, v)

    def to_blocks(t):
        return jnp.moveaxis(t.reshape(b, nb, Q_BLOCK, *t.shape[2:]), 1, 0)

    out = lax.map(block, (jnp.arange(nb, dtype=jnp.int32), to_blocks(q1), to_blocks(q2)))
    return jnp.moveaxis(out, 0, 1).reshape(b, s, N_ATTN_HEADS, V_DIM)


def short_conv(u, w, bias):
    up = jnp.pad(u, ((0, 0), (1, 1), (0, 0)))
    return w[0] * up[:, :-2] + w[1] * up[:, 1:-1] + w[2] * up[:, 2:] + bias


def encoder_layer(x, l, w_in, w_out, conv_w, conv_b, lambda_q1, lambda_k1, lambda_q2,
                  lambda_k2, subln_g, rel_bias, ln1_g, ln1_b, w_mlp1, w_mlp2, ln2_g, ln2_b):
    b, s, _ = x.shape
    proj = jnp.einsum('bsd,dn->bsn', x, w_in[l])
    q, k, v, gb, gc, hc = jnp.split(
        proj, [QK_COLS, 2 * QK_COLS, 2 * QK_COLS + ATTN_WIDTH,
               2 * QK_COLS + ATTN_WIDTH + CONV_WIDTH,
               2 * QK_COLS + ATTN_WIDTH + 2 * CONV_WIDTH], axis=-1)
    f32 = jnp.float32
    q = q.reshape(b, s, N_ATTN_HEADS, 2 * HEAD_DIM).astype(f32)
    k = k.reshape(b, s, N_ATTN_HEADS, 2 * HEAD_DIM).astype(f32)
    v = v.reshape(b, s, N_ATTN_HEADS, V_DIM).astype(f32)
    lambda_init = 0.8 - 0.6 * math.exp(-0.3 * l)
    lam = (jnp.exp(jnp.sum(lambda_q1[l].astype(f32) * lambda_k1[l].astype(f32)))
           - jnp.exp(jnp.sum(lambda_q2[l].astype(f32) * lambda_k2[l].astype(f32)))
           + lambda_init)
    a = diff_attention(q, k, v, lam, rel_bias.astype(f32))
    a = a * lax.rsqrt(jnp.mean(jnp.square(a), axis=-1, keepdims=True) + LN_EPS)
    a = a * subln_g[l].astype(f32) * (1.0 - lambda_init)
    attn_out = a.reshape(b, s, ATTN_WIDTH).astype(x.dtype)
    conv_out = gb * short_conv(gc * hc, conv_w[l], conv_b[l])
    mix = jnp.einsum('bsn,nd->bsd', jnp.concatenate([attn_out, conv_out], axis=-1), w_out[l])
    x = layer_norm(DEEPNORM_ALPHA * x + mix, ln1_g[l], ln1_b[l])
    hdn = jnp.square(jax.nn.relu(jnp.einsum('bsd,df->bsf', x, w_mlp1[l])))
    ffn = jnp.einsum('bsf,fd->bsd', hdn, w_mlp2[l])
    return layer_norm(DEEPNORM_ALPHA * x + ffn, ln2_g[l], ln2_b[l])


def setup_inputs(seed: int = 0) -> dict:
    key = jax.random.key(seed)
    ks = jax.random.split(key, 20)
    nrm = jax.random.normal
    f32 = jnp.float32
    x_prompt = nrm(ks[0], (BATCH, SEQ, D_MODEL), f32)
    x_sample = nrm(ks[1], (DEC_BATCH, DEC_SEQ, D_MODEL), f32)
    col_scale = jnp.concatenate([
        jnp.ones((2 * QK_COLS,), f32),
        jnp.full((ATTN_WIDTH,), DEEPNORM_BETA, f32),
        jnp.ones((2 * CONV_WIDTH,), f32),
        jnp.full((CONV_WIDTH,), DEEPNORM_BETA, f32)])
    w_in = nrm(ks[2], (DEPTH, D_MODEL, IN_COLS), f32) * (D_MODEL ** -0.5) * col_scale
    w_out = nrm(ks[3], (DEPTH, D_MODEL, D_MODEL), f32) * (D_MODEL ** -0.5) * DEEPNORM_BETA
    conv_w = nrm(ks[4], (DEPTH, CONV_K, CONV_WIDTH), f32) * (CONV_K ** -0.5)
    conv_b = nrm(ks[5], (DEPTH, CONV_WIDTH), f32) * 0.02
    lambda_q1 = nrm(ks[6], (DEPTH, HEAD_DIM), f32) * 0.1
    lambda_k1 = nrm(ks[7], (DEPTH, HEAD_DIM), f32) * 0.1
    lambda_q2 = nrm(ks[8], (DEPTH, HEAD_DIM), f32) * 0.1
    lambda_k2 = nrm(ks[9], (DEPTH, HEAD_DIM), f32) * 0.1
    subln_g = 1.0 + 0.02 * nrm(ks[10], (DEPTH, V_DIM), f32)
    rel_bias = nrm(ks[11], (NUM_BUCKETS, N_ATTN_HEADS), f32) * 0.5
    ln1_g = 1.0 + 0.02 * nrm(ks[12], (DEPTH, D_MODEL), f32)
    ln1_b = 0.02 * nrm(ks[13], (DEPTH, D_MODEL), f32)
    w_mlp1 = nrm(ks[14], (DEPTH, D_MODEL, D_FF), f32) * (D_MODEL ** -0.5) * DEEPNORM_BETA
    w_mlp2 = nrm(ks[15], (DEPTH, D_FF, D_MODEL), f32) * (D_FF ** -0.5) * DEEPNORM_BETA
    ln2_g = 1.0 + 0.02 * nrm(ks[16], (DEPTH, D_MODEL), f32)
    ln2_b = 0.02 * nrm(ks[17], (DEPTH, D_MODEL), f32)
    return {"x_prompt": x_prompt, "x_sample": x_sample, "w_in": w_in, "w_out": w_out,
            "conv_w": conv_w, "conv_b": conv_b, "lambda_q1": lambda_q1, "lambda_k1": lambda_k1,
            "lambda_q2": lambda_q2, "lambda_k2": lambda_k2, "subln_g": subln_g,
            "rel_bias": rel_bias, "ln1_g": ln1_g, "ln1_b": ln1_b, "w_mlp1": w_mlp1,
            "w_mlp2": w_mlp2, "ln2_g": ln2_g, "ln2_b": ln2_b}


def reference(x_prompt, x_sample, w_in, w_out, conv_w, conv_b, lambda_q1, lambda_k1,
              lambda_q2, lambda_k2, subln_g, rel_bias, ln1_g, ln1_b, w_mlp1, w_mlp2,
              ln2_g, ln2_b):
    y_prompt = x_prompt
    y_sample = x_sample
    for l in range(DEPTH):
        y_prompt = encoder_layer(y_prompt, l, w_in, w_out, conv_w, conv_b, lambda_q1, lambda_k1,
                                 lambda_q2, lambda_k2, subln_g, rel_bias, ln1_g, ln1_b,
                                 w_mlp1, w_mlp2, ln2_g, ln2_b)
        y_sample = encoder_layer(y_sample, l, w_in, w_out, conv_w, conv_b, lambda_q1, lambda_k1,
                                 lambda_q2, lambda_k2, subln_g, rel_bias, ln1_g, ln1_b,
                                 w_mlp1, w_mlp2, ln2_g, ln2_b)
    return (y_prompt, y_sample)
```

```python
import contextlib
import os
import numpy as np
import concourse.bass as bass
import concourse.mybir as mybir
from concourse.bass_utils import run_bass_kernel_spmd

F32 = mybir.dt.float32
BF16 = mybir.dt.bfloat16
AF = mybir.ActivationFunctionType
ALU = mybir.AluOpType


class _Dummy:
    def __getitem__(self, k):
        return self

    def __getattr__(self, k):
        return self

    def __call__(self, *a, **k):
        return self


class Sched:
    CENG = ("pe", "act", "dve", "pool")

    def __init__(self, nc):
        self.nc = nc
        self.needed = None
        self.reset(dry=True)

    def reset(self, dry):
        self.dry = dry
        self.n = 0
        self.eng_of = []
        self.dma_of = []
        self.lastw = {}
        self.readers = {}
        self.need_now = set()
        self.sigcount = {e: 0 for e in self.CENG}
        self.sig_of = {}
        self.dma_cnt = {}
        self.seen = {e: {} for e in self.CENG + ("sp",)}
        self.last_on = {}
        self.dma_out = []
        self.nwaits = 0
        self.ninstr = 0

    def start_real(self, stack):
        self.needed = self.need_now
        self.reset(dry=False)
        self.stack = stack
        self.esem = {e: stack.enter_context(self.nc.semaphore("es_" + e)) for e in self.CENG}
        self.dsem = {}

    def eng(self, e):
        nc = self.nc
        return {"pe": nc.tensor, "act": nc.scalar, "dve": nc.vector, "pool": nc.gpsimd, "sp": nc.sync}[e]

    def _dsem(self, key):
        if key not in self.dsem:
            self.dsem[key] = self.stack.enter_context(self.nc.semaphore("ds_%d" % len(self.dsem)))
        return self.dsem[key]

    def op(self, eng, fn, reads=(), writes=(), dma=None, extra_deps=()):
        i = self.n
        self.n += 1
        hard = set(extra_deps)
        war = set()
        for k in reads:
            w = self.lastw.get(k)
            if w is not None:
                hard.add(w)
        for k in writes:
            w = self.lastw.get(k)
            if w is not None:
                hard.add(w)
            rd = self.readers.get(k)
            if rd:
                war.update(rd.values())
        for k in writes:
            self.lastw[k] = i
            self.readers[k] = {}
        for k in reads:
            self.readers.setdefault(k, {})[("d", i) if dma else eng] = i
        war -= hard
        self.eng_of.append(eng)
        if dma is not None:
            c = self.dma_cnt.get(dma, 0) + 16
            self.dma_cnt[dma] = c
            self.dma_of.append((dma, c))
            self.dma_out.append(i)
        else:
            self.dma_of.append(None)
        waits = []
        for d, is_war in [(d, False) for d in hard] + [(d, True) for d in war]:
            dd = self.dma_of[d]
            if dd is not None:
                waits.append(("d", dd[0], dd[1]))
                continue
            de = self.eng_of[d]
            if de == eng and dma is None:
                if eng == "pe" or is_war:
                    continue
            self.need_now.add(d)
            if not self.dry:
                waits.append(("e", de, self.sig_of[d]))
        if dma is None and fn is not None:
            if not self.dry and i in self.needed:
                self.sigcount[eng] += 1
                self.sig_of[i] = self.sigcount[eng]
            self.last_on[eng] = i
        if self.dry:
            return i
        E = self.eng(eng)
        seen = self.seen[eng]
        for kind, a, v in waits:
            key = (kind, a)
            if seen.get(key, 0) >= v:
                continue
            seen[key] = v
            sem = self._dsem(a) if kind == "d" else self.esem[a]
            E.wait_ge(sem, v)
            self.nwaits += 1
        if fn is not None:
            ins = fn(E)
            self.ninstr += 1
            if dma is not None:
                ins.then_inc(self._dsem(dma), 16)
            elif i in self.needed:
                ins.then_inc(self.esem[eng], 1)
        return i

    def barrier(self, engines=None):
        deps = list(self.last_on.values()) + list(self.dma_out)
        for e in (engines or (self.CENG + ("sp",))):
            self.op(e, None, extra_deps=deps)
        if engines is None:
            self.dma_out = []
            self.lastw = {}
            self.readers = {}

    @contextlib.contextmanager
    def scope(self):
        sc = _Scope(self)
        try:
            yield sc
        finally:
            sc.close()


class _Scope:
    def __init__(self, S):
        self.S = S
        self.stack = None if S.dry else contextlib.ExitStack()

    def sbuf(self, name, shape, dtype):
        if self.S.dry:
            return _Dummy()
        self.S.uid = getattr(self.S, "uid", 0) + 1
        return self.stack.enter_context(self.S.nc.sbuf_tensor("sb%d_%s" % (self.S.uid, name), list(shape), dtype))

    def psum(self, name, shape, dtype):
        if self.S.dry:
            return _Dummy()
        self.S.uid = getattr(self.S, "uid", 0) + 1
        return self.stack.enter_context(self.S.nc.psum_tensor("pp%d_%s" % (self.S.uid, name), list(shape), dtype))

    def close(self):
        if self.stack is not None:
            self.stack.close()


import math

D = 1024
LN_EPS = 1e-5
ALPHA = 8.0 ** 0.25
NEG = -30000.0


def lam_init(l):
    return 0.8 - 0.6 * math.exp(-0.3 * l)


def np_bucket(rel):
    half = 16
    me = 8
    ret = np.where(rel > 0, half, 0)
    n = np.abs(rel)
    nf = np.maximum(n, 1).astype(np.float32)
    large = me + (np.log(nf / np.float32(me)) / np.float32(math.log(128 / 8)) * np.float32(8)).astype(np.int32)
    large = np.minimum(large, half - 1)
    return ret + np.where(n < me, n, large)


def build_program(SD, L):
    nc = bass.Bass("TRN2", target_bir_lowering=False)
    NT = SD // 512
    NKT = SD // 128
    AX = mybir.AxisListType.X

    def din(name, shape):
        return nc.dram_tensor(name, list(shape), F32, kind="ExternalInput").ap()

    def dscr(name, shape, dt):
        return nc.dram_tensor(name, list(shape), dt, kind="Internal").ap()

    def dap(t, off, dims):
        return bass.AP(t.tensor, off, [list(d) for d in dims])

    x_in = din("x", [SD, D])
    vcol_in = din("vcol", [128, NKT])
    vrow_in = din("vrow", [128, SD])
    kmask_in = din("kmask", [128, NKT])
    oh_in = din("ohrev", [32, 1280])
    ident_in = din("ident", [128, 128])
    anti_in = din("anti", [128, 128])
    KFAST = os.environ.get("KFAST") == "1"
    KSET = int(os.environ.get("KSET", "99"))
    w_in = din("w_in", [4, D, 3072] if not KFAST else [1, 8, 8])
    w_out = din("w_out", [4, D, D] if not KFAST else [1, 8, 8])
    conv_w = din("conv_w", [4, 3, 512])
    conv_b = din("conv_b", [4, 512])
    lq1 = din("lambda_q1", [4, 64]); lk1 = din("lambda_k1", [4, 64])
    lq2 = din("lambda_q2", [4, 64]); lk2 = din("lambda_k2", [4, 64])
    subln_g = din("subln_g", [4, 128])
    rel_bias = din("rel_bias", [32, 4])
    ln1_g = din("ln1_g", [4, D]); ln1_b = din("ln1_b", [4, D])
    w_mlp1 = din("w_mlp1", [4, D, 4096] if not KFAST else [1, 8, 8]); w_mlp2 = din("w_mlp2", [4, 4096, D] if not KFAST else [1, 8, 8])
    ln2_g = din("ln2_g", [4, D]); ln2_b = din("ln2_b", [4, D])
    y_out = nc.dram_tensor("y", [SD, D], F32, kind="ExternalOutput").ap()

    winb = dscr("winb", [L, D, 3072], BF16)
    woutb = dscr("woutb", [L, D, D], BF16)
    w1b = dscr("w1b", [L, D, 4096], BF16)
    w2b = dscr("w2b", [L * 8 * 128, 4096], BF16)
    xTf = dscr("xTf", [D, SD], F32)
    xTb = dscr("xTb", [D, SD], BF16)
    QT = dscr("QT", [512, SD], BF16)
    KT = dscr("KT", [512, SD], BF16)
    Vs = dscr("Vs", [SD, 512], BF16)
    Us = dscr("Us", [512, SD + 2], F32)
    GBs = dscr("GBs", [512, SD], F32)
    ATs = dscr("ATs", [512, SD], BF16)
    Tscr = dscr("Tscr", [4, 1280], F32)
    Gscr = dscr("Gscr", [128, 4 * 1152], F32)

    S = Sched(nc)
    bankctr = [0]

    def prog(S):
        bankctr[0] = 0
        with S.scope() as pc:
            ident = pc.sbuf("ident", [128, 128], F32)
            anti = pc.sbuf("anti", [128, 128], F32)
            ones_f = pc.sbuf("ones_f", [128, 128], F32)
            ones_b = pc.sbuf("ones_b", [128, 128], BF16)
            vcol = pc.sbuf("vcol", [128, NKT], F32)
            kmask = pc.sbuf("kmask", [128, NKT], F32)
            cb = pc.sbuf("cb", [128, 4, 3, NKT], F32)
            neglam = pc.sbuf("neglam", [128, 4], F32)
            gsc = pc.sbuf("gsc", [128, 4], F32)
            g1 = pc.sbuf("g1", [128, 4, 8], F32); b1 = pc.sbuf("b1", [128, 4, 8], F32)
            g2 = pc.sbuf("g2", [128, 4, 8], F32); b2 = pc.sbuf("b2", [128, 4, 8], F32)
            cw = pc.sbuf("cw", [128, 4, 3, 4], F32)
            cbs = pc.sbuf("cbs", [128, 4, 4], F32)
            PSW = [pc.psum("psw%d" % i, [128, 1024], F32) for i in range(4)]
            PS = [PSW[i // 2][:, (i % 2) * 512:(i % 2 + 1) * 512] for i in range(8)]
            epsb = pc.sbuf("epsb", [128, 1], F32)

            def bank():
                i = bankctr[0] % 8
                bankctr[0] += 1
                return i

            wc = {}
            prevc = []

            def cdma(fn, wkey, l):
                i = S.op("pool", fn, writes=[wkey], dma=("wc", l), extra_deps=list(prevc))
                prevc[:] = [i]
                return i
            for l in range(L if not KFAST else 0):
                ops = []
                for q in range(4):
                    ops.append(cdma(lambda e, l=l, q=q: e.dma_start(out=dap(winb, l * D * 3072 + q * 768 * 1024, [[1024, 768], [1, 1024]]),
                                                                     in_=dap(w_in, l * D * 3072 + q * 768 * 1024, [[1024, 768], [1, 1024]])),
                                    ("winb", l, q), l))
                ops.append(cdma(lambda e, l=l: e.dma_start(out=dap(woutb, l * D * D, [[1024, 1024], [1, 1024]]),
                                                            in_=dap(w_out, l * D * D, [[1024, 1024], [1, 1024]])),
                                ("woutb", l), l))
                for hf in range(4):
                    ops.append(cdma(lambda e, l=l, hf=hf: e.dma_start(
                        out=dap(w1b, l * D * 4096 + hf * 1024 * 1024, [[1024, 1024], [1, 1024]]),
                        in_=dap(w_mlp1, l * D * 4096 + hf * 1024 * 1024, [[1024, 1024], [1, 1024]])),
                        ("w1b", l, hf), l))
                for j in range(8):
                    ops.append(cdma(lambda e, l=l, j=j: e.dma_start(
                        out=dap(w2b, (l * 8 + j) * 128 * 4096, [[4096, 128], [128, 32], [1, 128]]),
                        in_=dap(w_mlp2, l * 4096 * D + j * 128, [[1024, 128], [128 * 1024, 32], [1, 128]])),
                        ("w2b", l, j), l))
                wc[l] = ops

            with S.scope() as sc:
                lam4 = sc.sbuf("lam4", [128, 4, 4, 64], F32)
                prod = sc.sbuf("prod", [128, 2, 4, 64], F32)
                ss = sc.sbuf("ss", [128, 2, 4], F32)
                ee = sc.sbuf("ee", [128, 2, 4], F32)
                lam = sc.sbuf("lam", [128, 4], F32)
                cl = sc.sbuf("cl", [128, 4], F32); cr = sc.sbuf("cr", [128, 4], F32)
                rb = sc.sbuf("rb", [32, 4], F32)
                oh = sc.sbuf("oh", [32, 1280], F32)
                tsb = sc.sbuf("tsb", [4, 1280], F32)
                hk = sc.sbuf("hk", [128, 4, 1152], F32)
                G = sc.sbuf("Gs", [128, 4, 1152], F32)
                zt = sc.sbuf("zt", [128, 4, 1], F32)

                def ld(out_fn, in_fn, nonc=False):
                    if nonc:
                        S.op("sp", lambda e: e.dma_start(out=out_fn(), in_=in_fn(), allow_slow_non_contiguous=True), dma="setup")
                    else:
                        S.op("sp", lambda e: e.dma_start(out=out_fn(), in_=in_fn()), dma="setup")
                ld(lambda: ident[:], lambda: ident_in)
                ld(lambda: anti[:], lambda: anti_in)
                ld(lambda: vcol[:], lambda: vcol_in)
                ld(lambda: kmask[:], lambda: kmask_in)
                ld(lambda: oh[:], lambda: oh_in)
                ld(lambda: rb[:], lambda: rel_bias)
                for i, t in enumerate((lq1, lk1, lq2, lk2)):
                    ld(lambda i=i: lam4[:, i, :, :], lambda t=t: dap(t, 0, [[0, 128], [64, 4], [1, 64]]))
                ld(lambda: cl[:], lambda: dap(rel_bias, 15 * 4, [[0, 128], [1, 4]]))
                ld(lambda: cr[:], lambda: dap(rel_bias, 31 * 4, [[0, 128], [1, 4]]))
                ld(lambda: gsc[:], lambda: dap(subln_g, 0, [[1, 128], [128, 4]]), nonc=True)
                for dst, src in ((g1, ln1_g), (b1, ln1_b), (g2, ln2_g), (b2, ln2_b)):
                    for l in range(4):
                        ld(lambda dst=dst, l=l: dst[:, l, :], lambda src=src, l=l: dap(src, l * D, [[1, 128], [128, 8]]), nonc=True)
                for l in range(4):
                    for k in range(3):
                        ld(lambda l=l, k=k: cw[:, l, k, :], lambda l=l, k=k: dap(conv_w, (l * 3 + k) * 512, [[1, 128], [128, 4]]), nonc=True)
                    ld(lambda l=l: cbs[:, l, :], lambda l=l: dap(conv_b, l * 512, [[1, 128], [128, 4]]), nonc=True)
                S.op("dve", lambda e: e.memset(ones_f[:], 1.0))
                S.op("dve", lambda e: e.memset(epsb[:], LN_EPS))
                S.op("dve", lambda e: e.memset(ones_b[:], 1.0))
                S.op("dve", lambda e: e.memset(zt[:], 0.0))
                S.barrier()
                if KSET <= 1:
                    return
                S.op("sp", lambda e: e.dma_start(out=dap(Us, 0, [[SD + 2, 128], [128 * (SD + 2), 4], [1, 1]]), in_=zt[:], allow_slow_non_contiguous=True), dma="setup")
                S.op("sp", lambda e: e.dma_start(out=dap(Us, SD + 1, [[SD + 2, 128], [128 * (SD + 2), 4], [1, 1]]), in_=zt[:], allow_slow_non_contiguous=True), dma="setup")
                S.op("dve", lambda e: e.tensor_tensor(out=prod[:, 0, :, :], in0=lam4[:, 0, :, :], in1=lam4[:, 1, :, :], op=ALU.mult))
                S.op("dve", lambda e: e.tensor_tensor(out=prod[:, 1, :, :], in0=lam4[:, 2, :, :], in1=lam4[:, 3, :, :], op=ALU.mult))
                S.barrier()
                if KSET <= 2:
                    return
                S.op("dve", lambda e: e.reduce_sum(out=ss[:], in_=prod[:], axis=AX))
                S.barrier()
                if KSET <= 3:
                    return
                S.op("act", lambda e: e.activation(out=ee[:], in_=ss[:], func=AF.Exp))
                S.barrier()
                if KSET <= 4:
                    return
                S.op("dve", lambda e: e.tensor_tensor(out=lam[:], in0=ee[:, 0, :], in1=ee[:, 1, :], op=ALU.subtract))
                S.barrier()
                if KSET <= 5:
                    return
                for l in range(4):
                    S.op("dve", lambda e, l=l: e.tensor_scalar(out=neglam[:, l:l + 1], in0=lam[:, l:l + 1], scalar1=lam_init(l), scalar2=-1.0, op0=ALU.add, op1=ALU.mult))
                    S.op("dve", lambda e, l=l: e.tensor_scalar(out=gsc[:, l:l + 1], in0=gsc[:, l:l + 1], scalar1=1.0 - lam_init(l), scalar2=None, op0=ALU.mult))
                for h in range(4):
                    S.op("dve", lambda e, h=h: e.tensor_scalar(out=cb[:, h, 0, :], in0=kmask[:], scalar1=cl[:, h:h + 1], scalar2=None, op0=ALU.add))
                    S.op("dve", lambda e, h=h: e.tensor_copy(out=cb[:, h, 1, :], in_=kmask[:]))
                    S.op("dve", lambda e, h=h: e.tensor_scalar(out=cb[:, h, 2, :], in0=kmask[:], scalar1=cr[:, h:h + 1], scalar2=None, op0=ALU.add))
                for c, w in ((0, 512), (1, 512), (2, 256)):
                    S.op("pe", lambda e, c=c, w=w: e.matmul(PS[c][0:4, 0:w], lhsT=rb[:], rhs=oh[:, c * 512:c * 512 + w], start=True, stop=True))
                S.barrier()
                if KSET <= 6:
                    return
                for c, w in ((0, 512), (1, 512), (2, 256)):
                    S.op("act", lambda e, c=c, w=w: e.copy(out=tsb[:, c * 512:c * 512 + w], in_=PS[c][0:4, 0:w]))
                S.barrier()
                if KSET <= 7:
                    return
                S.op("sp", lambda e: e.dma_start(out=Tscr, in_=tsb[:]), dma="setup")
                S.barrier()
                if KSET <= 8:
                    return
                for h in range(4):
                    S.op("sp", lambda e, h=h: e.dma_start(out=hk[:, h, :], in_=dap(Tscr, h * 1280, [[1, 128], [1, 1152]])), dma="setup")
                S.barrier()
                if KSET <= 9:
                    return
                for h in range(4):
                    for c, w in ((0, 512), (1, 512), (2, 128)):
                        S.op("pe", lambda e, h=h, c=c, w=w: e.matmul(PS[3 + c][:, 0:w], lhsT=anti[:], rhs=hk[:, h, c * 512:c * 512 + w], start=True, stop=True))
                    S.barrier()
                    if KSET <= 10:
                        return
                    for c, w in ((0, 512), (1, 512), (2, 128)):
                        S.op("act", lambda e, h=h, c=c, w=w: e.copy(out=G[:, h, c * 512:c * 512 + w], in_=PS[3 + c][:, 0:w]))
                    S.barrier()
                    if KSET <= 11:
                        return
                S.barrier()
                if KSET <= 12:
                    return
                S.op("sp", lambda e: e.dma_start(out=Gscr, in_=G[:].rearrange("p h m -> p (h m)")), dma="setup")
                S.barrier()
                if KSET <= 13:
                    return

            with S.scope() as sc:
                xin = [sc.sbuf("xin%d" % i, [128, 4, D], F32) for i in range(2)]
                xfs = [sc.sbuf("xfs%d" % i, [128, 8, 512], F32) for i in range(2)]
                xbs = [sc.sbuf("xbs%d" % i, [128, 8, 512], BF16) for i in range(2)]
                XV = int(os.environ.get("XV", "0"))
                for t in range(NT):
                    s = t % 2
                    S.op("sp", lambda e, t=t, s=s: e.dma_start(out=xin[s][:], in_=dap(x_in, t * 512 * D, [[D, 128], [128 * D, 4], [1, D]])),
                         writes=[("xin", s)], dma=("xin", s))
                    if XV == 1:
                        continue
                    for dc in range(8):
                        b = bank()
                        for sub in range(4):
                            if XV == 2 and sub > 0:
                                continue
                            S.op("pe", lambda e, b=b, s=s, sub=sub, dc=dc: e.transpose(out=PS[b][:, sub * 128:(sub + 1) * 128], in_=xin[s][:, sub, dc * 128:(dc + 1) * 128], identity=ident[:]),
                                 reads=[("xin", s)], writes=[("ps", b)])
                        if XV == 4:
                            continue
                        S.op("act", lambda e, b=b, s=s, dc=dc: e.copy(out=xfs[s][:, dc, :], in_=PS[b][:]), reads=[("ps", b)], writes=[("xfs", s)])
                        if XV == 5:
                            continue
                        S.op("dve", lambda e, b=b, s=s, dc=dc: e.tensor_copy(out=xbs[s][:, dc, :], in_=xfs[s][:, dc, :]), reads=[("xfs", s)], writes=[("xbs", s)])
                    if XV >= 3:
                        continue
                    S.op("sp", lambda e, t=t, s=s: e.dma_start(out=dap(xTf, t * 512, [[SD, 128], [128 * SD, 8], [1, 512]]), in_=xfs[s][:]),
                         reads=[("xfs", s)], dma=("xst", s))
                    S.op("sp", lambda e, t=t, s=s: e.dma_start(out=dap(xTb, t * 512, [[SD, 128], [128 * SD, 8], [1, 512]]), in_=xbs[s][:]),
                         reads=[("xbs", s)], dma=("xst", s))
                S.barrier()

            def phase_a(l):
                with S.scope() as sc:
                    win = sc.sbuf("win", [128, 8, 3072], BF16)
                    xb = [sc.sbuf("xb%d" % i, [128, 8, 512], BF16) for i in range(2)]
                    vr = [sc.sbuf("vr%d" % i, [128, 512], F32) for i in range(2)]
                    qkst = [sc.sbuf("qkst%d" % i, [128, 8, 512], BF16) for i in range(2)]
                    vst = [sc.sbuf("vst%d" % i, [128, 4, 512], BF16) for i in range(2)]
                    hcs = [sc.sbuf("hcs%d" % i, [128, 512], F32) for i in range(2)]
                    ust = [sc.sbuf("ust%d" % i, [128, 4, 512], F32) for i in range(2)]
                    gbst = [sc.sbuf("gbst%d" % i, [128, 4, 512], F32) for i in range(2)]
                    for hf in range(2):
                        S.op("sp", lambda e, hf=hf: e.dma_start(out=win[:, hf * 4:(hf + 1) * 4, :], in_=dap(winb, l * D * 3072 + hf * 4 * 128 * 3072, [[3072, 128], [128 * 3072, 4], [1, 3072]])),
                             writes=[("win", hf)], dma=("win", hf), extra_deps=wc[l][0:4])

                    def loads(t):
                        s = t % 2
                        S.op("sp", lambda e: e.dma_start(out=xb[s][:], in_=dap(xTb, t * 512, [[SD, 128], [128 * SD, 8], [1, 512]])),
                             writes=[("xb", s)], dma=("xb", s))
                        S.op("sp", lambda e: e.dma_start(out=vr[s][:], in_=dap(vrow_in, t * 512, [[SD, 128], [1, 512]])),
                             writes=[("vr", s)], dma=("vr", s))
                    loads(0)
                    hk_ = 0
                    for t in range(NT):
                        s = t % 2
                        if t + 1 < NT:
                            loads(t + 1)
                        for c in range(8):
                            b = bank()
                            for dc in range(8):
                                S.op("pe", lambda e, b=b, c=c, dc=dc: e.matmul(PS[b][:], lhsT=win[:, dc, c * 128:(c + 1) * 128], rhs=xb[s][:, dc, :], start=(dc == 0), stop=(dc == 7)),
                                     reads=[("win", dc // 4), ("xb", s)], writes=[("ps", b)])
                            if c % 2 == 0:
                                S.op("act", lambda e, b=b, c=c: e.copy(out=qkst[s][:, c, :], in_=PS[b][:]), reads=[("ps", b)], writes=[("qkst", s)])
                            else:
                                S.op("dve", lambda e, b=b, c=c: e.tensor_copy(out=qkst[s][:, c, :], in_=PS[b][:]), reads=[("ps", b)], writes=[("qkst", s)])
                        S.op("sp", lambda e: e.dma_start(out=dap(QT, t * 512, [[SD, 128], [128 * SD, 4], [1, 512]]), in_=qkst[s][:, 0:4, :]), reads=[("qkst", s)], dma=("qkst", s))
                        S.op("sp", lambda e: e.dma_start(out=dap(KT, t * 512, [[SD, 128], [128 * SD, 4], [1, 512]]), in_=qkst[s][:, 4:8, :]), reads=[("qkst", s)], dma=("qkst", s))
                        for sub in range(4):
                            b = bank()
                            for dc in range(8):
                                S.op("pe", lambda e, b=b, sub=sub, dc=dc: e.matmul(PS[b][:], lhsT=xb[s][:, dc, sub * 128:(sub + 1) * 128], rhs=win[:, dc, 1024:1536], start=(dc == 0), stop=(dc == 7)),
                                     reads=[("win", dc // 4), ("xb", s)], writes=[("ps", b)])
                            S.op("dve", lambda e, b=b, sub=sub: e.tensor_scalar(out=vst[s][:, sub, :], in0=PS[b][:], scalar1=vcol[:, t * 4 + sub:t * 4 + sub + 1], scalar2=None, op0=ALU.mult),
                                 reads=[("ps", b)], writes=[("vst", s)])
                        S.op("sp", lambda e: e.dma_start(out=dap(Vs, t * 512 * 512, [[512, 128], [128 * 512, 4], [1, 512]]), in_=vst[s][:]), reads=[("vst", s)], dma=("vst", s))
                        for cc in range(4):
                            bh = bank()
                            for dc in range(8):
                                S.op("pe", lambda e, b=bh, cc=cc, dc=dc: e.matmul(PS[b][:], lhsT=win[:, dc, 2560 + cc * 128:2560 + (cc + 1) * 128], rhs=xb[s][:, dc, :], start=(dc == 0), stop=(dc == 7)),
                                     reads=[("win", dc // 4), ("xb", s)], writes=[("ps", bh)])
                            k = hk_ % 2
                            hk_ += 1
                            S.op("dve", lambda e, b=bh, k=k: e.tensor_tensor(out=hcs[k][:], in0=PS[b][:], in1=vr[s][:], op=ALU.mult),
                                 reads=[("ps", bh), ("vr", s)], writes=[("hcs", k)])
                            bg = bank()
                            for dc in range(8):
                                S.op("pe", lambda e, b=bg, cc=cc, dc=dc: e.matmul(PS[b][:], lhsT=win[:, dc, 2048 + cc * 128:2048 + (cc + 1) * 128], rhs=xb[s][:, dc, :], start=(dc == 0), stop=(dc == 7)),
                                     reads=[("win", dc // 4), ("xb", s)], writes=[("ps", bg)])
                            S.op("dve", lambda e, b=bg, k=k, cc=cc: e.tensor_tensor(out=ust[s][:, cc, :], in0=PS[b][:], in1=hcs[k][:], op=ALU.mult),
                                 reads=[("ps", bg), ("hcs", k)], writes=[("ust", s)])
                            bb = bank()
                            for dc in range(8):
                                S.op("pe", lambda e, b=bb, cc=cc, dc=dc: e.matmul(PS[b][:], lhsT=win[:, dc, 1536 + cc * 128:1536 + (cc + 1) * 128], rhs=xb[s][:, dc, :], start=(dc == 0), stop=(dc == 7)),
                                     reads=[("win", dc // 4), ("xb", s)], writes=[("ps", bb)])
                            S.op("act", lambda e, b=bb, cc=cc: e.copy(out=gbst[s][:, cc, :], in_=PS[b][:]), reads=[("ps", bb)], writes=[("gbst", s)])
                        S.op("sp", lambda e: e.dma_start(out=dap(Us, 1 + t * 512, [[SD + 2, 128], [128 * (SD + 2), 4], [1, 512]]), in_=ust[s][:]), reads=[("ust", s)], dma=("ust", s))
                        S.op("sp", lambda e: e.dma_start(out=dap(GBs, t * 512, [[SD, 128], [128 * SD, 4], [1, 512]]), in_=gbst[s][:]), reads=[("gbst", s)], dma=("gbst", s))
                    S.barrier()

            def attention(l):
                with S.scope() as sc:
                    qh = [sc.sbuf("qh%d" % i, [128, SD], BF16) for i in range(2)]
                    kh = [sc.sbuf("kh%d" % i, [128, SD], BF16) for i in range(2)]
                    vh = [sc.sbuf("vh%d" % i, [128, NKT, 128], BF16) for i in range(2)]
                    G = sc.sbuf("G", [128, 4, 1152], F32)
                    NE = 6
                    E = [sc.sbuf("E%d" % i, [128, 1024], BF16) for i in range(NE)]
                    bt = [sc.sbuf("bt%d" % i, [128, 1024], F32) for i in range(2)]
                    acc = [[sc.sbuf("acc%d_%d" % (g_, i), [128, 512], F32) for i in range(3)] for g_ in range(2)]
                    lnb = sc.sbuf("lnb", [128, 1024], F32)
                    rcp = sc.sbuf("rcp", [128, 1024], F32)
                    t1 = sc.sbuf("t1", [128, 512], F32); t2 = sc.sbuf("t2", [128, 512], F32)
                    av = sc.sbuf("av", [128, 512], F32); sq = sc.sbuf("sq", [128, 512], F32)
                    lr = sc.sbuf("lr", [128, 512], F32); rinv = sc.sbuf("rinv", [128, 512], F32)
                    aost = [sc.sbuf("aost%d" % i, [128, 512], BF16) for i in range(2)]
                    S.op("sp", lambda e: e.dma_start(out=G[:].rearrange("p h m -> p (h m)"), in_=Gscr), writes=["G"], dma="gld")

                    def hload(h):
                        s = h % 2
                        S.op("sp", lambda e: e.dma_start(out=qh[s][:], in_=dap(QT, h * 128 * SD, [[SD, 128], [1, SD]])), writes=[("qh", s)], dma=("qh", s))
                        S.op("sp", lambda e: e.dma_start(out=kh[s][:], in_=dap(KT, h * 128 * SD, [[SD, 128], [1, SD]])), writes=[("kh", s)], dma=("kh", s))
                        S.op("sp", lambda e: e.dma_start(out=vh[s][:], in_=dap(Vs, h * 128, [[512, 128], [128 * 512, NKT], [1, 128]])), writes=[("vh", s)], dma=("vh", s))
                    hload(0)
                    units = [(h, qt, kt) for h in range(4) for qt in range(NT) for kt in range(NKT)]
                    NU = len(units)

                    def scores(u):
                        h, qt, kt = units[u]
                        s = h % 2
                        p = u % 2
                        if qt == 0 and kt == 2 and h + 1 < 4:
                            hload(h + 1)
                        S.op("pe", lambda e: e.matmul(PSW[p][:, 0:512], lhsT=kh[s][0:64, kt * 128:(kt + 1) * 128], rhs=qh[s][0:64, qt * 512:(qt + 1) * 512], start=True, stop=True, tile_position=(0, 0)),
                             reads=[("qh", s), ("kh", s)], writes=[("psw", p)])
                        S.op("pe", lambda e: e.matmul(PSW[p][:, 512:1024], lhsT=kh[s][64:128, kt * 128:(kt + 1) * 128], rhs=qh[s][64:128, qt * 512:(qt + 1) * 512], start=True, stop=True, tile_position=(64, 0)),
                             reads=[("qh", s), ("kh", s)], writes=[("psw", p)])
                        j = kt - 4 * qt
                        if -1 <= j <= 4:
                            off = 512 - 128 * j
                            for m in range(2):
                                S.op("dve", lambda e, m=m: e.scalar_tensor_tensor(out=bt[p][:, m * 512:(m + 1) * 512], in0=PSW[p][:, m * 512:(m + 1) * 512], scalar=0.125, in1=G[:, h, off:off + 512], op0=ALU.mult, op1=ALU.add),
                                     reads=[("psw", p), "G"], writes=[("bt", p, m)])

                    def unit(u):
                        h, qt, kt = units[u]
                        s = h % 2
                        p = u % 2
                        r = u % NE
                        gi = (h * NT + qt) % 2
                        A = acc[gi]
                        j = kt - 4 * qt
                        if -1 <= j <= 4:
                            S.op("act", lambda e: e.activation(out=E[r][:], in_=bt[p][:], func=AF.Exp, bias=cb[:, h, 1, kt:kt + 1], scale=1.0),
                                 reads=[("bt", p, 0), ("bt", p, 1)], writes=[("E", r)])
                        else:
                            side = 0 if j < -1 else 2
                            S.op("act", lambda e: e.activation(out=E[r][:], in_=PSW[p][:], func=AF.Exp, bias=cb[:, h, side, kt:kt + 1], scale=0.125),
                                 reads=[("psw", p)], writes=[("E", r)])

                    def unit_b(u):
                        h, qt, kt = units[u]
                        s = h % 2
                        r = u % NE
                        gi = (h * NT + qt) % 2
                        A = acc[gi]
                        st = (kt == 0)
                        sp_ = (kt == NKT - 1)
                        S.op("pe", lambda e: e.matmul(PS[4][:], lhsT=vh[s][:, kt, :], rhs=E[r][:, 0:512], start=st, stop=sp_), reads=[("vh", s), ("E", r)], writes=[("ps", 4)])
                        S.op("pe", lambda e: e.matmul(PS[5][:], lhsT=vh[s][:, kt, :], rhs=E[r][:, 512:1024], start=st, stop=sp_), reads=[("vh", s), ("E", r)], writes=[("ps", 5)])
                        if kt % 3 != 2:
                            S.op("pe", lambda e: e.matmul(PS[6][:], lhsT=ones_b[:], rhs=E[r][:, 512:1024], start=st, stop=False), reads=[("E", r)], writes=[("ps", 6)])
                        elif kt == 2:
                            S.op("dve", lambda e: e.tensor_copy(out=A[2][:], in_=E[r][:, 512:1024]), reads=[("E", r)], writes=[("acc", gi, 2)])
                        else:
                            S.op("dve", lambda e: e.tensor_tensor(out=A[2][:], in0=A[2][:], in1=E[r][:, 512:1024], op=ALU.add), reads=[("E", r), ("acc", gi, 2)], writes=[("acc", gi, 2)])
                        a1 = kt % 2
                        if kt < 2:
                            S.op("dve", lambda e: e.tensor_copy(out=A[a1][:], in_=E[r][:, 0:512]), reads=[("E", r)], writes=[("acc", gi, a1)])
                        else:
                            S.op("dve", lambda e: e.tensor_tensor(out=A[a1][:], in0=A[a1][:], in1=E[r][:, 0:512], op=ALU.add), reads=[("E", r), ("acc", gi, a1)], writes=[("acc", gi, a1)])
                        if kt == NKT - 1:
                            epilogue(h, qt, gi)

                    epi = [0]

                    def epilogue(h, qt, gi):
                        k = epi[0] % 2
                        epi[0] += 1
                        A = acc[gi]
                        S.op("pe", lambda e: e.matmul(PS[6][:], lhsT=ones_f[:], rhs=A[2][:], start=False, stop=True), reads=[("acc", gi, 2)], writes=[("ps", 6)])
                        S.op("pe", lambda e: e.matmul(PS[7][:], lhsT=ones_f[:], rhs=A[0][:], start=True, stop=False), reads=[("acc", gi, 0)], writes=[("ps", 7)])
                        S.op("pe", lambda e: e.matmul(PS[7][:], lhsT=ones_f[:], rhs=A[1][:], start=False, stop=True), reads=[("acc", gi, 1)], writes=[("ps", 7)])
                        S.op("act", lambda e: e.activation(out=lnb[:], in_=PSW[3][:], func=AF.Ln), reads=[("ps", 6), ("ps", 7)], writes=["lnb"])
                        S.op("act", lambda e: e.activation(out=rcp[:], in_=lnb[:], func=AF.Exp, scale=-1.0), reads=["lnb"], writes=["rcp"])
                        S.op("dve", lambda e: e.tensor_tensor(out=t1[:], in0=PS[4][:], in1=rcp[:, 512:1024], op=ALU.mult), reads=[("ps", 4), "rcp"], writes=["t1"])
                        S.op("dve", lambda e: e.tensor_tensor(out=t2[:], in0=PS[5][:], in1=rcp[:, 0:512], op=ALU.mult), reads=[("ps", 5), "rcp"], writes=["t2"])
                        S.op("dve", lambda e: e.scalar_tensor_tensor(out=av[:], in0=t2[:], scalar=neglam[:, l:l + 1], in1=t1[:], op0=ALU.mult, op1=ALU.add), reads=["t1", "t2"], writes=["av"])
                        S.op("dve", lambda e: e.tensor_tensor(out=sq[:], in0=av[:], in1=av[:], op=ALU.mult), reads=["av"], writes=["sq"])
                        S.op("pe", lambda e: e.matmul(PS[7][:], lhsT=ones_f[:], rhs=sq[:], start=True, stop=True), reads=["sq"], writes=[("ps", 7)])
                        S.op("act", lambda e: e.activation(out=lr[:], in_=PS[7][:], func=AF.Ln, bias=epsb[:, 0:1], scale=1.0 / 128.0), reads=[("ps", 7)], writes=["lr"])
                        S.op("act", lambda e: e.activation(out=rinv[:], in_=lr[:], func=AF.Exp, scale=-0.5), reads=["lr"], writes=["rinv"])
                        S.op("dve", lambda e: e.scalar_tensor_tensor(out=aost[k][:], in0=av[:], scalar=gsc[:, l:l + 1], in1=rinv[:], op0=ALU.mult, op1=ALU.mult), reads=["av", "rinv"], writes=[("aost", k)])
                        S.op("sp", lambda e: e.dma_start(out=dap(ATs, h * 128 * SD + qt * 512, [[SD, 128], [1, 512]]), in_=aost[k][:]), reads=[("aost", k)], dma=("aost", k))

                    scores(0)
                    if NU > 1:
                        scores(1)
                    for u in range(NU):
                        unit(u)
                        if u + 2 < NU:
                            scores(u + 2)
                        unit_b(u)
                    S.barrier()

            def phase_c(l):
                last = (l == L - 1)
                with S.scope() as sc:
                    wout = sc.sbuf("wout", [128, 8, D], BF16)
                    wr = [sc.sbuf("wr%d" % i, [128, 4096], BF16) for i in range(3)]
                    xz = [sc.sbuf("xz%d" % i, [128, 8, 512], F32) for i in range(2)]
                    mix = [sc.sbuf("mix%d" % i, [128, 8, 512], BF16) for i in range(2)]
                    ub = [sc.sbuf("ub%d" % i, [128, 4, 514], F32) for i in range(2)]
                    gbb = [sc.sbuf("gbb%d" % i, [128, 4, 512], F32) for i in range(2)]
                    x1b = sc.sbuf("x1b", [128, 8, 512], BF16)
                    hT = sc.sbuf("hT", [128, 32, 512], BF16)
                    xob = None if last else sc.sbuf("xob", [128, 8, 512], BF16)
                    sqb = [sc.sbuf("sqb%d" % i, [128, 512], F32) for i in range(2)]
                    mean = sc.sbuf("mean", [128, 512], F32); msq = sc.sbuf("msq", [128, 512], F32)
                    sd = sc.sbuf("sd", [128, 512], F32)
                    rinv = sc.sbuf("rinvc", [128, 512], F32)
                    tt = [sc.sbuf("tt%d" % i, [128, 512], F32) for i in range(2)]
                    rr = [sc.sbuf("rr%d" % i, [128, 512], F32) for i in range(3)]
                    ca = [sc.sbuf("ca%d" % i, [128, 512], F32) for i in range(2)]
                    yst = [sc.sbuf("yst%d" % i, [128, D], F32) for i in range(2)] if last else None
                    S.op("sp", lambda e: e.dma_start(out=wout[:], in_=dap(woutb, l * D * D, [[D, 128], [128 * D, 8], [1, D]])), writes=["wout"], dma="wout", extra_deps=wc[l])
                    wctr = [0]
                    cnt = [0]

                    def loads(t):
                        s = t % 2
                        S.op("sp", lambda e: e.dma_start(out=xz[s][:], in_=dap(xTf, t * 512, [[SD, 128], [128 * SD, 8], [1, 512]])), writes=[("z", s, j) for j in range(8)], dma=("xz", s))
                        S.op("sp", lambda e: e.dma_start(out=mix[s][:, 0:4, :], in_=dap(ATs, t * 512, [[SD, 128], [128 * SD, 4], [1, 512]])), writes=[("mixa", s)], dma=("mixa", s))
                        S.op("sp", lambda e: e.dma_start(out=ub[s][:], in_=dap(Us, t * 512, [[SD + 2, 128], [128 * (SD + 2), 4], [1, 514]])), writes=[("ub", s)], dma=("ub", s))
                        S.op("sp", lambda e: e.dma_start(out=gbb[s][:], in_=dap(GBs, t * 512, [[SD, 128], [128 * SD, 4], [1, 512]])), writes=[("gbb", s)], dma=("gbb", s))

                    def conv(t):
                        s = t % 2
                        for cc in range(4):
                            S.op("dve", lambda e, cc=cc: e.tensor_scalar(out=ca[0][:], in0=ub[s][:, cc, 0:512], scalar1=cw[:, l, 0, cc:cc + 1], scalar2=None, op0=ALU.mult), reads=[("ub", s)], writes=[("ca", 0)])
                            S.op("dve", lambda e, cc=cc: e.scalar_tensor_tensor(out=ca[1][:], in0=ub[s][:, cc, 1:513], scalar=cw[:, l, 1, cc:cc + 1], in1=ca[0][:], op0=ALU.mult, op1=ALU.add), reads=[("ub", s), ("ca", 0)], writes=[("ca", 1)])
                            S.op("dve", lambda e, cc=cc: e.scalar_tensor_tensor(out=ca[0][:], in0=ub[s][:, cc, 2:514], scalar=cw[:, l, 2, cc:cc + 1], in1=ca[1][:], op0=ALU.mult, op1=ALU.add), reads=[("ub", s), ("ca", 1)], writes=[("ca", 0)])
                            S.op("dve", lambda e, cc=cc: e.scalar_tensor_tensor(out=mix[s][:, 4 + cc, :], in0=ca[0][:], scalar=cbs[:, l, cc:cc + 1], in1=gbb[s][:, cc, :], op0=ALU.add, op1=ALU.mult), reads=[("ca", 0), ("gbb", s)], writes=[("mixc", s, cc)])

                    def ln_stats(z, zs):
                        bs = bank()
                        for j in range(8):
                            S.op("pe", lambda e, j=j: e.matmul(PS[bs][:], lhsT=ones_f[:], rhs=z[:, j, :], start=(j == 0), stop=(j == 7)), reads=[("z", zs, j)], writes=[("ps", bs)])
                        bq = bank()
                        for j in range(8):
                            k = cnt[0] % 2
                            cnt[0] += 1
                            if j % 2 == 0:
                                S.op("act", lambda e, j=j, k=k: e.activation(out=sqb[k][:], in_=z[:, j, :], func=AF.Square), reads=[("z", zs, j)], writes=[("sqb", k)])
                            else:
                                S.op("dve", lambda e, j=j, k=k: e.tensor_tensor(out=sqb[k][:], in0=z[:, j, :], in1=z[:, j, :], op=ALU.mult), reads=[("z", zs, j)], writes=[("sqb", k)])
                            S.op("pe", lambda e, j=j, k=k: e.matmul(PS[bq][:], lhsT=ones_f[:], rhs=sqb[k][:], start=(j == 0), stop=(j == 7)), reads=[("sqb", k)], writes=[("ps", bq)])
                        return bs, bq

                    def ln_norm(z, zs, bs, bq, gt, bt_, outb):
                        S.op("act", lambda e: e.mul(out=mean[:], in_=PS[bs][:], mul=1.0 / D), reads=[("ps", bs)], writes=["mean"])
                        S.op("dve", lambda e: e.tensor_tensor(out=msq[:], in0=mean[:], in1=mean[:], op=ALU.mult), reads=["mean"], writes=["msq"])
                        S.op("dve", lambda e: e.scalar_tensor_tensor(out=rinv[:], in0=PS[bq][:], scalar=1.0 / D, in1=msq[:], op0=ALU.mult, op1=ALU.subtract), reads=[("ps", bq), "msq"], writes=["rinvc"])
                        S.op("act", lambda e: e.activation(out=sd[:], in_=rinv[:], func=AF.Sqrt, bias=LN_EPS, scale=1.0), reads=["rinvc"], writes=["sd"])
                        S.op("dve", lambda e: e.reciprocal(out=rinv[:], in_=sd[:]), reads=["sd"], writes=["rinvc"])
                        for j in range(8):
                            k = j % 2
                            S.op("dve", lambda e, j=j, k=k: e.tensor_tensor(out=tt[k][:], in0=z[:, j, :], in1=mean[:], op=ALU.subtract), reads=[("z", zs, j), "mean"], writes=[("tt", k)])
                            S.op("dve", lambda e, j=j, k=k: e.tensor_tensor(out=tt[k][:], in0=tt[k][:], in1=rinv[:], op=ALU.mult), reads=[("tt", k), "rinvc"], writes=[("tt", k)])
                            if outb is not None:
                                S.op("act", lambda e, j=j, k=k: e.activation(out=outb[:, j, :], in_=tt[k][:], func=AF.Identity, bias=bt_[:, l, j:j + 1], scale=gt[:, l, j:j + 1]), reads=[("tt", k)], writes=[("ob", j)])
                            S.op("act", lambda e, j=j, k=k: e.activation(out=z[:, j, :], in_=tt[k][:], func=AF.Identity, bias=bt_[:, l, j:j + 1], scale=gt[:, l, j:j + 1]), reads=[("tt", k)], writes=[("z", zs, j)])

                    loads(0)
                    conv(0)
                    pend = [None]
                    for t in range(NT):
                        s = t % 2
                        z = xz[s]
                        for j in range(8):
                            b = bank()
                            for n_ in range(8):
                                S.op("pe", lambda e, b=b, j=j, n_=n_: e.matmul(PS[b][:], lhsT=wout[:, n_, j * 128:(j + 1) * 128], rhs=mix[s][:, n_, :], start=(n_ == 0), stop=(n_ == 7)),
                                     reads=["wout", ("mixa", s) if n_ < 4 else ("mixc", s, n_ - 4)], writes=[("ps", b)])
                            S.op("dve", lambda e, b=b, j=j: e.scalar_tensor_tensor(out=z[:, j, :], in0=z[:, j, :], scalar=ALPHA, in1=PS[b][:], op0=ALU.mult, op1=ALU.add), reads=[("ps", b), ("z", s, j)], writes=[("z", s, j)])
                        if pend[0] is not None:
                            pend[0]()
                            pend[0] = None
                        if t + 1 < NT:
                            loads(t + 1)
                        bs, bq = ln_stats(z, s)
                        ln_norm(z, s, bs, bq, g1, b1, x1b)
                        if t + 1 < NT:
                            conv(t + 1)
                        for g in range(8):
                            ws = wctr[0] % 3
                            wctr[0] += 1
                            S.op("sp", lambda e, g=g, ws=ws: e.dma_start(out=wr[ws][:].rearrange("p (c f) -> p c f", c=8), in_=dap(w1b, l * D * 4096 + g * 512, [[4096, 128], [128 * 4096, 8], [1, 512]])),
                                 writes=[("wr", ws)], dma=("wr", ws), extra_deps=wc[l])
                            for fl in range(4):
                                fc = g * 4 + fl
                                b = bank()
                                for dc in range(8):
                                    S.op("pe", lambda e, b=b, ws=ws, fl=fl, dc=dc: e.matmul(PS[b][:], lhsT=wr[ws][:, dc * 512 + fl * 128:dc * 512 + (fl + 1) * 128], rhs=x1b[:, dc, :], start=(dc == 0), stop=(dc == 7)),
                                         reads=[("wr", ws), ("ob", dc)], writes=[("ps", b)])
                                k = fc % 3
                                S.op("act", lambda e, b=b, k=k: e.activation(out=rr[k][:], in_=PS[b][:], func=AF.Relu), reads=[("ps", b)], writes=[("rr", k)])
                                S.op("pool" if fc % 3 == 2 else "dve", lambda e, k=k, fc=fc: e.tensor_tensor(out=hT[:, fc, :], in0=rr[k][:], in1=rr[k][:], op=ALU.mult), reads=[("rr", k)], writes=[("hT", fc)])
                        for j in range(8):
                            ws = wctr[0] % 3
                            wctr[0] += 1
                            S.op("sp", lambda e, j=j, ws=ws: e.dma_start(out=wr[ws][:], in_=dap(w2b, (l * 8 + j) * 128 * 4096, [[4096, 128], [1, 4096]])),
                                 writes=[("wr", ws)], dma=("wr", ws), extra_deps=wc[l])
                            b = bank()
                            for fc in range(32):
                                S.op("pe", lambda e, b=b, ws=ws, fc=fc: e.matmul(PS[b][:], lhsT=wr[ws][:, fc * 128:(fc + 1) * 128], rhs=hT[:, fc, :], start=(fc == 0), stop=(fc == 31)),
                                     reads=[("wr", ws), ("hT", fc)], writes=[("ps", b)])
                            S.op("dve", lambda e, b=b, j=j: e.scalar_tensor_tensor(out=z[:, j, :], in0=z[:, j, :], scalar=ALPHA, in1=PS[b][:], op0=ALU.mult, op1=ALU.add), reads=[("ps", b), ("z", s, j)], writes=[("z", s, j)])
                        bs, bq = ln_stats(z, s)
                        ln_norm(z, s, bs, bq, g2, b2, xob)
                        zr = [("z", s, j) for j in range(8)]
                        if not last:
                            S.op("sp", lambda e, t=t, s=s: e.dma_start(out=dap(xTf, t * 512, [[SD, 128], [128 * SD, 8], [1, 512]]), in_=xz[s][:]), reads=zr, dma=("xzst", s))
                            S.op("sp", lambda e, t=t: e.dma_start(out=dap(xTb, t * 512, [[SD, 128], [128 * SD, 8], [1, 512]]), in_=xob[:]), reads=[("ob", j) for j in range(8)], dma="xobst")
                        else:
                            def fin(t=t, s=s):
                                for sub in range(4):
                                    k = sub % 2
                                    for hf in range(2):
                                        b = bank()
                                        for dl in range(4):
                                            dc = hf * 4 + dl
                                            S.op("pe", lambda e, b=b, sub=sub, dc=dc, dl=dl, s=s: e.transpose(out=PS[b][:, dl * 128:(dl + 1) * 128], in_=xz[s][:, dc, sub * 128:(sub + 1) * 128], identity=ident[:]),
                                                 reads=[("z", s, dc)], writes=[("ps", b)])
                                        S.op("act", lambda e, b=b, k=k, hf=hf: e.copy(out=yst[k][:, hf * 512:(hf + 1) * 512], in_=PS[b][:]), reads=[("ps", b)], writes=[("yst", k)])
                                    S.op("sp", lambda e, t=t, sub=sub, k=k: e.dma_start(out=dap(y_out, (t * 512 + sub * 128) * D, [[D, 128], [1, D]]), in_=yst[k][:]), reads=[("yst", k)], dma=("yst", k))
                            pend[0] = fin
                    if pend[0] is not None:
                        pend[0]()
                    S.barrier()

            kstop = int(os.environ.get("KSTOP", "99"))
            for l in range(L):
                if kstop >= 2:
                    phase_a(l)
                if kstop >= 3:
                    attention(l)
                if kstop >= 4:
                    phase_c(l)

    prog(S)
    with contextlib.ExitStack() as st:
        S.start_real(st)
        prog(S)
    return nc, S


_CACHE = {}


def _consts():
    i = np.arange(1280)
    b = np_bucket(639 - i)
    oh = (b[None, :] == np.arange(32)[:, None]).astype(np.float32)
    ident = np.eye(128, dtype=np.float32)
    anti = np.ascontiguousarray(ident[::-1])
    return oh, ident, anti


def run_frames(frames, valids, weights, SD, L):
    key = (SD, L)
    if key not in _CACHE:
        _CACHE[key] = build_program(SD, L)[0]
    nc = _CACHE[key]
    oh, ident, anti = _consts()
    in_maps = []
    for x, nv in zip(frames, valids):
        tok = (np.arange(SD) < nv).astype(np.float32)
        vcol = np.ascontiguousarray(tok.reshape(SD // 128, 128).T)
        vrow = np.ascontiguousarray(np.broadcast_to(tok[None, :], (128, SD)))
        kt_valid = tok.reshape(SD // 128, 128)[:, 0]
        kmask = np.ascontiguousarray(np.broadcast_to(((1.0 - kt_valid) * NEG).astype(np.float32)[None, :], (128, SD // 128)))
        m = {"x": x, "vcol": vcol, "vrow": vrow, "kmask": kmask, "ohrev": oh, "ident": ident, "anti": anti}
        m.update(weights)
        if os.environ.get("KFAST") == "1":
            for k in ("w_in", "w_out", "w_mlp1", "w_mlp2"):
                m[k] = np.zeros((1, 8, 8), np.float32)
        in_maps.append(m)
    res = run_bass_kernel_spmd(nc, in_maps, core_ids=list(range(8)))
    return [r["y"] for r in res.results]


def kernel(x_prompt, x_sample, w_in, w_out, conv_w, conv_b, lambda_q1, lambda_k1, lambda_q2,
           lambda_k2, subln_g, rel_bias, ln1_g, ln1_b, w_mlp1, w_mlp2, ln2_g, ln2_b):
    SD = 8192
    f = lambda a: np.ascontiguousarray(np.asarray(a, dtype=np.float32))
    weights = {"w_in": f(w_in), "w_out": f(w_out), "conv_w": f(conv_w), "conv_b": f(conv_b),
               "lambda_q1": f(lambda_q1), "lambda_k1": f(lambda_k1), "lambda_q2": f(lambda_q2),
               "lambda_k2": f(lambda_k2), "subln_g": f(subln_g), "rel_bias": f(rel_bias),
               "ln1_g": f(ln1_g), "ln1_b": f(ln1_b), "w_mlp1": f(w_mlp1), "w_mlp2": f(w_mlp2),
               "ln2_g": f(ln2_g), "ln2_b": f(ln2_b)}
    xp = f(x_prompt)
    xs = f(x_sample)
    frames = [xp[b] for b in range(4)]
    valids = [SD] * 4
    for b in range(4):
        fr = np.zeros((SD, D), np.float32)
        fr[:xs.shape[1]] = xs[b]
        frames.append(fr)
        valids.append(xs.shape[1])
    ys = run_frames(frames, valids, weights, SD, 4)
    y_prompt = np.stack([ys[b] for b in range(4)], axis=0)
    y_sample = np.stack([ys[4 + b][:xs.shape[1]] for b in range(4)], axis=0)
    return (y_prompt, y_sample)
```

```python
import contextlib
import os
import numpy as np
import concourse.bass as bass
import concourse.mybir as mybir
from concourse.bass_utils import run_bass_kernel_spmd

F32 = mybir.dt.float32
BF16 = mybir.dt.bfloat16
AF = mybir.ActivationFunctionType
ALU = mybir.AluOpType


class _Dummy:
    def __getitem__(self, k):
        return self

    def __getattr__(self, k):
        return self

    def __call__(self, *a, **k):
        return self


class Sched:
    CENG = ("pe", "act", "dve", "pool")

    def __init__(self, nc):
        self.nc = nc
        self.needed = None
        self.reset(dry=True)

    def reset(self, dry):
        self.dry = dry
        self.n = 0
        self.eng_of = []
        self.dma_of = []
        self.lastw = {}
        self.readers = {}
        self.need_now = set()
        self.sigcount = {e: 0 for e in self.CENG}
        self.sig_of = {}
        self.dma_cnt = {}
        self.seen = {e: {} for e in self.CENG + ("sp",)}
        self.last_on = {}
        self.dma_out = []
        self.nwaits = 0
        self.ninstr = 0

    def start_real(self, stack):
        self.needed = self.need_now
        self.reset(dry=False)
        self.stack = stack
        self.esem = {e: stack.enter_context(self.nc.semaphore("es_" + e)) for e in self.CENG}
        self.dsem = {}

    def eng(self, e):
        nc = self.nc
        return {"pe": nc.tensor, "act": nc.scalar, "dve": nc.vector, "pool": nc.gpsimd, "sp": nc.sync}[e]

    def _dsem(self, key):
        if key not in self.dsem:
            self.dsem[key] = self.stack.enter_context(self.nc.semaphore("ds_%d" % len(self.dsem)))
        return self.dsem[key]

    def op(self, eng, fn, reads=(), writes=(), dma=None, extra_deps=()):
        i = self.n
        self.n += 1
        hard = set(extra_deps)
        war = set()
        for k in reads:
            w = self.lastw.get(k)
            if w is not None:
                hard.add(w)
        for k in writes:
            w = self.lastw.get(k)
            if w is not None:
                hard.add(w)
            rd = self.readers.get(k)
            if rd:
                war.update(rd.values())
        for k in writes:
            self.lastw[k] = i
            self.readers[k] = {}
        for k in reads:
            self.readers.setdefault(k, {})[("d", i) if dma else eng] = i
        war -= hard
        self.eng_of.append(eng)
        if dma is not None:
            c = self.dma_cnt.get(dma, 0) + 16
            self.dma_cnt[dma] = c
            self.dma_of.append((dma, c))
            self.dma_out.append(i)
        else:
            self.dma_of.append(None)
        waits = []
        for d, is_war in [(d, False) for d in hard] + [(d, True) for d in war]:
            dd = self.dma_of[d]
            if dd is not None:
                waits.append(("d", dd[0], dd[1]))
                continue
            de = self.eng_of[d]
            if de == eng and dma is None:
                if eng == "pe" or is_war:
                    continue
            self.need_now.add(d)
            if not self.dry:
                waits.append(("e", de, self.sig_of[d]))
        if dma is None and fn is not None:
            if not self.dry and i in self.needed:
                self.sigcount[eng] += 1
                self.sig_of[i] = self.sigcount[eng]
            self.last_on[eng] = i
        if self.dry:
            return i
        E = self.eng(eng)
        seen = self.seen[eng]
        for kind, a, v in waits:
            key = (kind, a)
            if seen.get(key, 0) >= v:
                continue
            seen[key] = v
            sem = self._dsem(a) if kind == "d" else self.esem[a]
            E.wait_ge(sem, v)
            self.nwaits += 1
        if fn is not None:
            ins = fn(E)
            self.ninstr += 1
            if dma is not None:
                ins.then_inc(self._dsem(dma), 16)
            elif i in self.needed:
                ins.then_inc(self.esem[eng], 1)
        return i

    def barrier(self, engines=None):
        deps = list(self.last_on.values()) + list(self.dma_out)
        for e in (engines or (self.CENG + ("sp",))):
            self.op(e, None, extra_deps=deps)
        if engines is None:
            self.dma_out = []
            self.lastw = {}
            self.readers = {}

    @contextlib.contextmanager
    def scope(self):
        sc = _Scope(self)
        try:
            yield sc
        finally:
            sc.close()


class _Scope:
    def __init__(self, S):
        self.S = S
        self.stack = None if S.dry else contextlib.ExitStack()

    def sbuf(self, name, shape, dtype):
        if self.S.dry:
            return _Dummy()
        self.S.uid = getattr(self.S, "uid", 0) + 1
        return self.stack.enter_context(self.S.nc.sbuf_tensor("sb%d_%s" % (self.S.uid, name), list(shape), dtype))

    def psum(self, name, shape, dtype):
        if self.S.dry:
            return _Dummy()
        self.S.uid = getattr(self.S, "uid", 0) + 1
        return self.stack.enter_context(self.S.nc.psum_tensor("pp%d_%s" % (self.S.uid, name), list(shape), dtype))

    def close(self):
        if self.stack is not None:
            self.stack.close()


import math

D = 1024
LN_EPS = 1e-5
ALPHA = 8.0 ** 0.25
NEG = -30000.0


def lam_init(l):
    return 0.8 - 0.6 * math.exp(-0.3 * l)


def np_bucket(rel):
    half = 16
    me = 8
    ret = np.where(rel > 0, half, 0)
    n = np.abs(rel)
    nf = np.maximum(n, 1).astype(np.float32)
    large = me + (np.log(nf / np.float32(me)) / np.float32(math.log(128 / 8)) * np.float32(8)).astype(np.int32)
    large = np.minimum(large, half - 1)
    return ret + np.where(n < me, n, large)


def build_program(SD, L):
    nc = bass.Bass("TRN2", target_bir_lowering=False)
    NT = SD // 512
    NKT = SD // 128
    AX = mybir.AxisListType.X

    def din(name, shape):
        return nc.dram_tensor(name, list(shape), F32, kind="ExternalInput").ap()

    def dscr(name, shape, dt):
        return nc.dram_tensor(name, list(shape), dt, kind="Internal").ap()

    def dap(t, off, dims):
        return bass.AP(t.tensor, off, [list(d) for d in dims])

    x_in = din("x", [SD, D])
    vcol_in = din("vcol", [128, NKT])
    vrow_in = din("vrow", [128, SD])
    kmask_in = din("kmask", [128, NKT])
    oh_in = din("ohrev", [32, 1280])
    ident_in = din("ident", [128, 128])
    anti_in = din("anti", [128, 128])
    KFAST = os.environ.get("KFAST") == "1"
    KSET = int(os.environ.get("KSET", "99"))
    w_in = din("w_in", [4, D, 3072] if not KFAST else [1, 8, 8])
    w_out = din("w_out", [4, D, D] if not KFAST else [1, 8, 8])
    conv_w = din("conv_w", [4, 3, 512])
    conv_b = din("conv_b", [4, 512])
    lq1 = din("lambda_q1", [4, 64]); lk1 = din("lambda_k1", [4, 64])
    lq2 = din("lambda_q2", [4, 64]); lk2 = din("lambda_k2", [4, 64])
    subln_g = din("subln_g", [4, 128])
    rel_bias = din("rel_bias", [32, 4])
    ln1_g = din("ln1_g", [4, D]); ln1_b = din("ln1_b", [4, D])
    w_mlp1 = din("w_mlp1", [4, D, 4096] if not KFAST else [1, 8, 8]); w_mlp2 = din("w_mlp2", [4, 4096, D] if not KFAST else [1, 8, 8])
    ln2_g = din("ln2_g", [4, D]); ln2_b = din("ln2_b", [4, D])
    y_out = nc.dram_tensor("y", [SD, D], F32, kind="ExternalOutput").ap()

    winb = dscr("winb", [L, D, 3072], BF16)
    woutb = dscr("woutb", [L, D, D], BF16)
    w1b = dscr("w1b", [L, D, 4096], BF16)
    w2b = dscr("w2b", [L * 8 * 128, 4096], BF16)
    xTf = dscr("xTf", [D, SD], F32)
    xTb = dscr("xTb", [D, SD], BF16)
    QT = dscr("QT", [512, SD], BF16)
    KT = dscr("KT", [512, SD], BF16)
    Vs = dscr("Vs", [SD, 512], BF16)
    Us = dscr("Us", [512, SD + 2], F32)
    GBs = dscr("GBs", [512, SD], F32)
    ATs = dscr("ATs", [512, SD], BF16)
    Tscr = dscr("Tscr", [4, 1280], F32)
    Gscr = dscr("Gscr", [128, 4 * 1152], F32)

    S = Sched(nc)
    bankctr = [0]

    def prog(S):
        bankctr[0] = 0
        with S.scope() as pc:
            ident = pc.sbuf("ident", [128, 128], F32)
            anti = pc.sbuf("anti", [128, 128], F32)
            ones_f = pc.sbuf("ones_f", [128, 128], F32)
            ones_b = pc.sbuf("ones_b", [128, 128], BF16)
            vcol = pc.sbuf("vcol", [128, NKT], F32)
            kmask = pc.sbuf("kmask", [128, NKT], F32)
            cb = pc.sbuf("cb", [128, 4, 3, NKT], F32)
            neglam = pc.sbuf("neglam", [128, 4], F32)
            gsc = pc.sbuf("gsc", [128, 4], F32)
            g1 = pc.sbuf("g1", [128, 4, 8], F32); b1 = pc.sbuf("b1", [128, 4, 8], F32)
            g2 = pc.sbuf("g2", [128, 4, 8], F32); b2 = pc.sbuf("b2", [128, 4, 8], F32)
            cw = pc.sbuf("cw", [128, 4, 3, 4], F32)
            cbs = pc.sbuf("cbs", [128, 4, 4], F32)
            PSW = [pc.psum("psw%d" % i, [128, 1024], F32) for i in range(4)]
            PS = [PSW[i // 2][:, (i % 2) * 512:(i % 2 + 1) * 512] for i in range(8)]
            epsb = pc.sbuf("epsb", [128, 1], F32)

            def bank():
                i = bankctr[0] % 8
                bankctr[0] += 1
                return i

            wc = {}
            prevc = []

            def cdma(fn, wkey, l):
                i = S.op("pool", fn, writes=[wkey], dma=("wc", l), extra_deps=list(prevc))
                prevc[:] = [i]
                return i
            for l in range(L if not KFAST else 0):
                ops = []
                for q in range(4):
                    ops.append(cdma(lambda e, l=l, q=q: e.dma_start(out=dap(winb, l * D * 3072 + q * 768 * 1024, [[1024, 768], [1, 1024]]),
                                                                     in_=dap(w_in, l * D * 3072 + q * 768 * 1024, [[1024, 768], [1, 1024]])),
                                    ("winb", l, q), l))
                ops.append(cdma(lambda e, l=l: e.dma_start(out=dap(woutb, l * D * D, [[1024, 1024], [1, 1024]]),
                                                            in_=dap(w_out, l * D * D, [[1024, 1024], [1, 1024]])),
                                ("woutb", l), l))
                for hf in range(4):
                    ops.append(cdma(lambda e, l=l, hf=hf: e.dma_start(
                        out=dap(w1b, l * D * 4096 + hf * 1024 * 1024, [[1024, 1024], [1, 1024]]),
                        in_=dap(w_mlp1, l * D * 4096 + hf * 1024 * 1024, [[1024, 1024], [1, 1024]])),
                        ("w1b", l, hf), l))
                for j in range(8):
                    ops.append(cdma(lambda e, l=l, j=j: e.dma_start(
                        out=dap(w2b, (l * 8 + j) * 128 * 4096, [[4096, 128], [128, 32], [1, 128]]),
                        in_=dap(w_mlp2, l * 4096 * D + j * 128, [[1024, 128], [128 * 1024, 32], [1, 128]])),
                        ("w2b", l, j), l))
                wc[l] = ops

            with S.scope() as sc:
                lam4 = sc.sbuf("lam4", [128, 4, 4, 64], F32)
                prod = sc.sbuf("prod", [128, 2, 4, 64], F32)
                ss = sc.sbuf("ss", [128, 2, 4], F32)
                ee = sc.sbuf("ee", [128, 2, 4], F32)
                lam = sc.sbuf("lam", [128, 4], F32)
                cl = sc.sbuf("cl", [128, 4], F32); cr = sc.sbuf("cr", [128, 4], F32)
                rb = sc.sbuf("rb", [32, 4], F32)
                oh = sc.sbuf("oh", [32, 1280], F32)
                tsb = sc.sbuf("tsb", [4, 1280], F32)
                hk = sc.sbuf("hk", [128, 4, 1152], F32)
                G = sc.sbuf("Gs", [128, 4, 1152], F32)
                zt = sc.sbuf("zt", [128, 4, 1], F32)

                def ld(out_fn, in_fn, nonc=False):
                    if nonc:
                        S.op("sp", lambda e: e.dma_start(out=out_fn(), in_=in_fn(), allow_slow_non_contiguous=True), dma="setup")
                    else:
                        S.op("sp", lambda e: e.dma_start(out=out_fn(), in_=in_fn()), dma="setup")
                ld(lambda: ident[:], lambda: ident_in)
                ld(lambda: anti[:], lambda: anti_in)
                ld(lambda: vcol[:], lambda: vcol_in)
                ld(lambda: kmask[:], lambda: kmask_in)
                ld(lambda: oh[:], lambda: oh_in)
                ld(lambda: rb[:], lambda: rel_bias)
                for i, t in enumerate((lq1, lk1, lq2, lk2)):
                    ld(lambda i=i: lam4[:, i, :, :], lambda t=t: dap(t, 0, [[0, 128], [64, 4], [1, 64]]))
                ld(lambda: cl[:], lambda: dap(rel_bias, 15 * 4, [[0, 128], [1, 4]]))
                ld(lambda: cr[:], lambda: dap(rel_bias, 31 * 4, [[0, 128], [1, 4]]))
                ld(lambda: gsc[:], lambda: dap(subln_g, 0, [[1, 128], [128, 4]]), nonc=True)
                for dst, src in ((g1, ln1_g), (b1, ln1_b), (g2, ln2_g), (b2, ln2_b)):
                    for l in range(4):
                        ld(lambda dst=dst, l=l: dst[:, l, :], lambda src=src, l=l: dap(src, l * D, [[1, 128], [128, 8]]), nonc=True)
                for l in range(4):
                    for k in range(3):
                        ld(lambda l=l, k=k: cw[:, l, k, :], lambda l=l, k=k: dap(conv_w, (l * 3 + k) * 512, [[1, 128], [128, 4]]), nonc=True)
                    ld(lambda l=l: cbs[:, l, :], lambda l=l: dap(conv_b, l * 512, [[1, 128], [128, 4]]), nonc=True)
                S.op("dve", lambda e: e.memset(ones_f[:], 1.0))
                S.op("dve", lambda e: e.memset(epsb[:], LN_EPS))
                S.op("dve", lambda e: e.memset(ones_b[:], 1.0))
                S.op("dve", lambda e: e.memset(zt[:], 0.0))
                S.barrier()
                if KSET <= 1:
                    return
                S.op("sp", lambda e: e.dma_start(out=dap(Us, 0, [[SD + 2, 128], [128 * (SD + 2), 4], [1, 1]]), in_=zt[:], allow_slow_non_contiguous=True), dma="setup")
                S.op("sp", lambda e: e.dma_start(out=dap(Us, SD + 1, [[SD + 2, 128], [128 * (SD + 2), 4], [1, 1]]), in_=zt[:], allow_slow_non_contiguous=True), dma="setup")
                S.op("dve", lambda e: e.tensor_tensor(out=prod[:, 0, :, :], in0=lam4[:, 0, :, :], in1=lam4[:, 1, :, :], op=ALU.mult))
                S.op("dve", lambda e: e.tensor_tensor(out=prod[:, 1, :, :], in0=lam4[:, 2, :, :], in1=lam4[:, 3, :, :], op=ALU.mult))
                S.barrier()
                if KSET <= 2:
                    return
                S.op("dve", lambda e: e.reduce_sum(out=ss[:], in_=prod[:], axis=AX))
                S.barrier()
                if KSET <= 3:
                    return
                S.op("act", lambda e: e.activation(out=ee[:], in_=ss[:], func=AF.Exp))
                S.barrier()
                if KSET <= 4:
                    return
                S.op("dve", lambda e: e.tensor_tensor(out=lam[:], in0=ee[:, 0, :], in1=ee[:, 1, :], op=ALU.subtract))
                S.barrier()
                if KSET <= 5:
                    return
                for l in range(4):
                    S.op("dve", lambda e, l=l: e.tensor_scalar(out=neglam[:, l:l + 1], in0=lam[:, l:l + 1], scalar1=lam_init(l), scalar2=-1.0, op0=ALU.add, op1=ALU.mult))
                    S.op("dve", lambda e, l=l: e.tensor_scalar(out=gsc[:, l:l + 1], in0=gsc[:, l:l + 1], scalar1=1.0 - lam_init(l), scalar2=None, op0=ALU.mult))
                for h in range(4):
                    S.op("dve", lambda e, h=h: e.tensor_scalar(out=cb[:, h, 0, :], in0=kmask[:], scalar1=cl[:, h:h + 1], scalar2=None, op0=ALU.add))
                    S.op("dve", lambda e, h=h: e.tensor_copy(out=cb[:, h, 1, :], in_=kmask[:]))
                    S.op("dve", lambda e, h=h: e.tensor_scalar(out=cb[:, h, 2, :], in0=kmask[:], scalar1=cr[:, h:h + 1], scalar2=None, op0=ALU.add))
                for c, w in ((0, 512), (1, 512), (2, 256)):
                    S.op("pe", lambda e, c=c, w=w: e.matmul(PS[c][0:4, 0:w], lhsT=rb[:], rhs=oh[:, c * 512:c * 512 + w], start=True, stop=True))
                S.barrier()
                if KSET <= 6:
                    return
                for c, w in ((0, 512), (1, 512), (2, 256)):
                    S.op("act", lambda e, c=c, w=w: e.copy(out=tsb[:, c * 512:c * 512 + w], in_=PS[c][0:4, 0:w]))
                S.barrier()
                if KSET <= 7:
                    return
                S.op("sp", lambda e: e.dma_start(out=Tscr, in_=tsb[:]), dma="setup")
                S.barrier()
                if KSET <= 8:
                    return
                for h in range(4):
                    S.op("sp", lambda e, h=h: e.dma_start(out=hk[:, h, :], in_=dap(Tscr, h * 1280, [[1, 128], [1, 1152]])), dma="setup")
                S.barrier()
                if KSET <= 9:
                    return
                for h in range(4):
                    for c, w in ((0, 512), (1, 512), (2, 128)):
                        S.op("pe", lambda e, h=h, c=c, w=w: e.matmul(PS[3 + c][:, 0:w], lhsT=anti[:], rhs=hk[:, h, c * 512:c * 512 + w], start=True, stop=True))
                    S.barrier()
                    if KSET <= 10:
                        return
                    for c, w in ((0, 512), (1, 512), (2, 128)):
                        S.op("act", lambda e, h=h, c=c, w=w: e.copy(out=G[:, h, c * 512:c * 512 + w], in_=PS[3 + c][:, 0:w]))
                    S.barrier()
                    if KSET <= 11:
                        return
                S.barrier()
                if KSET <= 12:
                    return
                S.op("sp", lambda e: e.dma_start(out=Gscr, in_=G[:].rearrange("p h m -> p (h m)")), dma="setup")
                S.barrier()
                if KSET <= 13:
                    return

            with S.scope() as sc:
                xin = [sc.sbuf("xin%d" % i, [128, 4, D], F32) for i in range(2)]
                xfs = [sc.sbuf("xfs%d" % i, [128, 8, 512], F32) for i in range(2)]
                xbs = [sc.sbuf("xbs%d" % i, [128, 8, 512], BF16) for i in range(2)]
                XV = int(os.environ.get("XV", "0"))
                for t in range(NT):
                    s = t % 2
                    S.op("sp", lambda e, t=t, s=s: e.dma_start(out=xin[s][:], in_=dap(x_in, t * 512 * D, [[D, 128], [128 * D, 4], [1, D]])),
                         writes=[("xin", s)], dma=("xin", s))
                    if XV == 1:
                        continue
                    for dc in range(8):
                        b = bank()
                        for sub in range(4):
                            if XV == 2 and sub > 0:
                                continue
                            S.op("pe", lambda e, b=b, s=s, sub=sub, dc=dc: e.transpose(out=PS[b][:, sub * 128:(sub + 1) * 128], in_=xin[s][:, sub, dc * 128:(dc + 1) * 128], identity=ident[:]),
                                 reads=[("xin", s)], writes=[("ps", b)])
                        if XV == 4:
                            continue
                        S.op("act", lambda e, b=b, s=s, dc=dc: e.copy(out=xfs[s][:, dc, :], in_=PS[b][:]), reads=[("ps", b)], writes=[("xfs", s)])
                        if XV == 5:
                            continue
                        S.op("dve", lambda e, b=b, s=s, dc=dc: e.tensor_copy(out=xbs[s][:, dc, :], in_=xfs[s][:, dc, :]), reads=[("xfs", s)], writes=[("xbs", s)])
                    if XV >= 3:
                        continue
                    S.op("sp", lambda e, t=t, s=s: e.dma_start(out=dap(xTf, t * 512, [[SD, 128], [128 * SD, 8], [1, 512]]), in_=xfs[s][:]),
                         reads=[("xfs", s)], dma=("xst", s))
                    S.op("sp", lambda e, t=t, s=s: e.dma_start(out=dap(xTb, t * 512, [[SD, 128], [128 * SD, 8], [1, 512]]), in_=xbs[s][:]),
                         reads=[("xbs", s)], dma=("xst", s))
                S.barrier()

            def phase_a(l):
                with S.scope() as sc:
                    win = sc.sbuf("win", [128, 8, 3072], BF16)
                    xb = [sc.sbuf("xb%d" % i, [128, 8, 512], BF16) for i in range(2)]
                    vr = [sc.sbuf("vr%d" % i, [128, 512], F32) for i in range(2)]
                    qkst = [sc.sbuf("qkst%d" % i, [128, 8, 512], BF16) for i in range(2)]
                    vst = [sc.sbuf("vst%d" % i, [128, 4, 512], BF16) for i in range(2)]
                    hcs = [sc.sbuf("hcs%d" % i, [128, 512], F32) for i in range(2)]
                    ust = [sc.sbuf("ust%d" % i, [128, 4, 512], F32) for i in range(2)]
                    gbst = [sc.sbuf("gbst%d" % i, [128, 4, 512], F32) for i in range(2)]
                    for hf in range(2):
                        S.op("sp", lambda e, hf=hf: e.dma_start(out=win[:, hf * 4:(hf + 1) * 4, :], in_=dap(winb, l * D * 3072 + hf * 4 * 128 * 3072, [[3072, 128], [128 * 3072, 4], [1, 3072]])),
                             writes=[("win", hf)], dma=("win", hf), extra_deps=wc[l][0:4])

                    def loads(t):
                        s = t % 2
                        S.op("sp", lambda e: e.dma_start(out=xb[s][:], in_=dap(xTb, t * 512, [[SD, 128], [128 * SD, 8], [1, 512]])),
                             writes=[("xb", s)], dma=("xb", s))
                        S.op("sp", lambda e: e.dma_start(out=vr[s][:], in_=dap(vrow_in, t * 512, [[SD, 128], [1, 512]])),
                             writes=[("vr", s)], dma=("vr", s))
                    loads(0)
                    hk_ = 0
                    for t in range(NT):
                        s = t % 2
                        if t + 1 < NT:
                            loads(t + 1)
                        for c in range(8):
                            b = bank()
                            for dc in range(8):
                                S.op("pe", lambda e, b=b, c=c, dc=dc: e.matmul(PS[b][:], lhsT=win[:, dc, c * 128:(c + 1) * 128], rhs=xb[s][:, dc, :], start=(dc == 0), stop=(dc == 7)),
                                     reads=[("win", dc // 4), ("xb", s)], writes=[("ps", b)])
                            if c % 2 == 0:
                                S.op("act", lambda e, b=b, c=c: e.copy(out=qkst[s][:, c, :], in_=PS[b][:]), reads=[("ps", b)], writes=[("qkst", s)])
                            else:
                                S.op("dve", lambda e, b=b, c=c: e.tensor_copy(out=qkst[s][:, c, :], in_=PS[b][:]), reads=[("ps", b)], writes=[("qkst", s)])
                        S.op("sp", lambda e: e.dma_start(out=dap(QT, t * 512, [[SD, 128], [128 * SD, 4], [1, 512]]), in_=qkst[s][:, 0:4, :]), reads=[("qkst", s)], dma=("qkst", s))
                        S.op("sp", lambda e: e.dma_start(out=dap(KT, t * 512, [[SD, 128], [128 * SD, 4], [1, 512]]), in_=qkst[s][:, 4:8, :]), reads=[("qkst", s)], dma=("qkst", s))
                        for sub in range(4):
                            b = bank()
                            for dc in range(8):
                                S.op("pe", lambda e, b=b, sub=sub, dc=dc: e.matmul(PS[b][:], lhsT=xb[s][:, dc, sub * 128:(sub + 1) * 128], rhs=win[:, dc, 1024:1536], start=(dc == 0), stop=(dc == 7)),
                                     reads=[("win", dc // 4), ("xb", s)], writes=[("ps", b)])
                            S.op("dve", lambda e, b=b, sub=sub: e.tensor_scalar(out=vst[s][:, sub, :], in0=PS[b][:], scalar1=vcol[:, t * 4 + sub:t * 4 + sub + 1], scalar2=None, op0=ALU.mult),
                                 reads=[("ps", b)], writes=[("vst", s)])
                        S.op("sp", lambda e: e.dma_start(out=dap(Vs, t * 512 * 512, [[512, 128], [128 * 512, 4], [1, 512]]), in_=vst[s][:]), reads=[("vst", s)], dma=("vst", s))
                        for cc in range(4):
                            bh = bank()
                            for dc in range(8):
                                S.op("pe", lambda e, b=bh, cc=cc, dc=dc: e.matmul(PS[b][:], lhsT=win[:, dc, 2560 + cc * 128:2560 + (cc + 1) * 128], rhs=xb[s][:, dc, :], start=(dc == 0), stop=(dc == 7)),
                                     reads=[("win", dc // 4), ("xb", s)], writes=[("ps", bh)])
                            k = hk_ % 2
                            hk_ += 1
                            S.op("dve", lambda e, b=bh, k=k: e.tensor_tensor(out=hcs[k][:], in0=PS[b][:], in1=vr[s][:], op=ALU.mult),
                                 reads=[("ps", bh), ("vr", s)], writes=[("hcs", k)])
                            bg = bank()
                            for dc in range(8):
                                S.op("pe", lambda e, b=bg, cc=cc, dc=dc: e.matmul(PS[b][:], lhsT=win[:, dc, 2048 + cc * 128:2048 + (cc + 1) * 128], rhs=xb[s][:, dc, :], start=(dc == 0), stop=(dc == 7)),
                                     reads=[("win", dc // 4), ("xb", s)], writes=[("ps", bg)])
                            S.op("dve", lambda e, b=bg, k=k, cc=cc: e.tensor_tensor(out=ust[s][:, cc, :], in0=PS[b][:], in1=hcs[k][:], op=ALU.mult),
                                 reads=[("ps", bg), ("hcs", k)], writes=[("ust", s)])
                            bb = bank()
                            for dc in range(8):
                                S.op("pe", lambda e, b=bb, cc=cc, dc=dc: e.matmul(PS[b][:], lhsT=win[:, dc, 1536 + cc * 128:1536 + (cc + 1) * 128], rhs=xb[s][:, dc, :], start=(dc == 0), stop=(dc == 7)),
                                     reads=[("win", dc // 4), ("xb", s)], writes=[("ps", bb)])
                            S.op("act", lambda e, b=bb, cc=cc: e.copy(out=gbst[s][:, cc, :], in_=PS[b][:]), reads=[("ps", bb)], writes=[("gbst", s)])
                        S.op("sp", lambda e: e.dma_start(out=dap(Us, 1 + t * 512, [[SD + 2, 128], [128 * (SD + 2), 4], [1, 512]]), in_=ust[s][:]), reads=[("ust", s)], dma=("ust", s))
                        S.op("sp", lambda e: e.dma_start(out=dap(GBs, t * 512, [[SD, 128], [128 * SD, 4], [1, 512]]), in_=gbst[s][:]), reads=[("gbst", s)], dma=("gbst", s))
                    S.barrier()

            def attention(l):
                with S.scope() as sc:
                    qh = [sc.sbuf("qh%d" % i, [128, SD], BF16) for i in range(2)]
                    kh = [sc.sbuf("kh%d" % i, [128, SD], BF16) for i in range(2)]
                    vh = [sc.sbuf("vh%d" % i, [128, NKT, 128], BF16) for i in range(2)]
                    G = sc.sbuf("G", [128, 4, 1152], F32)
                    NE = 4
                    E = [sc.sbuf("E%d" % i, [128, 1024], BF16) for i in range(NE)]
                    bt = [sc.sbuf("bt%d" % i, [128, 1024], F32) for i in range(2)]
                    acc = [[sc.sbuf("acc%d_%d" % (g_, i), [128, 512], F32) for i in range(2)] for g_ in range(2)]
                    lnb = sc.sbuf("lnb", [128, 1024], F32)
                    rcp = sc.sbuf("rcp", [128, 1024], F32)
                    t1 = sc.sbuf("t1", [128, 512], F32); t2 = sc.sbuf("t2", [128, 512], F32)
                    av = sc.sbuf("av", [128, 512], F32); sq = sc.sbuf("sq", [128, 512], F32)
                    lr = sc.sbuf("lr", [128, 512], F32); rinv = sc.sbuf("rinv", [128, 512], F32)
                    aost = [sc.sbuf("aost%d" % i, [128, 512], BF16) for i in range(2)]
                    S.op("sp", lambda e: e.dma_start(out=G[:].rearrange("p h m -> p (h m)"), in_=Gscr), writes=["G"], dma="gld")

                    def hload(h):
                        s = h % 2
                        S.op("sp", lambda e: e.dma_start(out=qh[s][:], in_=dap(QT, h * 128 * SD, [[SD, 128], [1, SD]])), writes=[("qh", s)], dma=("qh", s))
                        S.op("sp", lambda e: e.dma_start(out=kh[s][:], in_=dap(KT, h * 128 * SD, [[SD, 128], [1, SD]])), writes=[("kh", s)], dma=("kh", s))
                        S.op("sp", lambda e: e.dma_start(out=vh[s][:], in_=dap(Vs, h * 128, [[512, 128], [128 * 512, NKT], [1, 128]])), writes=[("vh", s)], dma=("vh", s))
                    hload(0)
                    units = [(h, qt, kt) for h in range(4) for qt in range(NT) for kt in range(NKT)]
                    NU = len(units)

                    def scores(u):
                        h, qt, kt = units[u]
                        s = h % 2
                        p = u % 2
                        if qt == 0 and kt == 2 and h + 1 < 4:
                            hload(h + 1)
                        S.op("pe", lambda e: e.matmul(PSW[p][:, 0:512], lhsT=kh[s][0:64, kt * 128:(kt + 1) * 128], rhs=qh[s][0:64, qt * 512:(qt + 1) * 512], start=True, stop=True, tile_position=(0, 0)),
                             reads=[("qh", s), ("kh", s)], writes=[("psw", p)])
                        S.op("pe", lambda e: e.matmul(PSW[p][:, 512:1024], lhsT=kh[s][64:128, kt * 128:(kt + 1) * 128], rhs=qh[s][64:128, qt * 512:(qt + 1) * 512], start=True, stop=True, tile_position=(64, 0)),
                             reads=[("qh", s), ("kh", s)], writes=[("psw", p)])
                        j = kt - 4 * qt
                        if -1 <= j <= 4:
                            off = 512 - 128 * j
                            for m in range(2):
                                S.op("dve", lambda e, m=m: e.scalar_tensor_tensor(out=bt[p][:, m * 512:(m + 1) * 512], in0=PSW[p][:, m * 512:(m + 1) * 512], scalar=0.125, in1=G[:, h, off:off + 512], op0=ALU.mult, op1=ALU.add),
                                     reads=[("psw", p), "G"], writes=[("bt", p, m)])

                    def unit(u):
                        h, qt, kt = units[u]
                        s = h % 2
                        p = u % 2
                        r = u % NE
                        gi = (h * NT + qt) % 2
                        A = acc[gi]
                        j = kt - 4 * qt
                        if -1 <= j <= 4:
                            S.op("act", lambda e: e.activation(out=E[r][:], in_=bt[p][:], func=AF.Exp, bias=cb[:, h, 1, kt:kt + 1], scale=1.0),
                                 reads=[("bt", p, 0), ("bt", p, 1)], writes=[("E", r)])
                        else:
                            side = 0 if j < -1 else 2
                            S.op("act", lambda e: e.activation(out=E[r][:], in_=PSW[p][:], func=AF.Exp, bias=cb[:, h, side, kt:kt + 1], scale=0.125),
                                 reads=[("psw", p)], writes=[("E", r)])

                    def unit_b(u):
                        h, qt, kt = units[u]
                        s = h % 2
                        r = u % NE
                        gi = (h * NT + qt) % 2
                        A = acc[gi]
                        st = (kt == 0)
                        sp_ = (kt == NKT - 1)
                        S.op("pe", lambda e: e.matmul(PS[4][:], lhsT=vh[s][:, kt, :], rhs=E[r][:, 0:512], start=st, stop=sp_), reads=[("vh", s), ("E", r)], writes=[("ps", 4)])
                        S.op("pe", lambda e: e.matmul(PS[5][:], lhsT=vh[s][:, kt, :], rhs=E[r][:, 512:1024], start=st, stop=sp_), reads=[("vh", s), ("E", r)], writes=[("ps", 5)])
                        S.op("pe", lambda e: e.matmul(PS[6][:], lhsT=ones_b[:], rhs=E[r][:, 512:1024], start=st, stop=sp_), reads=[("E", r)], writes=[("ps", 6)])
                        a1 = kt % 2
                        if kt < 2:
                            S.op("dve", lambda e: e.tensor_copy(out=A[a1][:], in_=E[r][:, 0:512]), reads=[("E", r)], writes=[("acc", gi, a1)])
                        else:
                            S.op("dve", lambda e: e.tensor_tensor(out=A[a1][:], in0=A[a1][:], in1=E[r][:, 0:512], op=ALU.add), reads=[("E", r), ("acc", gi, a1)], writes=[("acc", gi, a1)])
                        if kt == NKT - 1:
                            epilogue(h, qt, gi)

                    epi = [0]

                    def epilogue(h, qt, gi):
                        k = epi[0] % 2
                        epi[0] += 1
                        A = acc[gi]
                        S.op("pe", lambda e: e.matmul(PS[7][:], lhsT=ones_f[:], rhs=A[0][:], start=True, stop=False), reads=[("acc", gi, 0)], writes=[("ps", 7)])
                        S.op("pe", lambda e: e.matmul(PS[7][:], lhsT=ones_f[:], rhs=A[1][:], start=False, stop=True), reads=[("acc", gi, 1)], writes=[("ps", 7)])
                        S.op("act", lambda e: e.activation(out=lnb[:], in_=PSW[3][:], func=AF.Ln), reads=[("ps", 6), ("ps", 7)], writes=["lnb"])
                        S.op("act", lambda e: e.activation(out=rcp[:], in_=lnb[:], func=AF.Exp, scale=-1.0), reads=["lnb"], writes=["rcp"])
                        S.op("dve", lambda e: e.tensor_tensor(out=t1[:], in0=PS[4][:], in1=rcp[:, 512:1024], op=ALU.mult), reads=[("ps", 4), "rcp"], writes=["t1"])
                        S.op("dve", lambda e: e.tensor_tensor(out=t2[:], in0=PS[5][:], in1=rcp[:, 0:512], op=ALU.mult), reads=[("ps", 5), "rcp"], writes=["t2"])
                        S.op("dve", lambda e: e.scalar_tensor_tensor(out=av[:], in0=t2[:], scalar=neglam[:, l:l + 1], in1=t1[:], op0=ALU.mult, op1=ALU.add), reads=["t1", "t2"], writes=["av"])
                        S.op("dve", lambda e: e.tensor_tensor(out=sq[:], in0=av[:], in1=av[:], op=ALU.mult), reads=["av"], writes=["sq"])
                        S.op("pe", lambda e: e.matmul(PS[7][:], lhsT=ones_f[:], rhs=sq[:], start=True, stop=True), reads=["sq"], writes=[("ps", 7)])
                        S.op("act", lambda e: e.activation(out=lr[:], in_=PS[7][:], func=AF.Ln, bias=epsb[:, 0:1], scale=1.0 / 128.0), reads=[("ps", 7)], writes=["lr"])
                        S.op("act", lambda e: e.activation(out=rinv[:], in_=lr[:], func=AF.Exp, scale=-0.5), reads=["lr"], writes=["rinv"])
                        S.op("dve", lambda e: e.scalar_tensor_tensor(out=aost[k][:], in0=av[:], scalar=gsc[:, l:l + 1], in1=rinv[:], op0=ALU.mult, op1=ALU.mult), reads=["av", "rinv"], writes=[("aost", k)])
                        S.op("sp", lambda e: e.dma_start(out=dap(ATs, h * 128 * SD + qt * 512, [[SD, 128], [1, 512]]), in_=aost[k][:]), reads=[("aost", k)], dma=("aost", k))

                    scores(0)
                    if NU > 1:
                        scores(1)
                    for u in range(NU):
                        unit(u)
                        if u + 2 < NU:
                            scores(u + 2)
                        unit_b(u)
                    S.barrier()

            def phase_c(l):
                last = (l == L - 1)
                with S.scope() as sc:
                    wout = sc.sbuf("wout", [128, 8, D], BF16)
                    wr = [sc.sbuf("wr%d" % i, [128, 4096], BF16) for i in range(3)]
                    xz = [sc.sbuf("xz%d" % i, [128, 8, 512], F32) for i in range(2)]
                    mix = [sc.sbuf("mix%d" % i, [128, 8, 512], BF16) for i in range(2)]
                    ub = [sc.sbuf("ub", [128, 4, 514], F32)] * 2
                    gbb = [sc.sbuf("gbb", [128, 4, 512], F32)] * 2
                    x1b = sc.sbuf("x1b", [128, 8, 512], BF16)
                    hT = sc.sbuf("hT", [128, 32, 512], BF16)
                    xob = None if last else sc.sbuf("xob", [128, 8, 512], BF16)
                    sqb = [sc.sbuf("sqb%d" % i, [128, 512], F32) for i in range(2)]
                    sacc = [sc.sbuf("sacc%d" % i, [128, 512], F32) for i in range(4)]
                    mean = sc.sbuf("mean", [128, 512], F32); msq = sc.sbuf("msq", [128, 512], F32)
                    sd = sc.sbuf("sd", [128, 512], F32)
                    rinv = sc.sbuf("rinvc", [128, 512], F32)
                    tt = [sc.sbuf("tt%d" % i, [128, 512], F32) for i in range(2)]
                    rr = [sc.sbuf("rr%d" % i, [128, 512], F32) for i in range(3)]
                    ca = [sc.sbuf("ca%d" % i, [128, 512], F32) for i in range(2)]
                    yst = [sc.sbuf("yst%d" % i, [128, D], F32) for i in range(2)] if last else None
                    S.op("sp", lambda e: e.dma_start(out=wout[:], in_=dap(woutb, l * D * D, [[D, 128], [128 * D, 8], [1, D]])), writes=["wout"], dma="wout", extra_deps=wc[l])
                    wctr = [0]
                    cnt = [0]

                    def loads(t):
                        s = t % 2
                        S.op("sp", lambda e: e.dma_start(out=xz[s][:], in_=dap(xTf, t * 512, [[SD, 128], [128 * SD, 8], [1, 512]])), writes=[("z", s, j) for j in range(8)], dma=("xz", s))
                        S.op("sp", lambda e: e.dma_start(out=mix[s][:, 0:4, :], in_=dap(ATs, t * 512, [[SD, 128], [128 * SD, 4], [1, 512]])), writes=[("mixa", s)], dma=("mixa", s))
                        S.op("sp", lambda e: e.dma_start(out=ub[s][:], in_=dap(Us, t * 512, [[SD + 2, 128], [128 * (SD + 2), 4], [1, 514]])), writes=["ub"], dma="ub")
                        S.op("sp", lambda e: e.dma_start(out=gbb[s][:], in_=dap(GBs, t * 512, [[SD, 128], [128 * SD, 4], [1, 512]])), writes=["gbb"], dma="gbb")

                    def conv(t):
                        s = t % 2
                        for cc in range(4):
                            S.op("dve", lambda e, cc=cc: e.tensor_scalar(out=ca[0][:], in0=ub[s][:, cc, 0:512], scalar1=cw[:, l, 0, cc:cc + 1], scalar2=None, op0=ALU.mult), reads=["ub"], writes=[("ca", 0)])
                            S.op("dve", lambda e, cc=cc: e.scalar_tensor_tensor(out=ca[1][:], in0=ub[s][:, cc, 1:513], scalar=cw[:, l, 1, cc:cc + 1], in1=ca[0][:], op0=ALU.mult, op1=ALU.add), reads=["ub", ("ca", 0)], writes=[("ca", 1)])
                            S.op("dve", lambda e, cc=cc: e.scalar_tensor_tensor(out=ca[0][:], in0=ub[s][:, cc, 2:514], scalar=cw[:, l, 2, cc:cc + 1], in1=ca[1][:], op0=ALU.mult, op1=ALU.add), reads=["ub", ("ca", 1)], writes=[("ca", 0)])
                            S.op("dve", lambda e, cc=cc: e.scalar_tensor_tensor(out=mix[s][:, 4 + cc, :], in0=ca[0][:], scalar=cbs[:, l, cc:cc + 1], in1=gbb[s][:, cc, :], op0=ALU.add, op1=ALU.mult), reads=[("ca", 0), "gbb"], writes=[("mixc", s, cc)])

                    def stat_acc(z, zs, j):
                        g = j // 4
                        i = j % 4
                        if i == 1:
                            S.op("dve", lambda e: e.tensor_tensor(out=sacc[g][:], in0=z[:, j - 1, :], in1=z[:, j, :], op=ALU.add), reads=[("z", zs, j - 1), ("z", zs, j)], writes=[("sacc", g)])
                        elif i >= 2:
                            S.op("dve", lambda e: e.tensor_tensor(out=sacc[g][:], in0=sacc[g][:], in1=z[:, j, :], op=ALU.add), reads=[("z", zs, j), ("sacc", g)], writes=[("sacc", g)])
                        if i == 0:
                            S.op("act", lambda e: e.activation(out=sacc[2 + g][:], in_=z[:, j, :], func=AF.Square), reads=[("z", zs, j)], writes=[("sacc", 2 + g)])
                        else:
                            k = cnt[0] % 2
                            cnt[0] += 1
                            S.op("act", lambda e: e.activation(out=sqb[k][:], in_=z[:, j, :], func=AF.Square), reads=[("z", zs, j)], writes=[("sqb", k)])
                            S.op("dve", lambda e: e.tensor_tensor(out=sacc[2 + g][:], in0=sacc[2 + g][:], in1=sqb[k][:], op=ALU.add), reads=[("sqb", k), ("sacc", 2 + g)], writes=[("sacc", 2 + g)])

                    def ln_stats(z, zs):
                        bs = bank()
                        for g in range(2):
                            S.op("pe", lambda e, g=g: e.matmul(PS[bs][:], lhsT=ones_f[:], rhs=sacc[g][:], start=(g == 0), stop=(g == 1)), reads=[("sacc", g)], writes=[("ps", bs)])
                        bq = bank()
                        for g in range(2):
                            S.op("pe", lambda e, g=g: e.matmul(PS[bq][:], lhsT=ones_f[:], rhs=sacc[2 + g][:], start=(g == 0), stop=(g == 1)), reads=[("sacc", 2 + g)], writes=[("ps", bq)])
                        return bs, bq

                    def ln_norm(z, zs, bs, bq, gt, bt_, outb):
                        S.op("act", lambda e: e.mul(out=mean[:], in_=PS[bs][:], mul=1.0 / D), reads=[("ps", bs)], writes=["mean"])
                        S.op("dve", lambda e: e.tensor_tensor(out=msq[:], in0=mean[:], in1=mean[:], op=ALU.mult), reads=["mean"], writes=["msq"])
                        S.op("dve", lambda e: e.scalar_tensor_tensor(out=rinv[:], in0=PS[bq][:], scalar=1.0 / D, in1=msq[:], op0=ALU.mult, op1=ALU.subtract), reads=[("ps", bq), "msq"], writes=["rinvc"])
                        S.op("act", lambda e: e.activation(out=sd[:], in_=rinv[:], func=AF.Sqrt, bias=LN_EPS, scale=1.0), reads=["rinvc"], writes=["sd"])
                        S.op("dve", lambda e: e.reciprocal(out=rinv[:], in_=sd[:]), reads=["sd"], writes=["rinvc"])
                        for j in range(8):
                            k = j % 2
                            S.op("dve", lambda e, j=j, k=k: e.tensor_tensor(out=tt[k][:], in0=z[:, j, :], in1=mean[:], op=ALU.subtract), reads=[("z", zs, j), "mean"], writes=[("tt", k)])
                            S.op("dve", lambda e, j=j, k=k: e.tensor_tensor(out=tt[k][:], in0=tt[k][:], in1=rinv[:], op=ALU.mult), reads=[("tt", k), "rinvc"], writes=[("tt", k)])
                            if outb is not None:
                                S.op("act", lambda e, j=j, k=k: e.activation(out=outb[:, j, :], in_=tt[k][:], func=AF.Identity, bias=bt_[:, l, j:j + 1], scale=gt[:, l, j:j + 1]), reads=[("tt", k)], writes=[("ob", j)])
                            S.op("act", lambda e, j=j, k=k: e.activation(out=z[:, j, :], in_=tt[k][:], func=AF.Identity, bias=bt_[:, l, j:j + 1], scale=gt[:, l, j:j + 1]), reads=[("tt", k)], writes=[("z", zs, j)])

                    loads(0)
                    conv(0)
                    pend = [None]
                    for t in range(NT):
                        s = t % 2
                        z = xz[s]
                        for j in range(8):
                            b = bank()
                            for n_ in range(8):
                                S.op("pe", lambda e, b=b, j=j, n_=n_: e.matmul(PS[b][:], lhsT=wout[:, n_, j * 128:(j + 1) * 128], rhs=mix[s][:, n_, :], start=(n_ == 0), stop=(n_ == 7)),
                                     reads=["wout", ("mixa", s) if n_ < 4 else ("mixc", s, n_ - 4)], writes=[("ps", b)])
                            S.op("dve", lambda e, b=b, j=j: e.scalar_tensor_tensor(out=z[:, j, :], in0=z[:, j, :], scalar=ALPHA, in1=PS[b][:], op0=ALU.mult, op1=ALU.add), reads=[("ps", b), ("z", s, j)], writes=[("z", s, j)])
                            stat_acc(z, s, j)
                        if pend[0] is not None:
                            pend[0]()
                            pend[0] = None
                        if t + 1 < NT:
                            loads(t + 1)
                        bs, bq = ln_stats(z, s)
                        ln_norm(z, s, bs, bq, g1, b1, x1b)
                        if t + 1 < NT:
                            conv(t + 1)
                        for g in range(8):
                            ws = wctr[0] % 3
                            wctr[0] += 1
                            S.op("sp", lambda e, g=g, ws=ws: e.dma_start(out=wr[ws][:].rearrange("p (c f) -> p c f", c=8), in_=dap(w1b, l * D * 4096 + g * 512, [[4096, 128], [128 * 4096, 8], [1, 512]])),
                                 writes=[("wr", ws)], dma=("wr", ws), extra_deps=wc[l])
                            for fl in range(4):
                                fc = g * 4 + fl
                                b = bank()
                                for dc in range(8):
                                    S.op("pe", lambda e, b=b, ws=ws, fl=fl, dc=dc: e.matmul(PS[b][:], lhsT=wr[ws][:, dc * 512 + fl * 128:dc * 512 + (fl + 1) * 128], rhs=x1b[:, dc, :], start=(dc == 0), stop=(dc == 7)),
                                         reads=[("wr", ws), ("ob", dc)], writes=[("ps", b)])
                                k = fc % 3
                                S.op("act", lambda e, b=b, k=k: e.activation(out=rr[k][:], in_=PS[b][:], func=AF.Relu), reads=[("ps", b)], writes=[("rr", k)])
                                S.op("pool" if fc % 3 == 2 else "dve", lambda e, k=k, fc=fc: e.tensor_tensor(out=hT[:, fc, :], in0=rr[k][:], in1=rr[k][:], op=ALU.mult), reads=[("rr", k)], writes=[("hT", fc)])
                        for j in range(8):
                            ws = wctr[0] % 3
                            wctr[0] += 1
                            S.op("sp", lambda e, j=j, ws=ws: e.dma_start(out=wr[ws][:], in_=dap(w2b, (l * 8 + j) * 128 * 4096, [[4096, 128], [1, 4096]])),
                                 writes=[("wr", ws)], dma=("wr", ws), extra_deps=wc[l])
                            b = bank()
                            for fc in range(32):
                                S.op("pe", lambda e, b=b, ws=ws, fc=fc: e.matmul(PS[b][:], lhsT=wr[ws][:, fc * 128:(fc + 1) * 128], rhs=hT[:, fc, :], start=(fc == 0), stop=(fc == 31)),
                                     reads=[("wr", ws), ("hT", fc)], writes=[("ps", b)])
                            S.op("dve", lambda e, b=b, j=j: e.scalar_tensor_tensor(out=z[:, j, :], in0=z[:, j, :], scalar=ALPHA, in1=PS[b][:], op0=ALU.mult, op1=ALU.add), reads=[("ps", b), ("z", s, j)], writes=[("z", s, j)])
                            stat_acc(z, s, j)
                        bs, bq = ln_stats(z, s)
                        ln_norm(z, s, bs, bq, g2, b2, xob)
                        zr = [("z", s, j) for j in range(8)]
                        if not last:
                            S.op("sp", lambda e, t=t, s=s: e.dma_start(out=dap(xTf, t * 512, [[SD, 128], [128 * SD, 8], [1, 512]]), in_=xz[s][:]), reads=zr, dma=("xzst", s))
                            S.op("sp", lambda e, t=t: e.dma_start(out=dap(xTb, t * 512, [[SD, 128], [128 * SD, 8], [1, 512]]), in_=xob[:]), reads=[("ob", j) for j in range(8)], dma="xobst")
                        else:
                            def fin(t=t, s=s):
                                for sub in range(4):
                                    k = sub % 2
                                    for hf in range(2):
                                        b = bank()
                                        for dl in range(4):
                                            dc = hf * 4 + dl
                                            S.op("pe", lambda e, b=b, sub=sub, dc=dc, dl=dl, s=s: e.transpose(out=PS[b][:, dl * 128:(dl + 1) * 128], in_=xz[s][:, dc, sub * 128:(sub + 1) * 128], identity=ident[:]),
                                                 reads=[("z", s, dc)], writes=[("ps", b)])
                                        S.op("act", lambda e, b=b, k=k, hf=hf: e.copy(out=yst[k][:, hf * 512:(hf + 1) * 512], in_=PS[b][:]), reads=[("ps", b)], writes=[("yst", k)])
                                    S.op("sp", lambda e, t=t, sub=sub, k=k: e.dma_start(out=dap(y_out, (t * 512 + sub * 128) * D, [[D, 128], [1, D]]), in_=yst[k][:]), reads=[("yst", k)], dma=("yst", k))
                            pend[0] = fin
                    if pend[0] is not None:
                        pend[0]()
                    S.barrier()

            kstop = int(os.environ.get("KSTOP", "99"))
            for l in range(L):
                if kstop >= 2:
                    phase_a(l)
                if kstop >= 3:
                    attention(l)
                if kstop >= 4:
                    phase_c(l)

    prog(S)
    with contextlib.ExitStack() as st:
        S.start_real(st)
        prog(S)
    return nc, S


_CACHE = {}


def _consts():
    i = np.arange(1280)
    b = np_bucket(639 - i)
    oh = (b[None, :] == np.arange(32)[:, None]).astype(np.float32)
    ident = np.eye(128, dtype=np.float32)
    anti = np.ascontiguousarray(ident[::-1])
    return oh, ident, anti


def run_frames(frames, valids, weights, SD, L):
    key = (SD, L)
    if key not in _CACHE:
        _CACHE[key] = build_program(SD, L)[0]
    nc = _CACHE[key]
    oh, ident, anti = _consts()
    in_maps = []
    for x, nv in zip(frames, valids):
        tok = (np.arange(SD) < nv).astype(np.float32)
        vcol = np.ascontiguousarray(tok.reshape(SD // 128, 128).T)
        vrow = np.ascontiguousarray(np.broadcast_to(tok[None, :], (128, SD)))
        kt_valid = tok.reshape(SD // 128, 128)[:, 0]
        kmask = np.ascontiguousarray(np.broadcast_to(((1.0 - kt_valid) * NEG).astype(np.float32)[None, :], (128, SD // 128)))
        m = {"x": x, "vcol": vcol, "vrow": vrow, "kmask": kmask, "ohrev": oh, "ident": ident, "anti": anti}
        m.update(weights)
        if os.environ.get("KFAST") == "1":
            for k in ("w_in", "w_out", "w_mlp1", "w_mlp2"):
                m[k] = np.zeros((1, 8, 8), np.float32)
        in_maps.append(m)
    res = run_bass_kernel_spmd(nc, in_maps, core_ids=list(range(8)))
    return [r["y"] for r in res.results]


def kernel(x_prompt, x_sample, w_in, w_out, conv_w, conv_b, lambda_q1, lambda_k1, lambda_q2,
           lambda_k2, subln_g, rel_bias, ln1_g, ln1_b, w_mlp1, w_mlp2, ln2_g, ln2_b):
    SD = 8192
    f = lambda a: np.ascontiguousarray(np.asarray(a, dtype=np.float32))
    weights = {"w_in": f(w_in), "w_out": f(w_out), "conv_w": f(conv_w), "conv_b": f(conv_b),
               "lambda_q1": f(lambda_q1), "lambda_k1": f(lambda_k1), "lambda_q2": f(lambda_q2),
               "lambda_k2": f(lambda_k2), "subln_g": f(subln_g), "rel_bias": f(rel_bias),
               "ln1_g": f(ln1_g), "ln1_b": f(ln1_b), "w_mlp1": f(w_mlp1), "w_mlp2": f(w_mlp2),
               "ln2_g": f(ln2_g), "ln2_b": f(ln2_b)}
    xp = f(x_prompt)
    xs = f(x_sample)
    frames = [xp[b] for b in range(4)]
    valids = [SD] * 4
    for b in range(4):
        fr = np.zeros((SD, D), np.float32)
        fr[:xs.shape[1]] = xs[b]
        frames.append(fr)
        valids.append(xs.shape[1])
    ys = run_frames(frames, valids, weights, SD, 4)
    y_prompt = np.stack([ys[b] for b in range(4)], axis=0)
    y_sample = np.stack([ys[4 + b][:xs.shape[1]] for b in range(4)], axis=0)
    return (y_prompt, y_sample)
```

```python
import contextlib
import os
import numpy as np
import concourse.bass as bass
import concourse.mybir as mybir
from concourse.bass_utils import run_bass_kernel_spmd

F32 = mybir.dt.float32
BF16 = mybir.dt.bfloat16
AF = mybir.ActivationFunctionType
ALU = mybir.AluOpType


class _Dummy:
    def __getitem__(self, k):
        return self

    def __getattr__(self, k):
        return self

    def __call__(self, *a, **k):
        return self


class Sched:
    CENG = ("pe", "act", "dve", "pool")

    def __init__(self, nc):
        self.nc = nc
        self.needed = None
        self.reset(dry=True)

    def reset(self, dry):
        self.dry = dry
        self.n = 0
        self.eng_of = []
        self.dma_of = []
        self.lastw = {}
        self.readers = {}
        self.need_now = set()
        self.sigcount = {e: 0 for e in self.CENG}
        self.sig_of = {}
        self.dma_cnt = {}
        self.seen = {e: {} for e in self.CENG + ("sp",)}
        self.last_on = {}
        self.dma_out = []
        self.nwaits = 0
        self.ninstr = 0

    def start_real(self, stack):
        self.needed = self.need_now
        self.reset(dry=False)
        self.stack = stack
        self.esem = {e: stack.enter_context(self.nc.semaphore("es_" + e)) for e in self.CENG}
        self.dsem = {}

    def eng(self, e):
        nc = self.nc
        return {"pe": nc.tensor, "act": nc.scalar, "dve": nc.vector, "pool": nc.gpsimd, "sp": nc.sync}[e]

    def _dsem(self, key):
        if key not in self.dsem:
            self.dsem[key] = self.stack.enter_context(self.nc.semaphore("ds_%d" % len(self.dsem)))
        return self.dsem[key]

    def op(self, eng, fn, reads=(), writes=(), dma=None, extra_deps=()):
        i = self.n
        self.n += 1
        hard = set(extra_deps)
        war = set()
        for k in reads:
            w = self.lastw.get(k)
            if w is not None:
                hard.add(w)
        for k in writes:
            w = self.lastw.get(k)
            if w is not None:
                hard.add(w)
            rd = self.readers.get(k)
            if rd:
                war.update(rd.values())
        for k in writes:
            self.lastw[k] = i
            self.readers[k] = {}
        for k in reads:
            self.readers.setdefault(k, {})[("d", i) if dma else eng] = i
        war -= hard
        self.eng_of.append(eng)
        if dma is not None:
            c = self.dma_cnt.get(dma, 0) + 16
            self.dma_cnt[dma] = c
            self.dma_of.append((dma, c))
            self.dma_out.append(i)
        else:
            self.dma_of.append(None)
        waits = []
        for d, is_war in [(d, False) for d in hard] + [(d, True) for d in war]:
            dd = self.dma_of[d]
            if dd is not None:
                waits.append(("d", dd[0], dd[1]))
                continue
            de = self.eng_of[d]
            if de == eng and dma is None:
                if eng == "pe" or is_war:
                    continue
            self.need_now.add(d)
            if not self.dry:
                waits.append(("e", de, self.sig_of[d]))
        if dma is None and fn is not None:
            if not self.dry and i in self.needed:
                self.sigcount[eng] += 1
                self.sig_of[i] = self.sigcount[eng]
            self.last_on[eng] = i
        if self.dry:
            return i
        E = self.eng(eng)
        seen = self.seen[eng]
        for kind, a, v in waits:
            key = (kind, a)
            if seen.get(key, 0) >= v:
                continue
            seen[key] = v
            sem = self._dsem(a) if kind == "d" else self.esem[a]
            E.wait_ge(sem, v)
            self.nwaits += 1
        if fn is not None:
            ins = fn(E)
            self.ninstr += 1
            if dma is not None:
                ins.then_inc(self._dsem(dma), 16)
            elif i in self.needed:
                ins.then_inc(self.esem[eng], 1)
        return i

    def barrier(self, engines=None):
        deps = list(self.last_on.values()) + list(self.dma_out)
        for e in (engines or (self.CENG + ("sp",))):
            self.op(e, None, extra_deps=deps)
        if engines is None:
            self.dma_out = []
            self.lastw = {}
            self.readers = {}

    @contextlib.contextmanager
    def scope(self):
        sc = _Scope(self)
        try:
            yield sc
        finally:
            sc.close()


class _Scope:
    def __init__(self, S):
        self.S = S
        self.stack = None if S.dry else contextlib.ExitStack()

    def sbuf(self, name, shape, dtype):
        if self.S.dry:
            return _Dummy()
        self.S.uid = getattr(self.S, "uid", 0) + 1
        return self.stack.enter_context(self.S.nc.sbuf_tensor("sb%d_%s" % (self.S.uid, name), list(shape), dtype))

    def psum(self, name, shape, dtype):
        if self.S.dry:
            return _Dummy()
        self.S.uid = getattr(self.S, "uid", 0) + 1
        return self.stack.enter_context(self.S.nc.psum_tensor("pp%d_%s" % (self.S.uid, name), list(shape), dtype))

    def close(self):
        if self.stack is not None:
            self.stack.close()


import math

D = 1024
LN_EPS = 1e-5
ALPHA = 8.0 ** 0.25
NEG = -30000.0


def lam_init(l):
    return 0.8 - 0.6 * math.exp(-0.3 * l)


def np_bucket(rel):
    half = 16
    me = 8
    ret = np.where(rel > 0, half, 0)
    n = np.abs(rel)
    nf = np.maximum(n, 1).astype(np.float32)
    large = me + (np.log(nf / np.float32(me)) / np.float32(math.log(128 / 8)) * np.float32(8)).astype(np.int32)
    large = np.minimum(large, half - 1)
    return ret + np.where(n < me, n, large)


def build_program(SD, L):
    nc = bass.Bass("TRN2", target_bir_lowering=False)
    NT = SD // 512
    NKT = SD // 128
    AX = mybir.AxisListType.X

    def din(name, shape):
        return nc.dram_tensor(name, list(shape), F32, kind="ExternalInput").ap()

    def dscr(name, shape, dt):
        return nc.dram_tensor(name, list(shape), dt, kind="Internal").ap()

    def dap(t, off, dims):
        return bass.AP(t.tensor, off, [list(d) for d in dims])

    x_in = din("x", [SD, D])
    vcol_in = din("vcol", [128, NKT])
    vrow_in = din("vrow", [128, SD])
    kmask_in = din("kmask", [128, NKT])
    oh_in = din("ohrev", [32, 1280])
    ident_in = din("ident", [128, 128])
    anti_in = din("anti", [128, 128])
    KFAST = os.environ.get("KFAST") == "1"
    KSET = int(os.environ.get("KSET", "99"))
    w_in = din("w_in", [4, D, 3072] if not KFAST else [1, 8, 8])
    w_out = din("w_out", [4, D, D] if not KFAST else [1, 8, 8])
    conv_w = din("conv_w", [4, 3, 512])
    conv_b = din("conv_b", [4, 512])
    lq1 = din("lambda_q1", [4, 64]); lk1 = din("lambda_k1", [4, 64])
    lq2 = din("lambda_q2", [4, 64]); lk2 = din("lambda_k2", [4, 64])
    subln_g = din("subln_g", [4, 128])
    rel_bias = din("rel_bias", [32, 4])
    ln1_g = din("ln1_g", [4, D]); ln1_b = din("ln1_b", [4, D])
    w_mlp1 = din("w_mlp1", [4, D, 4096] if not KFAST else [1, 8, 8]); w_mlp2 = din("w_mlp2", [4, 4096, D] if not KFAST else [1, 8, 8])
    ln2_g = din("ln2_g", [4, D]); ln2_b = din("ln2_b", [4, D])
    y_out = nc.dram_tensor("y", [SD, D], F32, kind="ExternalOutput").ap()

    winb = dscr("winb", [L, D, 3072], BF16)
    woutb = dscr("woutb", [L, D, D], BF16)
    w1b = dscr("w1b", [L, D, 4096], BF16)
    w2b = dscr("w2b", [L * 8 * 128, 4096], BF16)
    xTf = dscr("xTf", [D, SD], F32)
    xTb = dscr("xTb", [D, SD], BF16)
    QT = dscr("QT", [512, SD], BF16)
    KT = dscr("KT", [512, SD], BF16)
    Vs = dscr("Vs", [SD, 512], BF16)
    Us = dscr("Us", [512, SD + 2], F32)
    GBs = dscr("GBs", [512, SD], F32)
    ATs = dscr("ATs", [512, SD], BF16)
    Tscr = dscr("Tscr", [4, 1280], F32)
    Gscr = dscr("Gscr", [128, 4 * 1152], F32)

    S = Sched(nc)
    bankctr = [0]

    def prog(S):
        bankctr[0] = 0
        with S.scope() as pc:
            ident = pc.sbuf("ident", [128, 128], F32)
            anti = pc.sbuf("anti", [128, 128], F32)
            ones_f = pc.sbuf("ones_f", [128, 128], F32)
            ones_b = pc.sbuf("ones_b", [128, 128], BF16)
            vcol = pc.sbuf("vcol", [128, NKT], F32)
            kmask = pc.sbuf("kmask", [128, NKT], F32)
            cb = pc.sbuf("cb", [128, 4, 3, NKT], F32)
            neglam = pc.sbuf("neglam", [128, 4], F32)
            gsc = pc.sbuf("gsc", [128, 4], F32)
            g1 = pc.sbuf("g1", [128, 4, 8], F32); b1 = pc.sbuf("b1", [128, 4, 8], F32)
            g2 = pc.sbuf("g2", [128, 4, 8], F32); b2 = pc.sbuf("b2", [128, 4, 8], F32)
            cw = pc.sbuf("cw", [128, 4, 3, 4], F32)
            cbs = pc.sbuf("cbs", [128, 4, 4], F32)
            PSW = [pc.psum("psw%d" % i, [128, 1024], F32) for i in range(4)]
            PS = [PSW[i // 2][:, (i % 2) * 512:(i % 2 + 1) * 512] for i in range(8)]
            epsb = pc.sbuf("epsb", [128, 1], F32)

            def bank():
                i = bankctr[0] % 8
                bankctr[0] += 1
                return i

            wc = {}
            prevc = []

            def cdma(fn, wkey, l):
                i = S.op("pool", fn, writes=[wkey], dma=("wc", l), extra_deps=list(prevc))
                prevc[:] = [i]
                S.dma_out.remove(i)
                return i
            for l in range(L if not KFAST else 0):
                ops = []
                for q in range(4):
                    ops.append(cdma(lambda e, l=l, q=q: e.dma_start(out=dap(winb, l * D * 3072 + q * 768 * 1024, [[1024, 768], [1, 1024]]),
                                                                     in_=dap(w_in, l * D * 3072 + q * 768 * 1024, [[1024, 768], [1, 1024]])),
                                    ("winb", l, q), l))
                ops.append(cdma(lambda e, l=l: e.dma_start(out=dap(woutb, l * D * D, [[1024, 1024], [1, 1024]]),
                                                            in_=dap(w_out, l * D * D, [[1024, 1024], [1, 1024]])),
                                ("woutb", l), l))
                for hf in range(4):
                    ops.append(cdma(lambda e, l=l, hf=hf: e.dma_start(
                        out=dap(w1b, l * D * 4096 + hf * 1024 * 1024, [[1024, 1024], [1, 1024]]),
                        in_=dap(w_mlp1, l * D * 4096 + hf * 1024 * 1024, [[1024, 1024], [1, 1024]])),
                        ("w1b", l, hf), l))
                for j in range(8):
                    ops.append(cdma(lambda e, l=l, j=j: e.dma_start(
                        out=dap(w2b, (l * 8 + j) * 128 * 4096, [[4096, 128], [128, 32], [1, 128]]),
                        in_=dap(w_mlp2, l * 4096 * D + j * 128, [[1024, 128], [128 * 1024, 32], [1, 128]])),
                        ("w2b", l, j), l))
                wc[l] = ops

            with S.scope() as sc:
                lam4 = sc.sbuf("lam4", [128, 4, 4, 64], F32)
                prod = sc.sbuf("prod", [128, 2, 4, 64], F32)
                ss = sc.sbuf("ss", [128, 2, 4], F32)
                ee = sc.sbuf("ee", [128, 2, 4], F32)
                lam = sc.sbuf("lam", [128, 4], F32)
                cl = sc.sbuf("cl", [128, 4], F32); cr = sc.sbuf("cr", [128, 4], F32)
                rb = sc.sbuf("rb", [32, 4], F32)
                oh = sc.sbuf("oh", [32, 1280], F32)
                tsb = sc.sbuf("tsb", [4, 1280], F32)
                hk = sc.sbuf("hk", [128, 4, 1152], F32)
                G = sc.sbuf("Gs", [128, 4, 1152], F32)
                zt = sc.sbuf("zt", [128, 4, 1], F32)

                def ld(out_fn, in_fn, nonc=False):
                    if nonc:
                        S.op("sp", lambda e: e.dma_start(out=out_fn(), in_=in_fn(), allow_slow_non_contiguous=True), dma="setup")
                    else:
                        S.op("sp", lambda e: e.dma_start(out=out_fn(), in_=in_fn()), dma="setup")
                ld(lambda: ident[:], lambda: ident_in)
                ld(lambda: anti[:], lambda: anti_in)
                ld(lambda: vcol[:], lambda: vcol_in)
                ld(lambda: kmask[:], lambda: kmask_in)
                ld(lambda: oh[:], lambda: oh_in)
                ld(lambda: rb[:], lambda: rel_bias)
                for i, t in enumerate((lq1, lk1, lq2, lk2)):
                    ld(lambda i=i: lam4[:, i, :, :], lambda t=t: dap(t, 0, [[0, 128], [64, 4], [1, 64]]))
                ld(lambda: cl[:], lambda: dap(rel_bias, 15 * 4, [[0, 128], [1, 4]]))
                ld(lambda: cr[:], lambda: dap(rel_bias, 31 * 4, [[0, 128], [1, 4]]))
                ld(lambda: gsc[:], lambda: dap(subln_g, 0, [[1, 128], [128, 4]]), nonc=True)
                for dst, src in ((g1, ln1_g), (b1, ln1_b), (g2, ln2_g), (b2, ln2_b)):
                    for l in range(4):
                        ld(lambda dst=dst, l=l: dst[:, l, :], lambda src=src, l=l: dap(src, l * D, [[1, 128], [128, 8]]), nonc=True)
                for l in range(4):
                    for k in range(3):
                        ld(lambda l=l, k=k: cw[:, l, k, :], lambda l=l, k=k: dap(conv_w, (l * 3 + k) * 512, [[1, 128], [128, 4]]), nonc=True)
                    ld(lambda l=l: cbs[:, l, :], lambda l=l: dap(conv_b, l * 512, [[1, 128], [128, 4]]), nonc=True)
                S.op("dve", lambda e: e.memset(ones_f[:], 1.0))
                S.op("dve", lambda e: e.memset(epsb[:], LN_EPS))
                S.op("dve", lambda e: e.memset(ones_b[:], 1.0))
                S.op("dve", lambda e: e.memset(zt[:], 0.0))
                S.barrier()
                if KSET <= 1:
                    return
                S.op("sp", lambda e: e.dma_start(out=dap(Us, 0, [[SD + 2, 128], [128 * (SD + 2), 4], [1, 1]]), in_=zt[:], allow_slow_non_contiguous=True), dma="setup")
                S.op("sp", lambda e: e.dma_start(out=dap(Us, SD + 1, [[SD + 2, 128], [128 * (SD + 2), 4], [1, 1]]), in_=zt[:], allow_slow_non_contiguous=True), dma="setup")
                S.op("dve", lambda e: e.tensor_tensor(out=prod[:, 0, :, :], in0=lam4[:, 0, :, :], in1=lam4[:, 1, :, :], op=ALU.mult))
                S.op("dve", lambda e: e.tensor_tensor(out=prod[:, 1, :, :], in0=lam4[:, 2, :, :], in1=lam4[:, 3, :, :], op=ALU.mult))
                S.barrier()
                if KSET <= 2:
                    return
                S.op("dve", lambda e: e.reduce_sum(out=ss[:], in_=prod[:], axis=AX))
                S.barrier()
                if KSET <= 3:
                    return
                S.op("act", lambda e: e.activation(out=ee[:], in_=ss[:], func=AF.Exp))
                S.barrier()
                if KSET <= 4:
                    return
                S.op("dve", lambda e: e.tensor_tensor(out=lam[:], in0=ee[:, 0, :], in1=ee[:, 1, :], op=ALU.subtract))
                S.barrier()
                if KSET <= 5:
                    return
                for l in range(4):
                    S.op("dve", lambda e, l=l: e.tensor_scalar(out=neglam[:, l:l + 1], in0=lam[:, l:l + 1], scalar1=lam_init(l), scalar2=-1.0, op0=ALU.add, op1=ALU.mult))
                    S.op("dve", lambda e, l=l: e.tensor_scalar(out=gsc[:, l:l + 1], in0=gsc[:, l:l + 1], scalar1=1.0 - lam_init(l), scalar2=None, op0=ALU.mult))
                for h in range(4):
                    S.op("dve", lambda e, h=h: e.tensor_scalar(out=cb[:, h, 0, :], in0=kmask[:], scalar1=cl[:, h:h + 1], scalar2=None, op0=ALU.add))
                    S.op("dve", lambda e, h=h: e.tensor_copy(out=cb[:, h, 1, :], in_=kmask[:]))
                    S.op("dve", lambda e, h=h: e.tensor_scalar(out=cb[:, h, 2, :], in0=kmask[:], scalar1=cr[:, h:h + 1], scalar2=None, op0=ALU.add))
                for c, w in ((0, 512), (1, 512), (2, 256)):
                    S.op("pe", lambda e, c=c, w=w: e.matmul(PS[c][0:4, 0:w], lhsT=rb[:], rhs=oh[:, c * 512:c * 512 + w], start=True, stop=True))
                S.barrier()
                if KSET <= 6:
                    return
                for c, w in ((0, 512), (1, 512), (2, 256)):
                    S.op("act", lambda e, c=c, w=w: e.copy(out=tsb[:, c * 512:c * 512 + w], in_=PS[c][0:4, 0:w]))
                S.barrier()
                if KSET <= 7:
                    return
                S.op("sp", lambda e: e.dma_start(out=Tscr, in_=tsb[:]), dma="setup")
                S.barrier()
                if KSET <= 8:
                    return
                for h in range(4):
                    S.op("sp", lambda e, h=h: e.dma_start(out=hk[:, h, :], in_=dap(Tscr, h * 1280, [[1, 128], [1, 1152]])), dma="setup")
                S.barrier()
                if KSET <= 9:
                    return
                for h in range(4):
                    for c, w in ((0, 512), (1, 512), (2, 128)):
                        S.op("pe", lambda e, h=h, c=c, w=w: e.matmul(PS[3 + c][:, 0:w], lhsT=anti[:], rhs=hk[:, h, c * 512:c * 512 + w], start=True, stop=True))
                    S.barrier()
                    if KSET <= 10:
                        return
                    for c, w in ((0, 512), (1, 512), (2, 128)):
                        S.op("act", lambda e, h=h, c=c, w=w: e.copy(out=G[:, h, c * 512:c * 512 + w], in_=PS[3 + c][:, 0:w]))
                    S.barrier()
                    if KSET <= 11:
                        return
                S.barrier()
                if KSET <= 12:
                    return
                S.op("sp", lambda e: e.dma_start(out=Gscr, in_=G[:].rearrange("p h m -> p (h m)")), dma="setup")
                S.barrier()
                if KSET <= 13:
                    return

            with S.scope() as sc:
                xin = [sc.sbuf("xin%d" % i, [128, 4, D], F32) for i in range(2)]
                xfs = [sc.sbuf("xfs%d" % i, [128, 8, 512], F32) for i in range(2)]
                xbs = [sc.sbuf("xbs%d" % i, [128, 8, 512], BF16) for i in range(2)]
                XV = int(os.environ.get("XV", "0"))
                for t in range(NT):
                    s = t % 2
                    S.op("sp", lambda e, t=t, s=s: e.dma_start(out=xin[s][:], in_=dap(x_in, t * 512 * D, [[D, 128], [128 * D, 4], [1, D]])),
                         writes=[("xin", s)], dma=("xin", s))
                    if XV == 1:
                        continue
                    for dc in range(8):
                        b = bank()
                        for sub in range(4):
                            if XV == 2 and sub > 0:
                                continue
                            S.op("pe", lambda e, b=b, s=s, sub=sub, dc=dc: e.transpose(out=PS[b][:, sub * 128:(sub + 1) * 128], in_=xin[s][:, sub, dc * 128:(dc + 1) * 128], identity=ident[:]),
                                 reads=[("xin", s)], writes=[("ps", b)])
                        if XV == 4:
                            continue
                        S.op("act", lambda e, b=b, s=s, dc=dc: e.copy(out=xfs[s][:, dc, :], in_=PS[b][:]), reads=[("ps", b)], writes=[("xfs", s)])
                        if XV == 5:
                            continue
                        S.op("dve", lambda e, b=b, s=s, dc=dc: e.tensor_copy(out=xbs[s][:, dc, :], in_=xfs[s][:, dc, :]), reads=[("xfs", s)], writes=[("xbs", s)])
                    if XV >= 3:
                        continue
                    S.op("sp", lambda e, t=t, s=s: e.dma_start(out=dap(xTf, t * 512, [[SD, 128], [128 * SD, 8], [1, 512]]), in_=xfs[s][:]),
                         reads=[("xfs", s)], dma=("xst", s))
                    S.op("sp", lambda e, t=t, s=s: e.dma_start(out=dap(xTb, t * 512, [[SD, 128], [128 * SD, 8], [1, 512]]), in_=xbs[s][:]),
                         reads=[("xbs", s)], dma=("xst", s))
                S.barrier()

            def phase_a(l):
                with S.scope() as sc:
                    win = sc.sbuf("win", [128, 8, 3072], BF16)
                    xb = [sc.sbuf("xb%d" % i, [128, 8, 512], BF16) for i in range(2)]
                    vr = [sc.sbuf("vr%d" % i, [128, 512], F32) for i in range(2)]
                    qkst = [sc.sbuf("qkst%d" % i, [128, 8, 512], BF16) for i in range(2)]
                    vst = [sc.sbuf("vst%d" % i, [128, 4, 512], BF16) for i in range(2)]
                    hcs = [sc.sbuf("hcs%d" % i, [128, 512], F32) for i in range(2)]
                    ust = [sc.sbuf("ust%d" % i, [128, 4, 512], F32) for i in range(2)]
                    gbst = [sc.sbuf("gbst%d" % i, [128, 4, 512], F32) for i in range(2)]
                    for hf in range(2):
                        S.op("sp", lambda e, hf=hf: e.dma_start(out=win[:, hf * 4:(hf + 1) * 4, :], in_=dap(winb, l * D * 3072 + hf * 4 * 128 * 3072, [[3072, 128], [128 * 3072, 4], [1, 3072]])),
                             writes=[("win", hf)], dma=("win", hf), extra_deps=wc[l][0:4])

                    def loads(t):
                        s = t % 2
                        S.op("sp", lambda e: e.dma_start(out=xb[s][:], in_=dap(xTb, t * 512, [[SD, 128], [128 * SD, 8], [1, 512]])),
                             writes=[("xb", s)], dma=("xb", s))
                        S.op("sp", lambda e: e.dma_start(out=vr[s][:], in_=dap(vrow_in, t * 512, [[SD, 128], [1, 512]])),
                             writes=[("vr", s)], dma=("vr", s))
                    loads(0)
                    hk_ = 0
                    for t in range(NT):
                        s = t % 2
                        if t + 1 < NT:
                            loads(t + 1)
                        for c in range(8):
                            b = bank()
                            for dc in range(8):
                                S.op("pe", lambda e, b=b, c=c, dc=dc: e.matmul(PS[b][:], lhsT=win[:, dc, c * 128:(c + 1) * 128], rhs=xb[s][:, dc, :], start=(dc == 0), stop=(dc == 7)),
                                     reads=[("win", dc // 4), ("xb", s)], writes=[("ps", b)])
                            if c % 2 == 0:
                                S.op("act", lambda e, b=b, c=c: e.copy(out=qkst[s][:, c, :], in_=PS[b][:]), reads=[("ps", b)], writes=[("qkst", s)])
                            else:
                                S.op("dve", lambda e, b=b, c=c: e.tensor_copy(out=qkst[s][:, c, :], in_=PS[b][:]), reads=[("ps", b)], writes=[("qkst", s)])
                        S.op("sp", lambda e: e.dma_start(out=dap(QT, t * 512, [[SD, 128], [128 * SD, 4], [1, 512]]), in_=qkst[s][:, 0:4, :]), reads=[("qkst", s)], dma=("qkst", s))
                        S.op("sp", lambda e: e.dma_start(out=dap(KT, t * 512, [[SD, 128], [128 * SD, 4], [1, 512]]), in_=qkst[s][:, 4:8, :]), reads=[("qkst", s)], dma=("qkst", s))
                        for sub in range(4):
                            b = bank()
                            for dc in range(8):
                                S.op("pe", lambda e, b=b, sub=sub, dc=dc: e.matmul(PS[b][:], lhsT=xb[s][:, dc, sub * 128:(sub + 1) * 128], rhs=win[:, dc, 1024:1536], start=(dc == 0), stop=(dc == 7)),
                                     reads=[("win", dc // 4), ("xb", s)], writes=[("ps", b)])
                            S.op("dve", lambda e, b=b, sub=sub: e.tensor_scalar(out=vst[s][:, sub, :], in0=PS[b][:], scalar1=vcol[:, t * 4 + sub:t * 4 + sub + 1], scalar2=None, op0=ALU.mult),
                                 reads=[("ps", b)], writes=[("vst", s)])
                        S.op("sp", lambda e: e.dma_start(out=dap(Vs, t * 512 * 512, [[512, 128], [128 * 512, 4], [1, 512]]), in_=vst[s][:]), reads=[("vst", s)], dma=("vst", s))
                        for cc in range(4):
                            bh = bank()
                            for dc in range(8):
                                S.op("pe", lambda e, b=bh, cc=cc, dc=dc: e.matmul(PS[b][:], lhsT=win[:, dc, 2560 + cc * 128:2560 + (cc + 1) * 128], rhs=xb[s][:, dc, :], start=(dc == 0), stop=(dc == 7)),
                                     reads=[("win", dc // 4), ("xb", s)], writes=[("ps", bh)])
                            k = hk_ % 2
                            hk_ += 1
                            S.op("dve", lambda e, b=bh, k=k: e.tensor_tensor(out=hcs[k][:], in0=PS[b][:], in1=vr[s][:], op=ALU.mult),
                                 reads=[("ps", bh), ("vr", s)], writes=[("hcs", k)])
                            bg = bank()
                            for dc in range(8):
                                S.op("pe", lambda e, b=bg, cc=cc, dc=dc: e.matmul(PS[b][:], lhsT=win[:, dc, 2048 + cc * 128:2048 + (cc + 1) * 128], rhs=xb[s][:, dc, :], start=(dc == 0), stop=(dc == 7)),
                                     reads=[("win", dc // 4), ("xb", s)], writes=[("ps", bg)])
                            S.op("dve", lambda e, b=bg, k=k, cc=cc: e.tensor_tensor(out=ust[s][:, cc, :], in0=PS[b][:], in1=hcs[k][:], op=ALU.mult),
                                 reads=[("ps", bg), ("hcs", k)], writes=[("ust", s)])
                            bb = bank()
                            for dc in range(8):
                                S.op("pe", lambda e, b=bb, cc=cc, dc=dc: e.matmul(PS[b][:], lhsT=win[:, dc, 1536 + cc * 128:1536 + (cc + 1) * 128], rhs=xb[s][:, dc, :], start=(dc == 0), stop=(dc == 7)),
                                     reads=[("win", dc // 4), ("xb", s)], writes=[("ps", bb)])
                            S.op("act", lambda e, b=bb, cc=cc: e.copy(out=gbst[s][:, cc, :], in_=PS[b][:]), reads=[("ps", bb)], writes=[("gbst", s)])
                        S.op("sp", lambda e: e.dma_start(out=dap(Us, 1 + t * 512, [[SD + 2, 128], [128 * (SD + 2), 4], [1, 512]]), in_=ust[s][:]), reads=[("ust", s)], dma=("ust", s))
                        S.op("sp", lambda e: e.dma_start(out=dap(GBs, t * 512, [[SD, 128], [128 * SD, 4], [1, 512]]), in_=gbst[s][:]), reads=[("gbst", s)], dma=("gbst", s))
                    S.barrier()

            def attention(l):
                with S.scope() as sc:
                    qh = [sc.sbuf("qh%d" % i, [128, SD], BF16) for i in range(2)]
                    kh = [sc.sbuf("kh%d" % i, [128, SD], BF16) for i in range(2)]
                    vh = [sc.sbuf("vh%d" % i, [128, NKT, 128], BF16) for i in range(2)]
                    G = sc.sbuf("G", [128, 4, 1152], F32)
                    NE = 4
                    E = [sc.sbuf("E%d" % i, [128, 1024], BF16) for i in range(NE)]
                    bt = [sc.sbuf("bt%d" % i, [128, 1024], F32) for i in range(2)]
                    acc = [[sc.sbuf("acc%d_%d" % (g_, i), [128, 512], F32) for i in range(2)] for g_ in range(2)]
                    lnb = sc.sbuf("lnb", [128, 1024], F32)
                    rcp = sc.sbuf("rcp", [128, 1024], F32)
                    t1 = sc.sbuf("t1", [128, 512], F32); t2 = sc.sbuf("t2", [128, 512], F32)
                    av = sc.sbuf("av", [128, 512], F32); sq = sc.sbuf("sq", [128, 512], F32)
                    lr = sc.sbuf("lr", [128, 512], F32); rinv = sc.sbuf("rinv", [128, 512], F32)
                    aost = [sc.sbuf("aost%d" % i, [128, 512], BF16) for i in range(2)]
                    S.op("sp", lambda e: e.dma_start(out=G[:].rearrange("p h m -> p (h m)"), in_=Gscr), writes=["G"], dma="gld")

                    def hload(h):
                        s = h % 2
                        S.op("sp", lambda e: e.dma_start(out=qh[s][:], in_=dap(QT, h * 128 * SD, [[SD, 128], [1, SD]])), writes=[("qh", s)], dma=("qh", s))
                        S.op("sp", lambda e: e.dma_start(out=kh[s][:], in_=dap(KT, h * 128 * SD, [[SD, 128], [1, SD]])), writes=[("kh", s)], dma=("kh", s))
                        S.op("sp", lambda e: e.dma_start(out=vh[s][:], in_=dap(Vs, h * 128, [[512, 128], [128 * 512, NKT], [1, 128]])), writes=[("vh", s)], dma=("vh", s))
                    hload(0)
                    units = [(h, qt, kt) for h in range(4) for qt in range(NT) for kt in range(NKT)]
                    NU = len(units)

                    def scores(u):
                        h, qt, kt = units[u]
                        s = h % 2
                        p = u % 2
                        if qt == 0 and kt == 2 and h + 1 < 4:
                            hload(h + 1)
                        S.op("pe", lambda e: e.matmul(PSW[p][:, 0:512], lhsT=kh[s][0:64, kt * 128:(kt + 1) * 128], rhs=qh[s][0:64, qt * 512:(qt + 1) * 512], start=True, stop=True, tile_position=(0, 0)),
                             reads=[("qh", s), ("kh", s)], writes=[("psw", p)])
                        S.op("pe", lambda e: e.matmul(PSW[p][:, 512:1024], lhsT=kh[s][64:128, kt * 128:(kt + 1) * 128], rhs=qh[s][64:128, qt * 512:(qt + 1) * 512], start=True, stop=True, tile_position=(64, 0)),
                             reads=[("qh", s), ("kh", s)], writes=[("psw", p)])
                        j = kt - 4 * qt
                        if -1 <= j <= 4:
                            off = 512 - 128 * j
                            for m in range(2):
                                S.op("dve", lambda e, m=m: e.scalar_tensor_tensor(out=bt[p][:, m * 512:(m + 1) * 512], in0=PSW[p][:, m * 512:(m + 1) * 512], scalar=0.125, in1=G[:, h, off:off + 512], op0=ALU.mult, op1=ALU.add),
                                     reads=[("psw", p), "G"], writes=[("bt", p, m)])

                    def unit(u):
                        h, qt, kt = units[u]
                        s = h % 2
                        p = u % 2
                        r = u % NE
                        gi = (h * NT + qt) % 2
                        A = acc[gi]
                        j = kt - 4 * qt
                        if -1 <= j <= 4:
                            S.op("act", lambda e: e.activation(out=E[r][:], in_=bt[p][:], func=AF.Exp, bias=cb[:, h, 1, kt:kt + 1], scale=1.0),
                                 reads=[("bt", p, 0), ("bt", p, 1)], writes=[("E", r)])
                        else:
                            side = 0 if j < -1 else 2
                            S.op("act", lambda e: e.activation(out=E[r][:], in_=PSW[p][:], func=AF.Exp, bias=cb[:, h, side, kt:kt + 1], scale=0.125),
                                 reads=[("psw", p)], writes=[("E", r)])

                    def unit_b(u):
                        h, qt, kt = units[u]
                        s = h % 2
                        r = u % NE
                        gi = (h * NT + qt) % 2
                        A = acc[gi]
                        st = (kt == 0)
                        sp_ = (kt == NKT - 1)
                        S.op("pe", lambda e: e.matmul(PS[4][:], lhsT=vh[s][:, kt, :], rhs=E[r][:, 0:512], start=st, stop=sp_), reads=[("vh", s), ("E", r)], writes=[("ps", 4)])
                        S.op("pe", lambda e: e.matmul(PS[5][:], lhsT=vh[s][:, kt, :], rhs=E[r][:, 512:1024], start=st, stop=sp_), reads=[("vh", s), ("E", r)], writes=[("ps", 5)])
                        S.op("pe", lambda e: e.matmul(PS[6][:], lhsT=ones_b[:], rhs=E[r][:, 512:1024], start=st, stop=sp_), reads=[("E", r)], writes=[("ps", 6)])
                        a1 = kt % 2
                        if kt < 2:
                            S.op("dve", lambda e: e.tensor_copy(out=A[a1][:], in_=E[r][:, 0:512]), reads=[("E", r)], writes=[("acc", gi, a1)])
                        else:
                            S.op("dve", lambda e: e.tensor_tensor(out=A[a1][:], in0=A[a1][:], in1=E[r][:, 0:512], op=ALU.add), reads=[("E", r), ("acc", gi, a1)], writes=[("acc", gi, a1)])
                        if kt == NKT - 1:
                            epilogue(h, qt, gi)

                    epi = [0]

                    def epilogue(h, qt, gi):
                        k = epi[0] % 2
                        epi[0] += 1
                        A = acc[gi]
                        S.op("pe", lambda e: e.matmul(PS[7][:], lhsT=ones_f[:], rhs=A[0][:], start=True, stop=False), reads=[("acc", gi, 0)], writes=[("ps", 7)])
                        S.op("pe", lambda e: e.matmul(PS[7][:], lhsT=ones_f[:], rhs=A[1][:], start=False, stop=True), reads=[("acc", gi, 1)], writes=[("ps", 7)])
                        S.op("act", lambda e: e.activation(out=lnb[:], in_=PSW[3][:], func=AF.Ln), reads=[("ps", 6), ("ps", 7)], writes=["lnb"])
                        S.op("act", lambda e: e.activation(out=rcp[:], in_=lnb[:], func=AF.Exp, scale=-1.0), reads=["lnb"], writes=["rcp"])
                        S.op("dve", lambda e: e.tensor_tensor(out=t1[:], in0=PS[4][:], in1=rcp[:, 512:1024], op=ALU.mult), reads=[("ps", 4), "rcp"], writes=["t1"])
                        S.op("dve", lambda e: e.tensor_tensor(out=t2[:], in0=PS[5][:], in1=rcp[:, 0:512], op=ALU.mult), reads=[("ps", 5), "rcp"], writes=["t2"])
                        S.op("dve", lambda e: e.scalar_tensor_tensor(out=av[:], in0=t2[:], scalar=neglam[:, l:l + 1], in1=t1[:], op0=ALU.mult, op1=ALU.add), reads=["t1", "t2"], writes=["av"])
                        S.op("dve", lambda e: e.tensor_tensor(out=sq[:], in0=av[:], in1=av[:], op=ALU.mult), reads=["av"], writes=["sq"])
                        S.op("pe", lambda e: e.matmul(PS[7][:], lhsT=ones_f[:], rhs=sq[:], start=True, stop=True), reads=["sq"], writes=[("ps", 7)])
                        S.op("act", lambda e: e.activation(out=lr[:], in_=PS[7][:], func=AF.Ln, bias=epsb[:, 0:1], scale=1.0 / 128.0), reads=[("ps", 7)], writes=["lr"])
                        S.op("act", lambda e: e.activation(out=rinv[:], in_=lr[:], func=AF.Exp, scale=-0.5), reads=["lr"], writes=["rinv"])
                        S.op("dve", lambda e: e.scalar_tensor_tensor(out=aost[k][:], in0=av[:], scalar=gsc[:, l:l + 1], in1=rinv[:], op0=ALU.mult, op1=ALU.mult), reads=["av", "rinv"], writes=[("aost", k)])
                        S.op("sp", lambda e: e.dma_start(out=dap(ATs, h * 128 * SD + qt * 512, [[SD, 128], [1, 512]]), in_=aost[k][:]), reads=[("aost", k)], dma=("aost", k))

                    scores(0)
                    if NU > 1:
                        scores(1)
                    for u in range(NU):
                        unit(u)
                        if u + 2 < NU:
                            scores(u + 2)
                        unit_b(u)
                    S.barrier()

            def phase_c(l):
                last = (l == L - 1)
                with S.scope() as sc:
                    wout = sc.sbuf("wout", [128, 8, D], BF16)
                    wr = [sc.sbuf("wr%d" % i, [128, 4096], BF16) for i in range(3)]
                    xz = [sc.sbuf("xz%d" % i, [128, 8, 512], F32) for i in range(2)]
                    mix = [sc.sbuf("mix%d" % i, [128, 8, 512], BF16) for i in range(2)]
                    ub = [sc.sbuf("ub", [128, 4, 514], F32)] * 2
                    gbb = [sc.sbuf("gbb", [128, 4, 512], F32)] * 2
                    x1b = sc.sbuf("x1b", [128, 8, 512], BF16)
                    hT = sc.sbuf("hT", [128, 32, 512], BF16)
                    xob = None if last else sc.sbuf("xob", [128, 8, 512], BF16)
                    sqb = [sc.sbuf("sqb%d" % i, [128, 512], F32) for i in range(2)]
                    sacc = [sc.sbuf("sacc%d" % i, [128, 512], F32) for i in range(4)]
                    mean = sc.sbuf("mean", [128, 512], F32); msq = sc.sbuf("msq", [128, 512], F32)
                    sd = sc.sbuf("sd", [128, 512], F32)
                    rinv = sc.sbuf("rinvc", [128, 512], F32)
                    tt = [sc.sbuf("tt%d" % i, [128, 512], F32) for i in range(2)]
                    rr = [sc.sbuf("rr%d" % i, [128, 512], F32) for i in range(3)]
                    ca = [sc.sbuf("ca%d" % i, [128, 512], F32) for i in range(2)]
                    yst = [sc.sbuf("yst%d" % i, [128, D], F32) for i in range(2)] if last else None
                    S.op("sp", lambda e: e.dma_start(out=wout[:], in_=dap(woutb, l * D * D, [[D, 128], [128 * D, 8], [1, D]])), writes=["wout"], dma="wout", extra_deps=wc[l])
                    wctr = [0]
                    cnt = [0]

                    def loads(t):
                        s = t % 2
                        S.op("sp", lambda e: e.dma_start(out=xz[s][:], in_=dap(xTf, t * 512, [[SD, 128], [128 * SD, 8], [1, 512]])), writes=[("z", s, j) for j in range(8)], dma=("xz", s))
                        S.op("sp", lambda e: e.dma_start(out=mix[s][:, 0:4, :], in_=dap(ATs, t * 512, [[SD, 128], [128 * SD, 4], [1, 512]])), writes=[("mixa", s)], dma=("mixa", s))
                        S.op("sp", lambda e: e.dma_start(out=ub[s][:], in_=dap(Us, t * 512, [[SD + 2, 128], [128 * (SD + 2), 4], [1, 514]])), writes=["ub"], dma="ub")
                        S.op("sp", lambda e: e.dma_start(out=gbb[s][:], in_=dap(GBs, t * 512, [[SD, 128], [128 * SD, 4], [1, 512]])), writes=["gbb"], dma="gbb")

                    def conv(t):
                        s = t % 2
                        for cc in range(4):
                            S.op("dve", lambda e, cc=cc: e.tensor_scalar(out=ca[0][:], in0=ub[s][:, cc, 0:512], scalar1=cw[:, l, 0, cc:cc + 1], scalar2=None, op0=ALU.mult), reads=["ub"], writes=[("ca", 0)])
                            S.op("dve", lambda e, cc=cc: e.scalar_tensor_tensor(out=ca[1][:], in0=ub[s][:, cc, 1:513], scalar=cw[:, l, 1, cc:cc + 1], in1=ca[0][:], op0=ALU.mult, op1=ALU.add), reads=["ub", ("ca", 0)], writes=[("ca", 1)])
                            S.op("dve", lambda e, cc=cc: e.scalar_tensor_tensor(out=ca[0][:], in0=ub[s][:, cc, 2:514], scalar=cw[:, l, 2, cc:cc + 1], in1=ca[1][:], op0=ALU.mult, op1=ALU.add), reads=["ub", ("ca", 1)], writes=[("ca", 0)])
                            S.op("dve", lambda e, cc=cc: e.scalar_tensor_tensor(out=mix[s][:, 4 + cc, :], in0=ca[0][:], scalar=cbs[:, l, cc:cc + 1], in1=gbb[s][:, cc, :], op0=ALU.add, op1=ALU.mult), reads=[("ca", 0), "gbb"], writes=[("mixc", s, cc)])

                    def stat_acc(z, zs, j):
                        g = j // 4
                        i = j % 4
                        if i == 1:
                            S.op("dve", lambda e: e.tensor_tensor(out=sacc[g][:], in0=z[:, j - 1, :], in1=z[:, j, :], op=ALU.add), reads=[("z", zs, j - 1), ("z", zs, j)], writes=[("sacc", g)])
                        elif i >= 2:
                            S.op("dve", lambda e: e.tensor_tensor(out=sacc[g][:], in0=sacc[g][:], in1=z[:, j, :], op=ALU.add), reads=[("z", zs, j), ("sacc", g)], writes=[("sacc", g)])
                        if i == 0:
                            S.op("act", lambda e: e.activation(out=sacc[2 + g][:], in_=z[:, j, :], func=AF.Square), reads=[("z", zs, j)], writes=[("sacc", 2 + g)])
                        else:
                            k = cnt[0] % 2
                            cnt[0] += 1
                            S.op("act", lambda e: e.activation(out=sqb[k][:], in_=z[:, j, :], func=AF.Square), reads=[("z", zs, j)], writes=[("sqb", k)])
                            S.op("dve", lambda e: e.tensor_tensor(out=sacc[2 + g][:], in0=sacc[2 + g][:], in1=sqb[k][:], op=ALU.add), reads=[("sqb", k), ("sacc", 2 + g)], writes=[("sacc", 2 + g)])

                    def ln_stats(z, zs):
                        bs = bank()
                        for g in range(2):
                            S.op("pe", lambda e, g=g: e.matmul(PS[bs][:], lhsT=ones_f[:], rhs=sacc[g][:], start=(g == 0), stop=(g == 1)), reads=[("sacc", g)], writes=[("ps", bs)])
                        bq = bank()
                        for g in range(2):
                            S.op("pe", lambda e, g=g: e.matmul(PS[bq][:], lhsT=ones_f[:], rhs=sacc[2 + g][:], start=(g == 0), stop=(g == 1)), reads=[("sacc", 2 + g)], writes=[("ps", bq)])
                        return bs, bq

                    def ln_norm(z, zs, bs, bq, gt, bt_, outb):
                        S.op("act", lambda e: e.mul(out=mean[:], in_=PS[bs][:], mul=1.0 / D), reads=[("ps", bs)], writes=["mean"])
                        S.op("dve", lambda e: e.tensor_tensor(out=msq[:], in0=mean[:], in1=mean[:], op=ALU.mult), reads=["mean"], writes=["msq"])
                        S.op("dve", lambda e: e.scalar_tensor_tensor(out=rinv[:], in0=PS[bq][:], scalar=1.0 / D, in1=msq[:], op0=ALU.mult, op1=ALU.subtract), reads=[("ps", bq), "msq"], writes=["rinvc"])
                        S.op("act", lambda e: e.activation(out=sd[:], in_=rinv[:], func=AF.Sqrt, bias=LN_EPS, scale=1.0), reads=["rinvc"], writes=["sd"])
                        S.op("dve", lambda e: e.reciprocal(out=rinv[:], in_=sd[:]), reads=["sd"], writes=["rinvc"])
                        for j in range(8):
                            k = j % 2
                            S.op("dve", lambda e, j=j, k=k: e.tensor_tensor(out=tt[k][:], in0=z[:, j, :], in1=mean[:], op=ALU.subtract), reads=[("z", zs, j), "mean"], writes=[("tt", k)])
                            S.op("dve", lambda e, j=j, k=k: e.tensor_tensor(out=tt[k][:], in0=tt[k][:], in1=rinv[:], op=ALU.mult), reads=[("tt", k), "rinvc"], writes=[("tt", k)])
                            if outb is not None:
                                S.op("act", lambda e, j=j, k=k: e.activation(out=outb[:, j, :], in_=tt[k][:], func=AF.Identity, bias=bt_[:, l, j:j + 1], scale=gt[:, l, j:j + 1]), reads=[("tt", k)], writes=[("ob", j)])
                            S.op("act", lambda e, j=j, k=k: e.activation(out=z[:, j, :], in_=tt[k][:], func=AF.Identity, bias=bt_[:, l, j:j + 1], scale=gt[:, l, j:j + 1]), reads=[("tt", k)], writes=[("z", zs, j)])

                    loads(0)
                    conv(0)
                    pend = [None]
                    for t in range(NT):
                        s = t % 2
                        z = xz[s]
                        for j in range(8):
                            b = bank()
                            for n_ in range(8):
                                S.op("pe", lambda e, b=b, j=j, n_=n_: e.matmul(PS[b][:], lhsT=wout[:, n_, j * 128:(j + 1) * 128], rhs=mix[s][:, n_, :], start=(n_ == 0), stop=(n_ == 7)),
                                     reads=["wout", ("mixa", s) if n_ < 4 else ("mixc", s, n_ - 4)], writes=[("ps", b)])
                            S.op("dve", lambda e, b=b, j=j: e.scalar_tensor_tensor(out=z[:, j, :], in0=z[:, j, :], scalar=ALPHA, in1=PS[b][:], op0=ALU.mult, op1=ALU.add), reads=[("ps", b), ("z", s, j)], writes=[("z", s, j)])
                            stat_acc(z, s, j)
                        if pend[0] is not None:
                            pend[0]()
                            pend[0] = None
                        if t + 1 < NT:
                            loads(t + 1)
                        bs, bq = ln_stats(z, s)
                        ln_norm(z, s, bs, bq, g1, b1, x1b)
                        if t + 1 < NT:
                            conv(t + 1)
                        for g in range(8):
                            ws = wctr[0] % 3
                            wctr[0] += 1
                            S.op("sp", lambda e, g=g, ws=ws: e.dma_start(out=wr[ws][:].rearrange("p (c f) -> p c f", c=8), in_=dap(w1b, l * D * 4096 + g * 512, [[4096, 128], [128 * 4096, 8], [1, 512]])),
                                 writes=[("wr", ws)], dma=("wr", ws), extra_deps=wc[l])
                            for fl in range(4):
                                fc = g * 4 + fl
                                b = bank()
                                for dc in range(8):
                                    S.op("pe", lambda e, b=b, ws=ws, fl=fl, dc=dc: e.matmul(PS[b][:], lhsT=wr[ws][:, dc * 512 + fl * 128:dc * 512 + (fl + 1) * 128], rhs=x1b[:, dc, :], start=(dc == 0), stop=(dc == 7)),
                                         reads=[("wr", ws), ("ob", dc)], writes=[("ps", b)])
                                k = fc % 3
                                S.op("act", lambda e, b=b, k=k: e.activation(out=rr[k][:], in_=PS[b][:], func=AF.Relu), reads=[("ps", b)], writes=[("rr", k)])
                                S.op("pool" if fc % 3 == 2 else "dve", lambda e, k=k, fc=fc: e.tensor_tensor(out=hT[:, fc, :], in0=rr[k][:], in1=rr[k][:], op=ALU.mult), reads=[("rr", k)], writes=[("hT", fc)])
                        for j in range(8):
                            ws = wctr[0] % 3
                            wctr[0] += 1
                            S.op("sp", lambda e, j=j, ws=ws: e.dma_start(out=wr[ws][:], in_=dap(w2b, (l * 8 + j) * 128 * 4096, [[4096, 128], [1, 4096]])),
                                 writes=[("wr", ws)], dma=("wr", ws), extra_deps=wc[l])
                            b = bank()
                            for fc in range(32):
                                S.op("pe", lambda e, b=b, ws=ws, fc=fc: e.matmul(PS[b][:], lhsT=wr[ws][:, fc * 128:(fc + 1) * 128], rhs=hT[:, fc, :], start=(fc == 0), stop=(fc == 31)),
                                     reads=[("wr", ws), ("hT", fc)], writes=[("ps", b)])
                            S.op("dve", lambda e, b=b, j=j: e.scalar_tensor_tensor(out=z[:, j, :], in0=z[:, j, :], scalar=ALPHA, in1=PS[b][:], op0=ALU.mult, op1=ALU.add), reads=[("ps", b), ("z", s, j)], writes=[("z", s, j)])
                            stat_acc(z, s, j)
                        bs, bq = ln_stats(z, s)
                        ln_norm(z, s, bs, bq, g2, b2, xob)
                        zr = [("z", s, j) for j in range(8)]
                        if not last:
                            S.op("sp", lambda e, t=t, s=s: e.dma_start(out=dap(xTf, t * 512, [[SD, 128], [128 * SD, 8], [1, 512]]), in_=xz[s][:]), reads=zr, dma=("xzst", s))
                            S.op("sp", lambda e, t=t: e.dma_start(out=dap(xTb, t * 512, [[SD, 128], [128 * SD, 8], [1, 512]]), in_=xob[:]), reads=[("ob", j) for j in range(8)], dma="xobst")
                        else:
                            def fin(t=t, s=s):
                                for sub in range(4):
                                    k = sub % 2
                                    for hf in range(2):
                                        b = bank()
                                        for dl in range(4):
                                            dc = hf * 4 + dl
                                            S.op("pe", lambda e, b=b, sub=sub, dc=dc, dl=dl, s=s: e.transpose(out=PS[b][:, dl * 128:(dl + 1) * 128], in_=xz[s][:, dc, sub * 128:(sub + 1) * 128], identity=ident[:]),
                                                 reads=[("z", s, dc)], writes=[("ps", b)])
                                        S.op("act", lambda e, b=b, k=k, hf=hf: e.copy(out=yst[k][:, hf * 512:(hf + 1) * 512], in_=PS[b][:]), reads=[("ps", b)], writes=[("yst", k)])
                                    S.op("sp", lambda e, t=t, sub=sub, k=k: e.dma_start(out=dap(y_out, (t * 512 + sub * 128) * D, [[D, 128], [1, D]]), in_=yst[k][:]), reads=[("yst", k)], dma=("yst", k))
                            pend[0] = fin
                    if pend[0] is not None:
                        pend[0]()
                    S.barrier()

            kstop = int(os.environ.get("KSTOP", "99"))
            for l in range(L):
                if kstop >= 2:
                    phase_a(l)
                if kstop >= 3:
                    attention(l)
                if kstop >= 4:
                    phase_c(l)

    prog(S)
    with contextlib.ExitStack() as st:
        S.start_real(st)
        prog(S)
    return nc, S


_CACHE = {}


def _consts():
    i = np.arange(1280)
    b = np_bucket(639 - i)
    oh = (b[None, :] == np.arange(32)[:, None]).astype(np.float32)
    ident = np.eye(128, dtype=np.float32)
    anti = np.ascontiguousarray(ident[::-1])
    return oh, ident, anti


def run_frames(frames, valids, weights, SD, L):
    key = (SD, L)
    if key not in _CACHE:
        _CACHE[key] = build_program(SD, L)[0]
    nc = _CACHE[key]
    oh, ident, anti = _consts()
    in_maps = []
    for x, nv in zip(frames, valids):
        tok = (np.arange(SD) < nv).astype(np.float32)
        vcol = np.ascontiguousarray(tok.reshape(SD // 128, 128).T)
        vrow = np.ascontiguousarray(np.broadcast_to(tok[None, :], (128, SD)))
        kt_valid = tok.reshape(SD // 128, 128)[:, 0]
        kmask = np.ascontiguousarray(np.broadcast_to(((1.0 - kt_valid) * NEG).astype(np.float32)[None, :], (128, SD // 128)))
        m = {"x": x, "vcol": vcol, "vrow": vrow, "kmask": kmask, "ohrev": oh, "ident": ident, "anti": anti}
        m.update(weights)
        if os.environ.get("KFAST") == "1":
            for k in ("w_in", "w_out", "w_mlp1", "w_mlp2"):
                m[k] = np.zeros((1, 8, 8), np.float32)
        in_maps.append(m)
    res = run_bass_kernel_spmd(nc, in_maps, core_ids=list(range(8)))
    return [r["y"] for r in res.results]


def kernel(x_prompt, x_sample, w_in, w_out, conv_w, conv_b, lambda_q1, lambda_k1, lambda_q2,
           lambda_k2, subln_g, rel_bias, ln1_g, ln1_b, w_mlp1, w_mlp2, ln2_g, ln2_b):
    SD = 8192
    f = lambda a: np.ascontiguousarray(np.asarray(a, dtype=np.float32))
    weights = {"w_in": f(w_in), "w_out": f(w_out), "conv_w": f(conv_w), "conv_b": f(conv_b),
               "lambda_q1": f(lambda_q1), "lambda_k1": f(lambda_k1), "lambda_q2": f(lambda_q2),
               "lambda_k2": f(lambda_k2), "subln_g": f(subln_g), "rel_bias": f(rel_bias),
               "ln1_g": f(ln1_g), "ln1_b": f(ln1_b), "w_mlp1": f(w_mlp1), "w_mlp2": f(w_mlp2),
               "ln2_g": f(ln2_g), "ln2_b": f(ln2_b)}
    xp = f(x_prompt)
    xs = f(x_sample)
    frames = [xp[b] for b in range(4)]
    valids = [SD] * 4
    for b in range(4):
        fr = np.zeros((SD, D), np.float32)
        fr[:xs.shape[1]] = xs[b]
        frames.append(fr)
        valids.append(xs.shape[1])
    ys = run_frames(frames, valids, weights, SD, 4)
    y_prompt = np.stack([ys[b] for b in range(4)], axis=0)
    y_sample = np.stack([ys[4 + b][:xs.shape[1]] for b in range(4)], axis=0)
    return (y_prompt, y_sample)
```

```python
import contextlib
import os
import numpy as np
import concourse.bass as bass
import concourse.mybir as mybir
from concourse.bass_utils import run_bass_kernel_spmd

F32 = mybir.dt.float32
BF16 = mybir.dt.bfloat16
AF = mybir.ActivationFunctionType
ALU = mybir.AluOpType


class _Dummy:
    def __getitem__(self, k):
        return self

    def __getattr__(self, k):
        return self

    def __call__(self, *a, **k):
        return self


class Sched:
    CENG = ("pe", "act", "dve", "pool")

    def __init__(self, nc):
        self.nc = nc
        self.needed = None
        self.reset(dry=True)

    def reset(self, dry):
        self.dry = dry
        self.n = 0
        self.eng_of = []
        self.dma_of = []
        self.lastw = {}
        self.readers = {}
        self.need_now = set()
        self.sigcount = {e: 0 for e in self.CENG}
        self.sig_of = {}
        self.dma_cnt = {}
        self.seen = {e: {} for e in self.CENG + ("sp",)}
        self.last_on = {}
        self.dma_out = []
        self.nwaits = 0
        self.ninstr = 0

    def start_real(self, stack):
        self.needed = self.need_now
        self.reset(dry=False)
        self.stack = stack
        self.esem = {e: stack.enter_context(self.nc.semaphore("es_" + e)) for e in self.CENG}
        self.dsem = {}

    def eng(self, e):
        nc = self.nc
        return {"pe": nc.tensor, "act": nc.scalar, "dve": nc.vector, "pool": nc.gpsimd, "sp": nc.sync}[e]

    def _dsem(self, key):
        if key not in self.dsem:
            self.dsem[key] = self.stack.enter_context(self.nc.semaphore("ds_%d" % len(self.dsem)))
        return self.dsem[key]

    def op(self, eng, fn, reads=(), writes=(), dma=None, extra_deps=()):
        i = self.n
        self.n += 1
        hard = set(extra_deps)
        war = set()
        for k in reads:
            w = self.lastw.get(k)
            if w is not None:
                hard.add(w)
        for k in writes:
            w = self.lastw.get(k)
            if w is not None:
                hard.add(w)
            rd = self.readers.get(k)
            if rd:
                war.update(rd.values())
        for k in writes:
            self.lastw[k] = i
            self.readers[k] = {}
        for k in reads:
            self.readers.setdefault(k, {})[("d", i) if dma else eng] = i
        war -= hard
        self.eng_of.append(eng)
        if dma is not None:
            c = self.dma_cnt.get(dma, 0) + 16
            self.dma_cnt[dma] = c
            self.dma_of.append((dma, c))
            self.dma_out.append(i)
        else:
            self.dma_of.append(None)
        waits = []
        for d, is_war in [(d, False) for d in hard] + [(d, True) for d in war]:
            dd = self.dma_of[d]
            if dd is not None:
                waits.append(("d", dd[0], dd[1]))
                continue
            de = self.eng_of[d]
            if de == eng and dma is None:
                if eng == "pe" or is_war:
                    continue
            self.need_now.add(d)
            if not self.dry:
                waits.append(("e", de, self.sig_of[d]))
        if dma is None and fn is not None:
            if not self.dry and i in self.needed:
                self.sigcount[eng] += 1
                self.sig_of[i] = self.sigcount[eng]
            self.last_on[eng] = i
        if self.dry:
            return i
        E = self.eng(eng)
        seen = self.seen[eng]
        for kind, a, v in waits:
            key = (kind, a)
            if seen.get(key, 0) >= v:
                continue
            seen[key] = v
            sem = self._dsem(a) if kind == "d" else self.esem[a]
            E.wait_ge(sem, v)
            self.nwaits += 1
        if fn is not None:
            ins = fn(E)
            self.ninstr += 1
            if dma is not None:
                ins.then_inc(self._dsem(dma), 16)
            elif i in self.needed:
                ins.then_inc(self.esem[eng], 1)
        return i

    def barrier(self, engines=None):
        deps = list(self.last_on.values()) + list(self.dma_out)
        for e in (engines or (self.CENG + ("sp",))):
            self.op(e, None, extra_deps=deps)
        if engines is None:
            self.dma_out = []
            self.lastw = {}
            self.readers = {}

    @contextlib.contextmanager
    def scope(self):
        sc = _Scope(self)
        try:
            yield sc
        finally:
            sc.close()


class _Scope:
    def __init__(self, S):
        self.S = S
        self.stack = None if S.dry else contextlib.ExitStack()

    def sbuf(self, name, shape, dtype):
        if self.S.dry:
            return _Dummy()
        self.S.uid = getattr(self.S, "uid", 0) + 1
        return self.stack.enter_context(self.S.nc.sbuf_tensor("sb%d_%s" % (self.S.uid, name), list(shape), dtype))

    def psum(self, name, shape, dtype):
        if self.S.dry:
            return _Dummy()
        self.S.uid = getattr(self.S, "uid", 0) + 1
        return self.stack.enter_context(self.S.nc.psum_tensor("pp%d_%s" % (self.S.uid, name), list(shape), dtype))

    def close(self):
        if self.stack is not None:
            self.stack.close()


import math

D = 1024
LN_EPS = 1e-5
ALPHA = 8.0 ** 0.25
NEG = -30000.0


def lam_init(l):
    return 0.8 - 0.6 * math.exp(-0.3 * l)


def np_bucket(rel):
    half = 16
    me = 8
    ret = np.where(rel > 0, half, 0)
    n = np.abs(rel)
    nf = np.maximum(n, 1).astype(np.float32)
    large = me + (np.log(nf / np.float32(me)) / np.float32(math.log(128 / 8)) * np.float32(8)).astype(np.int32)
    large = np.minimum(large, half - 1)
    return ret + np.where(n < me, n, large)


def build_program(SD, L):
    nc = bass.Bass("TRN2", target_bir_lowering=False)
    NT = SD // 512
    NKT = SD // 128
    AX = mybir.AxisListType.X

    def din(name, shape):
        return nc.dram_tensor(name, list(shape), F32, kind="ExternalInput").ap()

    def dscr(name, shape, dt):
        return nc.dram_tensor(name, list(shape), dt, kind="Internal").ap()

    def dap(t, off, dims):
        return bass.AP(t.tensor, off, [list(d) for d in dims])

    x_in = din("x", [SD, D])
    vcol_in = din("vcol", [128, NKT])
    vrow_in = din("vrow", [128, SD])
    kmask_in = din("kmask", [128, NKT])
    oh_in = din("ohrev", [32, 1280])
    ident_in = din("ident", [128, 128])
    anti_in = din("anti", [128, 128])
    KFAST = os.environ.get("KFAST") == "1"
    KSET = int(os.environ.get("KSET", "99"))
    w_in = din("w_in", [4, D, 3072] if not KFAST else [1, 8, 8])
    w_out = din("w_out", [4, D, D] if not KFAST else [1, 8, 8])
    conv_w = din("conv_w", [4, 3, 512])
    conv_b = din("conv_b", [4, 512])
    lq1 = din("lambda_q1", [4, 64]); lk1 = din("lambda_k1", [4, 64])
    lq2 = din("lambda_q2", [4, 64]); lk2 = din("lambda_k2", [4, 64])
    subln_g = din("subln_g", [4, 128])
    rel_bias = din("rel_bias", [32, 4])
    ln1_g = din("ln1_g", [4, D]); ln1_b = din("ln1_b", [4, D])
    w_mlp1 = din("w_mlp1", [4, D, 4096] if not KFAST else [1, 8, 8]); w_mlp2 = din("w_mlp2", [4, 4096, D] if not KFAST else [1, 8, 8])
    ln2_g = din("ln2_g", [4, D]); ln2_b = din("ln2_b", [4, D])
    y_out = nc.dram_tensor("y", [SD, D], F32, kind="ExternalOutput").ap()

    winb = dscr("winb", [L, D, 3072], BF16)
    woutb = dscr("woutb", [L, D, D], BF16)
    w1b = dscr("w1b", [L, D, 4096], BF16)
    w2b = dscr("w2b", [L * 8 * 128, 4096], BF16)
    xTf = dscr("xTf", [D, SD], F32)
    xTb = dscr("xTb", [D, SD], BF16)
    QT = dscr("QT", [512, SD], BF16)
    KT = dscr("KT", [512, SD], BF16)
    Vs = dscr("Vs", [SD, 512], BF16)
    Us = dscr("Us", [512, SD + 2], F32)
    GBs = dscr("GBs", [512, SD], F32)
    ATs = dscr("ATs", [512, SD], BF16)
    Tscr = dscr("Tscr", [4, 1280], F32)
    Gscr = dscr("Gscr", [128, 4 * 1152], F32)

    S = Sched(nc)
    bankctr = [0]

    def prog(S):
        bankctr[0] = 0
        with S.scope() as pc:
            ident = pc.sbuf("ident", [128, 128], F32)
            anti = pc.sbuf("anti", [128, 128], F32)
            ones_f = pc.sbuf("ones_f", [128, 128], F32)
            ones_b = pc.sbuf("ones_b", [128, 128], BF16)
            vcol = pc.sbuf("vcol", [128, NKT], F32)
            kmask = pc.sbuf("kmask", [128, NKT], F32)
            cb = pc.sbuf("cb", [128, 4, 3, NKT], F32)
            neglam = pc.sbuf("neglam", [128, 4], F32)
            gsc = pc.sbuf("gsc", [128, 4], F32)
            g1 = pc.sbuf("g1", [128, 4, 8], F32); b1 = pc.sbuf("b1", [128, 4, 8], F32)
            g2 = pc.sbuf("g2", [128, 4, 8], F32); b2 = pc.sbuf("b2", [128, 4, 8], F32)
            cw = pc.sbuf("cw", [128, 4, 3, 4], F32)
            cbs = pc.sbuf("cbs", [128, 4, 4], F32)
            PSW = [pc.psum("psw%d" % i, [128, 1024], F32) for i in range(4)]
            PS = [PSW[i // 2][:, (i % 2) * 512:(i % 2 + 1) * 512] for i in range(8)]
            epsb = pc.sbuf("epsb", [128, 1], F32)

            def bank():
                i = bankctr[0] % 8
                bankctr[0] += 1
                return i

            wc = {}
            prevc = []

            def cdma(fn, wkey, l):
                i = S.op("pool", fn, writes=[wkey], dma=("wc", l), extra_deps=list(prevc))
                prevc[:] = [i]
                S.dma_out.remove(i)
                return i
            for l in range(L if not KFAST else 0):
                ops = []
                for q in range(4):
                    ops.append(cdma(lambda e, l=l, q=q: e.dma_start(out=dap(winb, l * D * 3072 + q * 768 * 1024, [[1024, 768], [1, 1024]]),
                                                                     in_=dap(w_in, l * D * 3072 + q * 768 * 1024, [[1024, 768], [1, 1024]])),
                                    ("winb", l, q), l))
                ops.append(cdma(lambda e, l=l: e.dma_start(out=dap(woutb, l * D * D, [[1024, 1024], [1, 1024]]),
                                                            in_=dap(w_out, l * D * D, [[1024, 1024], [1, 1024]])),
                                ("woutb", l), l))
                for hf in range(4):
                    ops.append(cdma(lambda e, l=l, hf=hf: e.dma_start(
                        out=dap(w1b, l * D * 4096 + hf * 1024 * 1024, [[1024, 1024], [1, 1024]]),
                        in_=dap(w_mlp1, l * D * 4096 + hf * 1024 * 1024, [[1024, 1024], [1, 1024]])),
                        ("w1b", l, hf), l))
                for j in range(8):
                    ops.append(cdma(lambda e, l=l, j=j: e.dma_start(
                        out=dap(w2b, (l * 8 + j) * 128 * 4096, [[4096, 128], [128, 32], [1, 128]]),
                        in_=dap(w_mlp2, l * 4096 * D + j * 128, [[1024, 128], [128 * 1024, 32], [1, 128]])),
                        ("w2b", l, j), l))
                wc[l] = ops

            with S.scope() as sc:
                lam4 = sc.sbuf("lam4", [128, 4, 4, 64], F32)
                prod = sc.sbuf("prod", [128, 2, 4, 64], F32)
                ss = sc.sbuf("ss", [128, 2, 4], F32)
                ee = sc.sbuf("ee", [128, 2, 4], F32)
                lam = sc.sbuf("lam", [128, 4], F32)
                cl = sc.sbuf("cl", [128, 4], F32); cr = sc.sbuf("cr", [128, 4], F32)
                rb = sc.sbuf("rb", [32, 4], F32)
                oh = sc.sbuf("oh", [32, 1280], F32)
                tsb = sc.sbuf("tsb", [4, 1280], F32)
                hk = sc.sbuf("hk", [128, 4, 1152], F32)
                G = sc.sbuf("Gs", [128, 4, 1152], F32)
                zt = sc.sbuf("zt", [128, 4, 1], F32)

                def ld(out_fn, in_fn, nonc=False):
                    if nonc:
                        S.op("sp", lambda e: e.dma_start(out=out_fn(), in_=in_fn(), allow_slow_non_contiguous=True), dma="setup")
                    else:
                        S.op("sp", lambda e: e.dma_start(out=out_fn(), in_=in_fn()), dma="setup")
                ld(lambda: ident[:], lambda: ident_in)
                ld(lambda: anti[:], lambda: anti_in)
                ld(lambda: vcol[:], lambda: vcol_in)
                ld(lambda: kmask[:], lambda: kmask_in)
                ld(lambda: oh[:], lambda: oh_in)
                ld(lambda: rb[:], lambda: rel_bias)
                for i, t in enumerate((lq1, lk1, lq2, lk2)):
                    ld(lambda i=i: lam4[:, i, :, :], lambda t=t: dap(t, 0, [[0, 128], [64, 4], [1, 64]]))
                ld(lambda: cl[:], lambda: dap(rel_bias, 15 * 4, [[0, 128], [1, 4]]))
                ld(lambda: cr[:], lambda: dap(rel_bias, 31 * 4, [[0, 128], [1, 4]]))
                ld(lambda: gsc[:], lambda: dap(subln_g, 0, [[1, 128], [128, 4]]), nonc=True)
                for dst, src in ((g1, ln1_g), (b1, ln1_b), (g2, ln2_g), (b2, ln2_b)):
                    for l in range(4):
                        ld(lambda dst=dst, l=l: dst[:, l, :], lambda src=src, l=l: dap(src, l * D, [[1, 128], [128, 8]]), nonc=True)
                for l in range(4):
                    for k in range(3):
                        ld(lambda l=l, k=k: cw[:, l, k, :], lambda l=l, k=k: dap(conv_w, (l * 3 + k) * 512, [[1, 128], [128, 4]]), nonc=True)
                    ld(lambda l=l: cbs[:, l, :], lambda l=l: dap(conv_b, l * 512, [[1, 128], [128, 4]]), nonc=True)
                S.op("dve", lambda e: e.memset(ones_f[:], 1.0))
                S.op("dve", lambda e: e.memset(epsb[:], LN_EPS))
                S.op("dve", lambda e: e.memset(ones_b[:], 1.0))
                S.op("dve", lambda e: e.memset(zt[:], 0.0))
                S.barrier()
                if KSET <= 1:
                    return
                S.op("sp", lambda e: e.dma_start(out=dap(Us, 0, [[SD + 2, 128], [128 * (SD + 2), 4], [1, 1]]), in_=zt[:], allow_slow_non_contiguous=True), dma="setup")
                S.op("sp", lambda e: e.dma_start(out=dap(Us, SD + 1, [[SD + 2, 128], [128 * (SD + 2), 4], [1, 1]]), in_=zt[:], allow_slow_non_contiguous=True), dma="setup")
                S.op("dve", lambda e: e.tensor_tensor(out=prod[:, 0, :, :], in0=lam4[:, 0, :, :], in1=lam4[:, 1, :, :], op=ALU.mult))
                S.op("dve", lambda e: e.tensor_tensor(out=prod[:, 1, :, :], in0=lam4[:, 2, :, :], in1=lam4[:, 3, :, :], op=ALU.mult))
                S.barrier()
                if KSET <= 2:
                    return
                S.op("dve", lambda e: e.reduce_sum(out=ss[:], in_=prod[:], axis=AX))
                S.barrier()
                if KSET <= 3:
                    return
                S.op("act", lambda e: e.activation(out=ee[:], in_=ss[:], func=AF.Exp))
                S.barrier()
                if KSET <= 4:
                    return
                S.op("dve", lambda e: e.tensor_tensor(out=lam[:], in0=ee[:, 0, :], in1=ee[:, 1, :], op=ALU.subtract))
                S.barrier()
                if KSET <= 5:
                    return
                for l in range(4):
                    S.op("dve", lambda e, l=l: e.tensor_scalar(out=neglam[:, l:l + 1], in0=lam[:, l:l + 1], scalar1=lam_init(l), scalar2=-1.0, op0=ALU.add, op1=ALU.mult))
                    S.op("dve", lambda e, l=l: e.tensor_scalar(out=gsc[:, l:l + 1], in0=gsc[:, l:l + 1], scalar1=1.0 - lam_init(l), scalar2=None, op0=ALU.mult))
                for h in range(4):
                    S.op("dve", lambda e, h=h: e.tensor_scalar(out=cb[:, h, 0, :], in0=kmask[:], scalar1=cl[:, h:h + 1], scalar2=None, op0=ALU.add))
                    S.op("dve", lambda e, h=h: e.tensor_copy(out=cb[:, h, 1, :], in_=kmask[:]))
                    S.op("dve", lambda e, h=h: e.tensor_scalar(out=cb[:, h, 2, :], in0=kmask[:], scalar1=cr[:, h:h + 1], scalar2=None, op0=ALU.add))
                for c, w in ((0, 512), (1, 512), (2, 256)):
                    S.op("pe", lambda e, c=c, w=w: e.matmul(PS[c][0:4, 0:w], lhsT=rb[:], rhs=oh[:, c * 512:c * 512 + w], start=True, stop=True))
                S.barrier()
                if KSET <= 6:
                    return
                for c, w in ((0, 512), (1, 512), (2, 256)):
                    S.op("act", lambda e, c=c, w=w: e.copy(out=tsb[:, c * 512:c * 512 + w], in_=PS[c][0:4, 0:w]))
                S.barrier()
                if KSET <= 7:
                    return
                S.op("sp", lambda e: e.dma_start(out=Tscr, in_=tsb[:]), dma="setup")
                S.barrier()
                if KSET <= 8:
                    return
                for h in range(4):
                    S.op("sp", lambda e, h=h: e.dma_start(out=hk[:, h, :], in_=dap(Tscr, h * 1280, [[1, 128], [1, 1152]])), dma="setup")
                S.barrier()
                if KSET <= 9:
                    return
                for h in range(4):
                    for c, w in ((0, 512), (1, 512), (2, 128)):
                        S.op("pe", lambda e, h=h, c=c, w=w: e.matmul(PS[3 + c][:, 0:w], lhsT=anti[:], rhs=hk[:, h, c * 512:c * 512 + w], start=True, stop=True))
                    S.barrier()
                    if KSET <= 10:
                        return
                    for c, w in ((0, 512), (1, 512), (2, 128)):
                        S.op("act", lambda e, h=h, c=c, w=w: e.copy(out=G[:, h, c * 512:c * 512 + w], in_=PS[3 + c][:, 0:w]))
                    S.barrier()
                    if KSET <= 11:
                        return
                S.barrier()
                if KSET <= 12:
                    return
                S.op("sp", lambda e: e.dma_start(out=Gscr, in_=G[:].rearrange("p h m -> p (h m)")), dma="setup")
                S.barrier()
                if KSET <= 13:
                    return

            with S.scope() as sc:
                xin = [sc.sbuf("xin%d" % i, [128, 4, D], F32) for i in range(2)]
                xfs = [sc.sbuf("xfs%d" % i, [128, 8, 512], F32) for i in range(2)]
                xbs = [sc.sbuf("xbs%d" % i, [128, 8, 512], BF16) for i in range(2)]
                XV = int(os.environ.get("XV", "0"))
                for t in range(NT):
                    s = t % 2
                    S.op("sp", lambda e, t=t, s=s: e.dma_start(out=xin[s][:], in_=dap(x_in, t * 512 * D, [[D, 128], [128 * D, 4], [1, D]])),
                         writes=[("xin", s)], dma=("xin", s))
                    if XV == 1:
                        continue
                    for dc in range(8):
                        b = bank()
                        for sub in range(4):
                            if XV == 2 and sub > 0:
                                continue
                            S.op("pe", lambda e, b=b, s=s, sub=sub, dc=dc: e.transpose(out=PS[b][:, sub * 128:(sub + 1) * 128], in_=xin[s][:, sub, dc * 128:(dc + 1) * 128], identity=ident[:]),
                                 reads=[("xin", s)], writes=[("ps", b)])
                        if XV == 4:
                            continue
                        S.op("act", lambda e, b=b, s=s, dc=dc: e.copy(out=xfs[s][:, dc, :], in_=PS[b][:]), reads=[("ps", b)], writes=[("xfs", s)])
                        if XV == 5:
                            continue
                        S.op("dve", lambda e, b=b, s=s, dc=dc: e.tensor_copy(out=xbs[s][:, dc, :], in_=xfs[s][:, dc, :]), reads=[("xfs", s)], writes=[("xbs", s)])
                    if XV >= 3:
                        continue
                    S.op("sp", lambda e, t=t, s=s: e.dma_start(out=dap(xTf, t * 512, [[SD, 128], [128 * SD, 8], [1, 512]]), in_=xfs[s][:]),
                         reads=[("xfs", s)], dma=("xst", s))
                    S.op("sp", lambda e, t=t, s=s: e.dma_start(out=dap(xTb, t * 512, [[SD, 128], [128 * SD, 8], [1, 512]]), in_=xbs[s][:]),
                         reads=[("xbs", s)], dma=("xst", s))
                S.barrier()

            def phase_a(l):
                with S.scope() as sc:
                    win = sc.sbuf("win", [128, 8, 3072], BF16)
                    xb = [sc.sbuf("xb%d" % i, [128, 8, 512], BF16) for i in range(2)]
                    vr = [sc.sbuf("vr%d" % i, [128, 512], F32) for i in range(2)]
                    qkst = [sc.sbuf("qkst%d" % i, [128, 8, 512], BF16) for i in range(2)]
                    vst = [sc.sbuf("vst%d" % i, [128, 4, 512], BF16) for i in range(2)]
                    hcs = [sc.sbuf("hcs%d" % i, [128, 512], F32) for i in range(2)]
                    ust = [sc.sbuf("ust%d" % i, [128, 4, 512], F32) for i in range(2)]
                    gbst = [sc.sbuf("gbst%d" % i, [128, 4, 512], F32) for i in range(2)]
                    for hf in range(2):
                        S.op("sp", lambda e, hf=hf: e.dma_start(out=win[:, hf * 4:(hf + 1) * 4, :], in_=dap(winb, l * D * 3072 + hf * 4 * 128 * 3072, [[3072, 128], [128 * 3072, 4], [1, 3072]])),
                             writes=[("win", hf)], dma=("win", hf), extra_deps=wc[l][0:4])

                    def loads(t):
                        s = t % 2
                        S.op("sp", lambda e: e.dma_start(out=xb[s][:], in_=dap(xTb, t * 512, [[SD, 128], [128 * SD, 8], [1, 512]])),
                             writes=[("xb", s)], dma=("xb", s))
                        S.op("sp", lambda e: e.dma_start(out=vr[s][:], in_=dap(vrow_in, t * 512, [[SD, 128], [1, 512]])),
                             writes=[("vr", s)], dma=("vr", s))
                    loads(0)
                    hk_ = 0
                    for t in range(NT):
                        s = t % 2
                        if t + 1 < NT:
                            loads(t + 1)
                        for c in range(8):
                            b = bank()
                            for dc in range(8):
                                S.op("pe", lambda e, b=b, c=c, dc=dc: e.matmul(PS[b][:], lhsT=win[:, dc, c * 128:(c + 1) * 128], rhs=xb[s][:, dc, :], start=(dc == 0), stop=(dc == 7)),
                                     reads=[("win", dc // 4), ("xb", s)], writes=[("ps", b)])
                            if c % 2 == 0:
                                S.op("act", lambda e, b=b, c=c: e.copy(out=qkst[s][:, c, :], in_=PS[b][:]), reads=[("ps", b)], writes=[("qkst", s)])
                            else:
                                S.op("dve", lambda e, b=b, c=c: e.tensor_copy(out=qkst[s][:, c, :], in_=PS[b][:]), reads=[("ps", b)], writes=[("qkst", s)])
                        S.op("sp", lambda e: e.dma_start(out=dap(QT, t * 512, [[SD, 128], [128 * SD, 4], [1, 512]]), in_=qkst[s][:, 0:4, :]), reads=[("qkst", s)], dma=("qkst", s))
                        S.op("sp", lambda e: e.dma_start(out=dap(KT, t * 512, [[SD, 128], [128 * SD, 4], [1, 512]]), in_=qkst[s][:, 4:8, :]), reads=[("qkst", s)], dma=("qkst", s))
                        for sub in range(4):
                            b = bank()
                            for dc in range(8):
                                S.op("pe", lambda e, b=b, sub=sub, dc=dc: e.matmul(PS[b][:], lhsT=xb[s][:, dc, sub * 128:(sub + 1) * 128], rhs=win[:, dc, 1024:1536], start=(dc == 0), stop=(dc == 7)),
                                     reads=[("win", dc // 4), ("xb", s)], writes=[("ps", b)])
                            S.op("dve", lambda e, b=b, sub=sub: e.tensor_scalar(out=vst[s][:, sub, :], in0=PS[b][:], scalar1=vcol[:, t * 4 + sub:t * 4 + sub + 1], scalar2=None, op0=ALU.mult),
                                 reads=[("ps", b)], writes=[("vst", s)])
                        S.op("sp", lambda e: e.dma_start(out=dap(Vs, t * 512 * 512, [[512, 128], [128 * 512, 4], [1, 512]]), in_=vst[s][:]), reads=[("vst", s)], dma=("vst", s))
                        for cc in range(4):
                            bh = bank()
                            for dc in range(8):
                                S.op("pe", lambda e, b=bh, cc=cc, dc=dc: e.matmul(PS[b][:], lhsT=win[:, dc, 2560 + cc * 128:2560 + (cc + 1) * 128], rhs=xb[s][:, dc, :], start=(dc == 0), stop=(dc == 7)),
                                     reads=[("win", dc // 4), ("xb", s)], writes=[("ps", bh)])
                            k = hk_ % 2
                            hk_ += 1
                            S.op("dve", lambda e, b=bh, k=k: e.tensor_tensor(out=hcs[k][:], in0=PS[b][:], in1=vr[s][:], op=ALU.mult),
                                 reads=[("ps", bh), ("vr", s)], writes=[("hcs", k)])
                            bg = bank()
                            for dc in range(8):
                                S.op("pe", lambda e, b=bg, cc=cc, dc=dc: e.matmul(PS[b][:], lhsT=win[:, dc, 2048 + cc * 128:2048 + (cc + 1) * 128], rhs=xb[s][:, dc, :], start=(dc == 0), stop=(dc == 7)),
                                     reads=[("win", dc // 4), ("xb", s)], writes=[("ps", bg)])
                            S.op("dve", lambda e, b=bg, k=k, cc=cc: e.tensor_tensor(out=ust[s][:, cc, :], in0=PS[b][:], in1=hcs[k][:], op=ALU.mult),
                                 reads=[("ps", bg), ("hcs", k)], writes=[("ust", s)])
                            bb = bank()
                            for dc in range(8):
                                S.op("pe", lambda e, b=bb, cc=cc, dc=dc: e.matmul(PS[b][:], lhsT=win[:, dc, 1536 + cc * 128:1536 + (cc + 1) * 128], rhs=xb[s][:, dc, :], start=(dc == 0), stop=(dc == 7)),
                                     reads=[("win", dc // 4), ("xb", s)], writes=[("ps", bb)])
                            S.op("act", lambda e, b=bb, cc=cc: e.copy(out=gbst[s][:, cc, :], in_=PS[b][:]), reads=[("ps", bb)], writes=[("gbst", s)])
                        S.op("sp", lambda e: e.dma_start(out=dap(Us, 1 + t * 512, [[SD + 2, 128], [128 * (SD + 2), 4], [1, 512]]), in_=ust[s][:]), reads=[("ust", s)], dma=("ust", s))
                        S.op("sp", lambda e: e.dma_start(out=dap(GBs, t * 512, [[SD, 128], [128 * SD, 4], [1, 512]]), in_=gbst[s][:]), reads=[("gbst", s)], dma=("gbst", s))
                    S.barrier()

            def attention(l):
                with S.scope() as sc:
                    qh = [sc.sbuf("qh%d" % i, [128, SD], BF16) for i in range(2)]
                    kh = [sc.sbuf("kh%d" % i, [128, SD], BF16) for i in range(2)]
                    vh = [sc.sbuf("vh%d" % i, [128, NKT, 128], BF16) for i in range(2)]
                    G = sc.sbuf("G", [128, 4, 1152], F32)
                    NE = 4
                    E = [sc.sbuf("E%d" % i, [128, 1024], BF16) for i in range(NE)]
                    bt = [sc.sbuf("bt%d" % i, [128, 1024], F32) for i in range(2)]
                    acc = [[sc.sbuf("acc%d_%d" % (g_, i), [128, 512], F32) for i in range(2)] for g_ in range(2)]
                    lnb = sc.sbuf("lnb", [128, 1024], F32)
                    s2s = sc.sbuf("s2s", [128, 512], F32)
                    rcp = sc.sbuf("rcp", [128, 1024], F32)
                    t1 = sc.sbuf("t1", [128, 512], F32); t2 = sc.sbuf("t2", [128, 512], F32)
                    av = sc.sbuf("av", [128, 512], F32); sq = sc.sbuf("sq", [128, 512], F32)
                    lr = sc.sbuf("lr", [128, 512], F32); rinv = sc.sbuf("rinv", [128, 512], F32)
                    aost = [sc.sbuf("aost%d" % i, [128, 512], BF16) for i in range(2)]
                    S.op("sp", lambda e: e.dma_start(out=G[:].rearrange("p h m -> p (h m)"), in_=Gscr), writes=["G"], dma="gld")

                    def hload(h):
                        s = h % 2
                        S.op("sp", lambda e: e.dma_start(out=qh[s][:], in_=dap(QT, h * 128 * SD, [[SD, 128], [1, SD]])), writes=[("qh", s)], dma=("qh", s))
                        S.op("sp", lambda e: e.dma_start(out=kh[s][:], in_=dap(KT, h * 128 * SD, [[SD, 128], [1, SD]])), writes=[("kh", s)], dma=("kh", s))
                        S.op("sp", lambda e: e.dma_start(out=vh[s][:], in_=dap(Vs, h * 128, [[512, 128], [128 * 512, NKT], [1, 128]])), writes=[("vh", s)], dma=("vh", s))
                    hload(0)
                    units = [(h, qt, kt) for h in range(4) for qt in range(NT) for kt in range(NKT)]
                    NU = len(units)

                    def scores(u):
                        h, qt, kt = units[u]
                        s = h % 2
                        p = u % 2
                        if qt == 0 and kt == 2 and h + 1 < 4:
                            hload(h + 1)
                        S.op("pe", lambda e: e.matmul(PSW[p][:, 0:512], lhsT=kh[s][0:64, kt * 128:(kt + 1) * 128], rhs=qh[s][0:64, qt * 512:(qt + 1) * 512], start=True, stop=True, tile_position=(0, 0)),
                             reads=[("qh", s), ("kh", s)], writes=[("psw", p)])
                        S.op("pe", lambda e: e.matmul(PSW[p][:, 512:1024], lhsT=kh[s][64:128, kt * 128:(kt + 1) * 128], rhs=qh[s][64:128, qt * 512:(qt + 1) * 512], start=True, stop=True, tile_position=(64, 0)),
                             reads=[("qh", s), ("kh", s)], writes=[("psw", p)])
                        j = kt - 4 * qt
                        if -1 <= j <= 4:
                            off = 512 - 128 * j
                            for m in range(2):
                                S.op("dve", lambda e, m=m: e.scalar_tensor_tensor(out=bt[p][:, m * 512:(m + 1) * 512], in0=PSW[p][:, m * 512:(m + 1) * 512], scalar=0.125, in1=G[:, h, off:off + 512], op0=ALU.mult, op1=ALU.add),
                                     reads=[("psw", p), "G"], writes=[("bt", p, m)])

                    def unit(u):
                        h, qt, kt = units[u]
                        s = h % 2
                        p = u % 2
                        r = u % NE
                        gi = (h * NT + qt) % 2
                        A = acc[gi]
                        j = kt - 4 * qt
                        if -1 <= j <= 4:
                            S.op("act", lambda e: e.activation(out=E[r][:], in_=bt[p][:], func=AF.Exp, bias=cb[:, h, 1, kt:kt + 1], scale=1.0),
                                 reads=[("bt", p, 0), ("bt", p, 1)], writes=[("E", r)])
                        else:
                            side = 0 if j < -1 else 2
                            S.op("act", lambda e: e.activation(out=E[r][:], in_=PSW[p][:], func=AF.Exp, bias=cb[:, h, side, kt:kt + 1], scale=0.125),
                                 reads=[("psw", p)], writes=[("E", r)])

                    def unit_b(u):
                        h, qt, kt = units[u]
                        s = h % 2
                        r = u % NE
                        gi = (h * NT + qt) % 2
                        A = acc[gi]
                        st = (kt == 0)
                        sp_ = (kt == NKT - 1)
                        S.op("pe", lambda e: e.matmul(PS[4][:], lhsT=vh[s][:, kt, :], rhs=E[r][:, 0:512], start=st, stop=sp_), reads=[("vh", s), ("E", r)], writes=[("ps", 4)])
                        S.op("pe", lambda e: e.matmul(PS[5][:], lhsT=vh[s][:, kt, :], rhs=E[r][:, 512:1024], start=st, stop=sp_), reads=[("vh", s), ("E", r)], writes=[("ps", 5)])
                        S.op("pe", lambda e: e.matmul(PS[6][:], lhsT=ones_b[:], rhs=E[r][:, 512:1024], start=st, stop=sp_), reads=[("E", r)], writes=[("ps", 6)])
                        a1 = kt % 2
                        if kt < 2:
                            S.op("dve", lambda e: e.tensor_copy(out=A[a1][:], in_=E[r][:, 0:512]), reads=[("E", r)], writes=[("acc", gi, a1)])
                        else:
                            S.op("dve", lambda e: e.tensor_tensor(out=A[a1][:], in0=A[a1][:], in1=E[r][:, 0:512], op=ALU.add), reads=[("E", r), ("acc", gi, a1)], writes=[("acc", gi, a1)])
                        if kt == NKT - 1:
                            epilogue(h, qt, gi)

                    epi = [0]

                    pending = []
                    ucur = [0]

                    def epilogue(h, qt, gi):
                        k = epi[0] % 2
                        epi[0] += 1
                        A = acc[gi]
                        u0 = ucur[0]
                        S.op("dve", lambda e: e.tensor_copy(out=t1[:], in_=PS[4][:]), reads=[("ps", 4)], writes=["t1"])
                        S.op("dve", lambda e: e.tensor_copy(out=t2[:], in_=PS[5][:]), reads=[("ps", 5)], writes=["t2"])
                        S.op("dve", lambda e: e.tensor_copy(out=s2s[:], in_=PS[6][:]), reads=[("ps", 6)], writes=["s2s"])

                        def stage_b1():
                            S.op("pe", lambda e: e.matmul(PS[7][:], lhsT=ones_f[:], rhs=A[0][:], start=True, stop=False), reads=[("acc", gi, 0)], writes=[("ps", 7)])
                            S.op("pe", lambda e: e.matmul(PS[7][:], lhsT=ones_f[:], rhs=A[1][:], start=False, stop=True), reads=[("acc", gi, 1)], writes=[("ps", 7)])
                            S.op("act", lambda e: e.activation(out=lnb[:, 0:512], in_=s2s[:], func=AF.Ln), reads=["s2s"], writes=[("lnb", 0)])

                        def stage_b2():
                            S.op("act", lambda e: e.activation(out=lnb[:, 512:1024], in_=PS[7][:], func=AF.Ln), reads=[("ps", 7)], writes=[("lnb", 1)])
                            S.op("act", lambda e: e.activation(out=rcp[:], in_=lnb[:], func=AF.Exp, scale=-1.0), reads=[("lnb", 0), ("lnb", 1)], writes=["rcp"])

                        def stage_b3():
                            S.op("dve", lambda e: e.tensor_tensor(out=t1[:], in0=t1[:], in1=rcp[:, 512:1024], op=ALU.mult), reads=["t1", "rcp"], writes=["t1"])
                            S.op("dve", lambda e: e.tensor_tensor(out=t2[:], in0=t2[:], in1=rcp[:, 0:512], op=ALU.mult), reads=["t2", "rcp"], writes=["t2"])
                            S.op("dve", lambda e: e.scalar_tensor_tensor(out=av[:], in0=t2[:], scalar=neglam[:, l:l + 1], in1=t1[:], op0=ALU.mult, op1=ALU.add), reads=["t1", "t2"], writes=["av"])
                            S.op("dve", lambda e: e.tensor_tensor(out=sq[:], in0=av[:], in1=av[:], op=ALU.mult), reads=["av"], writes=["sq"])

                        def stage_c1():
                            S.op("pe", lambda e: e.matmul(PS[7][:], lhsT=ones_f[:], rhs=sq[:], start=True, stop=True), reads=["sq"], writes=[("ps", 7)])

                        def stage_c2():
                            S.op("act", lambda e: e.activation(out=lr[:], in_=PS[7][:], func=AF.Ln, bias=epsb[:, 0:1], scale=1.0 / 128.0), reads=[("ps", 7)], writes=["lr"])
                            S.op("act", lambda e: e.activation(out=rinv[:], in_=lr[:], func=AF.Exp, scale=-0.5), reads=["lr"], writes=["rinv"])

                        def stage_c3():
                            S.op("dve", lambda e: e.scalar_tensor_tensor(out=aost[k][:], in0=av[:], scalar=gsc[:, l:l + 1], in1=rinv[:], op0=ALU.mult, op1=ALU.mult), reads=["av", "rinv"], writes=[("aost", k)])
                            S.op("sp", lambda e: e.dma_start(out=dap(ATs, h * 128 * SD + qt * 512, [[SD, 128], [1, 512]]), in_=aost[k][:]), reads=[("aost", k)], dma=("aost", k))
                        pending.extend([(u0 + 1, stage_b1), (u0 + 3, stage_b2), (u0 + 5, stage_b3), (u0 + 7, stage_c1), (u0 + 9, stage_c2), (u0 + 11, stage_c3)])

                    def run_pending(u):
                        while pending and pending[0][0] <= u:
                            pending.pop(0)[1]()

                    scores(0)
                    if NU > 1:
                        scores(1)
                    for u in range(NU):
                        ucur[0] = u
                        unit(u)
                        if u + 2 < NU:
                            scores(u + 2)
                        unit_b(u)
                        run_pending(u)
                    run_pending(NU + 100)
                    S.barrier()

            def phase_c(l):
                last = (l == L - 1)
                with S.scope() as sc:
                    wout = sc.sbuf("wout", [128, 8, D], BF16)
                    wr = [sc.sbuf("wr%d" % i, [128, 4096], BF16) for i in range(3)]
                    xz = [sc.sbuf("xz%d" % i, [128, 8, 512], F32) for i in range(2)]
                    mix = [sc.sbuf("mix%d" % i, [128, 8, 512], BF16) for i in range(2)]
                    ub = [sc.sbuf("ub", [128, 4, 514], F32)] * 2
                    gbb = [sc.sbuf("gbb", [128, 4, 512], F32)] * 2
                    x1b = sc.sbuf("x1b", [128, 8, 512], BF16)
                    hT = sc.sbuf("hT", [128, 32, 512], BF16)
                    xob = None if last else sc.sbuf("xob", [128, 8, 512], BF16)
                    sqb = [sc.sbuf("sqb%d" % i, [128, 512], F32) for i in range(2)]
                    sacc = [sc.sbuf("sacc%d" % i, [128, 512], F32) for i in range(4)]
                    mean = sc.sbuf("mean", [128, 512], F32); msq = sc.sbuf("msq", [128, 512], F32)
                    sd = sc.sbuf("sd", [128, 512], F32)
                    rinv = sc.sbuf("rinvc", [128, 512], F32)
                    tt = [sc.sbuf("tt%d" % i, [128, 512], F32) for i in range(2)]
                    rr = [sc.sbuf("rr%d" % i, [128, 512], F32) for i in range(3)]
                    ca = [sc.sbuf("ca%d" % i, [128, 512], F32) for i in range(2)]
                    yst = [sc.sbuf("yst%d" % i, [128, D], F32) for i in range(2)] if last else None
                    S.op("sp", lambda e: e.dma_start(out=wout[:], in_=dap(woutb, l * D * D, [[D, 128], [128 * D, 8], [1, D]])), writes=["wout"], dma="wout", extra_deps=wc[l])
                    wctr = [0]
                    cnt = [0]

                    def loads(t):
                        s = t % 2
                        S.op("sp", lambda e: e.dma_start(out=xz[s][:], in_=dap(xTf, t * 512, [[SD, 128], [128 * SD, 8], [1, 512]])), writes=[("z", s, j) for j in range(8)], dma=("xz", s))
                        S.op("sp", lambda e: e.dma_start(out=mix[s][:, 0:4, :], in_=dap(ATs, t * 512, [[SD, 128], [128 * SD, 4], [1, 512]])), writes=[("mixa", s)], dma=("mixa", s))
                        S.op("sp", lambda e: e.dma_start(out=ub[s][:], in_=dap(Us, t * 512, [[SD + 2, 128], [128 * (SD + 2), 4], [1, 514]])), writes=["ub"], dma="ub")
                        S.op("sp", lambda e: e.dma_start(out=gbb[s][:], in_=dap(GBs, t * 512, [[SD, 128], [128 * SD, 4], [1, 512]])), writes=["gbb"], dma="gbb")

                    def conv(t):
                        s = t % 2
                        for cc in range(4):
                            S.op("dve", lambda e, cc=cc: e.tensor_scalar(out=ca[0][:], in0=ub[s][:, cc, 0:512], scalar1=cw[:, l, 0, cc:cc + 1], scalar2=None, op0=ALU.mult), reads=["ub"], writes=[("ca", 0)])
                            S.op("dve", lambda e, cc=cc: e.scalar_tensor_tensor(out=ca[1][:], in0=ub[s][:, cc, 1:513], scalar=cw[:, l, 1, cc:cc + 1], in1=ca[0][:], op0=ALU.mult, op1=ALU.add), reads=["ub", ("ca", 0)], writes=[("ca", 1)])
                            S.op("dve", lambda e, cc=cc: e.scalar_tensor_tensor(out=ca[0][:], in0=ub[s][:, cc, 2:514], scalar=cw[:, l, 2, cc:cc + 1], in1=ca[1][:], op0=ALU.mult, op1=ALU.add), reads=["ub", ("ca", 1)], writes=[("ca", 0)])
                            S.op("dve", lambda e, cc=cc: e.scalar_tensor_tensor(out=mix[s][:, 4 + cc, :], in0=ca[0][:], scalar=cbs[:, l, cc:cc + 1], in1=gbb[s][:, cc, :], op0=ALU.add, op1=ALU.mult), reads=[("ca", 0), "gbb"], writes=[("mixc", s, cc)])

                    def stat_acc(z, zs, j):
                        g = j // 4
                        i = j % 4
                        if i == 1:
                            S.op("dve", lambda e: e.tensor_tensor(out=sacc[g][:], in0=z[:, j - 1, :], in1=z[:, j, :], op=ALU.add), reads=[("z", zs, j - 1), ("z", zs, j)], writes=[("sacc", g)])
                        elif i >= 2:
                            S.op("dve", lambda e: e.tensor_tensor(out=sacc[g][:], in0=sacc[g][:], in1=z[:, j, :], op=ALU.add), reads=[("z", zs, j), ("sacc", g)], writes=[("sacc", g)])
                        if i == 0:
                            S.op("act", lambda e: e.activation(out=sacc[2 + g][:], in_=z[:, j, :], func=AF.Square), reads=[("z", zs, j)], writes=[("sacc", 2 + g)])
                        else:
                            k = cnt[0] % 2
                            cnt[0] += 1
                            S.op("act", lambda e: e.activation(out=sqb[k][:], in_=z[:, j, :], func=AF.Square), reads=[("z", zs, j)], writes=[("sqb", k)])
                            S.op("dve", lambda e: e.tensor_tensor(out=sacc[2 + g][:], in0=sacc[2 + g][:], in1=sqb[k][:], op=ALU.add), reads=[("sqb", k), ("sacc", 2 + g)], writes=[("sacc", 2 + g)])

                    def ln_stats(z, zs):
                        bs = bank()
                        for g in range(2):
                            S.op("pe", lambda e, g=g: e.matmul(PS[bs][:], lhsT=ones_f[:], rhs=sacc[g][:], start=(g == 0), stop=(g == 1)), reads=[("sacc", g)], writes=[("ps", bs)])
                        bq = bank()
                        for g in range(2):
                            S.op("pe", lambda e, g=g: e.matmul(PS[bq][:], lhsT=ones_f[:], rhs=sacc[2 + g][:], start=(g == 0), stop=(g == 1)), reads=[("sacc", 2 + g)], writes=[("ps", bq)])
                        return bs, bq

                    def ln_norm(z, zs, bs, bq, gt, bt_, outb):
                        S.op("act", lambda e: e.mul(out=mean[:], in_=PS[bs][:], mul=1.0 / D), reads=[("ps", bs)], writes=["mean"])
                        S.op("dve", lambda e: e.tensor_tensor(out=msq[:], in0=mean[:], in1=mean[:], op=ALU.mult), reads=["mean"], writes=["msq"])
                        S.op("dve", lambda e: e.scalar_tensor_tensor(out=rinv[:], in0=PS[bq][:], scalar=1.0 / D, in1=msq[:], op0=ALU.mult, op1=ALU.subtract), reads=[("ps", bq), "msq"], writes=["rinvc"])
                        S.op("act", lambda e: e.activation(out=sd[:], in_=rinv[:], func=AF.Sqrt, bias=LN_EPS, scale=1.0), reads=["rinvc"], writes=["sd"])
                        S.op("dve", lambda e: e.reciprocal(out=rinv[:], in_=sd[:]), reads=["sd"], writes=["rinvc"])
                        for j in range(8):
                            k = j % 2
                            S.op("dve", lambda e, j=j, k=k: e.tensor_tensor(out=tt[k][:], in0=z[:, j, :], in1=mean[:], op=ALU.subtract), reads=[("z", zs, j), "mean"], writes=[("tt", k)])
                            S.op("dve", lambda e, j=j, k=k: e.tensor_tensor(out=tt[k][:], in0=tt[k][:], in1=rinv[:], op=ALU.mult), reads=[("tt", k), "rinvc"], writes=[("tt", k)])
                            if outb is not None:
                                S.op("act", lambda e, j=j, k=k: e.activation(out=outb[:, j, :], in_=tt[k][:], func=AF.Identity, bias=bt_[:, l, j:j + 1], scale=gt[:, l, j:j + 1]), reads=[("tt", k)], writes=[("ob", j)])
                            S.op("act", lambda e, j=j, k=k: e.activation(out=z[:, j, :], in_=tt[k][:], func=AF.Identity, bias=bt_[:, l, j:j + 1], scale=gt[:, l, j:j + 1]), reads=[("tt", k)], writes=[("z", zs, j)])

                    loads(0)
                    conv(0)
                    pend = [None]
                    for t in range(NT):
                        s = t % 2
                        z = xz[s]
                        for j in range(8):
                            b = bank()
                            for n_ in range(8):
                                S.op("pe", lambda e, b=b, j=j, n_=n_: e.matmul(PS[b][:], lhsT=wout[:, n_, j * 128:(j + 1) * 128], rhs=mix[s][:, n_, :], start=(n_ == 0), stop=(n_ == 7)),
                                     reads=["wout", ("mixa", s) if n_ < 4 else ("mixc", s, n_ - 4)], writes=[("ps", b)])
                            S.op("dve", lambda e, b=b, j=j: e.scalar_tensor_tensor(out=z[:, j, :], in0=z[:, j, :], scalar=ALPHA, in1=PS[b][:], op0=ALU.mult, op1=ALU.add), reads=[("ps", b), ("z", s, j)], writes=[("z", s, j)])
                            stat_acc(z, s, j)
                        if pend[0] is not None:
                            pend[0]()
                            pend[0] = None
                        if t + 1 < NT:
                            loads(t + 1)
                        bs, bq = ln_stats(z, s)
                        ln_norm(z, s, bs, bq, g1, b1, x1b)
                        if t + 1 < NT:
                            conv(t + 1)
                        for g in range(8):
                            ws = wctr[0] % 3
                            wctr[0] += 1
                            S.op("sp", lambda e, g=g, ws=ws: e.dma_start(out=wr[ws][:].rearrange("p (c f) -> p c f", c=8), in_=dap(w1b, l * D * 4096 + g * 512, [[4096, 128], [128 * 4096, 8], [1, 512]])),
                                 writes=[("wr", ws)], dma=("wr", ws), extra_deps=wc[l])
                            for fl in range(4):
                                fc = g * 4 + fl
                                b = bank()
                                for dc in range(8):
                                    S.op("pe", lambda e, b=b, ws=ws, fl=fl, dc=dc: e.matmul(PS[b][:], lhsT=wr[ws][:, dc * 512 + fl * 128:dc * 512 + (fl + 1) * 128], rhs=x1b[:, dc, :], start=(dc == 0), stop=(dc == 7)),
                                         reads=[("wr", ws), ("ob", dc)], writes=[("ps", b)])
                                k = fc % 3
                                S.op("act", lambda e, b=b, k=k: e.activation(out=rr[k][:], in_=PS[b][:], func=AF.Relu), reads=[("ps", b)], writes=[("rr", k)])
                                S.op("pool" if fc % 3 == 2 else "dve", lambda e, k=k, fc=fc: e.tensor_tensor(out=hT[:, fc, :], in0=rr[k][:], in1=rr[k][:], op=ALU.mult), reads=[("rr", k)], writes=[("hT", fc)])
                        for j in range(8):
                            ws = wctr[0] % 3
                            wctr[0] += 1
                            S.op("sp", lambda e, j=j, ws=ws: e.dma_start(out=wr[ws][:], in_=dap(w2b, (l * 8 + j) * 128 * 4096, [[4096, 128], [1, 4096]])),
                                 writes=[("wr", ws)], dma=("wr", ws), extra_deps=wc[l])
                            b = bank()
                            for fc in range(32):
                                S.op("pe", lambda e, b=b, ws=ws, fc=fc: e.matmul(PS[b][:], lhsT=wr[ws][:, fc * 128:(fc + 1) * 128], rhs=hT[:, fc, :], start=(fc == 0), stop=(fc == 31)),
                                     reads=[("wr", ws), ("hT", fc)], writes=[("ps", b)])
                            S.op("dve", lambda e, b=b, j=j: e.scalar_tensor_tensor(out=z[:, j, :], in0=z[:, j, :], scalar=ALPHA, in1=PS[b][:], op0=ALU.mult, op1=ALU.add), reads=[("ps", b), ("z", s, j)], writes=[("z", s, j)])
                            stat_acc(z, s, j)
                        bs, bq = ln_stats(z, s)
                        ln_norm(z, s, bs, bq, g2, b2, xob)
                        zr = [("z", s, j) for j in range(8)]
                        if not last:
                            S.op("sp", lambda e, t=t, s=s: e.dma_start(out=dap(xTf, t * 512, [[SD, 128], [128 * SD, 8], [1, 512]]), in_=xz[s][:]), reads=zr, dma=("xzst", s))
                            S.op("sp", lambda e, t=t: e.dma_start(out=dap(xTb, t * 512, [[SD, 128], [128 * SD, 8], [1, 512]]), in_=xob[:]), reads=[("ob", j) for j in range(8)], dma="xobst")
                        else:
                            def fin(t=t, s=s):
                                for sub in range(4):
                                    k = sub % 2
                                    for hf in range(2):
                                        b = bank()
                                        for dl in range(4):
                                            dc = hf * 4 + dl
                                            S.op("pe", lambda e, b=b, sub=sub, dc=dc, dl=dl, s=s: e.transpose(out=PS[b][:, dl * 128:(dl + 1) * 128], in_=xz[s][:, dc, sub * 128:(sub + 1) * 128], identity=ident[:]),
                                                 reads=[("z", s, dc)], writes=[("ps", b)])
                                        S.op("act", lambda e, b=b, k=k, hf=hf: e.copy(out=yst[k][:, hf * 512:(hf + 1) * 512], in_=PS[b][:]), reads=[("ps", b)], writes=[("yst", k)])
                                    S.op("sp", lambda e, t=t, sub=sub, k=k: e.dma_start(out=dap(y_out, (t * 512 + sub * 128) * D, [[D, 128], [1, D]]), in_=yst[k][:]), reads=[("yst", k)], dma=("yst", k))
                            pend[0] = fin
                    if pend[0] is not None:
                        pend[0]()
                    S.barrier()

            kstop = int(os.environ.get("KSTOP", "99"))
            for l in range(L):
                if kstop >= 2:
                    phase_a(l)
                if kstop >= 3:
                    attention(l)
                if kstop >= 4:
                    phase_c(l)

    prog(S)
    with contextlib.ExitStack() as st:
        S.start_real(st)
        prog(S)
    return nc, S


_CACHE = {}


def _consts():
    i = np.arange(1280)
    b = np_bucket(639 - i)
    oh = (b[None, :] == np.arange(32)[:, None]).astype(np.float32)
    ident = np.eye(128, dtype=np.float32)
    anti = np.ascontiguousarray(ident[::-1])
    return oh, ident, anti


def run_frames(frames, valids, weights, SD, L):
    key = (SD, L)
    if key not in _CACHE:
        _CACHE[key] = build_program(SD, L)[0]
    nc = _CACHE[key]
    oh, ident, anti = _consts()
    in_maps = []
    for x, nv in zip(frames, valids):
        tok = (np.arange(SD) < nv).astype(np.float32)
        vcol = np.ascontiguousarray(tok.reshape(SD // 128, 128).T)
        vrow = np.ascontiguousarray(np.broadcast_to(tok[None, :], (128, SD)))
        kt_valid = tok.reshape(SD // 128, 128)[:, 0]
        kmask = np.ascontiguousarray(np.broadcast_to(((1.0 - kt_valid) * NEG).astype(np.float32)[None, :], (128, SD // 128)))
        m = {"x": x, "vcol": vcol, "vrow": vrow, "kmask": kmask, "ohrev": oh, "ident": ident, "anti": anti}
        m.update(weights)
        if os.environ.get("KFAST") == "1":
            for k in ("w_in", "w_out", "w_mlp1", "w_mlp2"):
                m[k] = np.zeros((1, 8, 8), np.float32)
        in_maps.append(m)
    res = run_bass_kernel_spmd(nc, in_maps, core_ids=list(range(8)))
    return [r["y"] for r in res.results]


def kernel(x_prompt, x_sample, w_in, w_out, conv_w, conv_b, lambda_q1, lambda_k1, lambda_q2,
           lambda_k2, subln_g, rel_bias, ln1_g, ln1_b, w_mlp1, w_mlp2, ln2_g, ln2_b):
    SD = 8192
    f = lambda a: np.ascontiguousarray(np.asarray(a, dtype=np.float32))
    weights = {"w_in": f(w_in), "w_out": f(w_out), "conv_w": f(conv_w), "conv_b": f(conv_b),
               "lambda_q1": f(lambda_q1), "lambda_k1": f(lambda_k1), "lambda_q2": f(lambda_q2),
               "lambda_k2": f(lambda_k2), "subln_g": f(subln_g), "rel_bias": f(rel_bias),
               "ln1_g": f(ln1_g), "ln1_b": f(ln1_b), "w_mlp1": f(w_mlp1), "w_mlp2": f(w_mlp2),
               "ln2_g": f(ln2_g), "ln2_b": f(ln2_b)}
    xp = f(x_prompt)
    xs = f(x_sample)
    frames = [xp[b] for b in range(4)]
    valids = [SD] * 4
    for b in range(4):
        fr = np.zeros((SD, D), np.float32)
        fr[:xs.shape[1]] = xs[b]
        frames.append(fr)
        valids.append(xs.shape[1])
    ys = run_frames(frames, valids, weights, SD, 4)
    y_prompt = np.stack([ys[b] for b in range(4)], axis=0)
    y_sample = np.stack([ys[4 + b][:xs.shape[1]] for b in range(4)], axis=0)
    return (y_prompt, y_sample)
```
